# Optimizing a Trainium2 kernel written in Bass

```python
import math
import jax
import jax.numpy as jnp
from jax import lax
import numpy as np

D_MODEL = 1024
BATCH = 2
SEQ = 8192
DEPTH = 4


N_MIXERS = 4
EPS = 1e-6
MACARON_W = 0.5
D_FF = 2816
ROPE_BASE = 10000.0
REL_BUCKETS = 32
REL_MAX_DIST = 128
MLA_HEADS = 16
MLA_Q_RANK = 384
MLA_KV_RANK = 256
MLA_NOPE = 64
MLA_ROPE = 32
MLA_V = 64
MLA_Q_BLOCK = 128
RWKV_HEAD = 64
RWKV_HEADS = D_MODEL // RWKV_HEAD
RWKV_DECAY_LORA = 64
RWKV_A_LORA = 64
RWKV_GATE_LORA = 160
RWKV_GN_EPS = 64e-5
MOBA_HEADS = 16
MOBA_HEAD_DIM = D_MODEL // MOBA_HEADS
MOBA_BLOCK = 256
MOBA_TOPK = 3
MOBA_Q_CHUNK = 32
RET_HEADS = 4
RET_DK = D_MODEL // RET_HEADS
RET_DV = 2 * D_MODEL // RET_HEADS
RET_CHUNK = 128
RET_GN_EPS = 1e-5

kernel_name = 'hybrid_mla_rwkv7_moba_retnet_macaron_adaln'


def _rmsnorm(x, g):
    xf = x.astype(jnp.float32)
    y = xf * lax.rsqrt(jnp.mean(xf * xf, axis=-1, keepdims=True) + EPS)
    return (y * g.astype(jnp.float32)).astype(x.dtype)


def _modulate(x, g, shift, scale):
    return _rmsnorm(x, g) * (1 + scale[:, None, :]) + shift[:, None, :]


def _groupnorm_heads(y, w, b, eps):
    B, S, H, N = y.shape
    yf = y.astype(jnp.float32)
    mu = jnp.mean(yf, axis=-1, keepdims=True)
    var = jnp.mean(jnp.square(yf - mu), axis=-1, keepdims=True)
    yn = ((yf - mu) * lax.rsqrt(var + eps)).reshape(B, S, H * N)
    return yn * w.astype(jnp.float32) + b.astype(jnp.float32)


def _swiglu(h, w_in, w_out):
    gt, up = jnp.split(h @ w_in, 2, axis=-1)
    return (jax.nn.silu(gt) * up) @ w_out


def _rope_tables(seq, dim):
    inv = ROPE_BASE ** (-jnp.arange(0, dim, 2, dtype=jnp.float32) / dim)
    ang = jnp.arange(seq, dtype=jnp.float32)[:, None] * inv[None, :]
    return jnp.cos(ang), jnp.sin(ang)


def _apply_rope(x, cos, sin):
    half = x.shape[-1] // 2
    x1 = x[..., :half].astype(jnp.float32)
    x2 = x[..., half:].astype(jnp.float32)
    c = cos[None, :, None, :]
    s = sin[None, :, None, :]
    return jnp.concatenate([x1 * c - x2 * s, x1 * s + x2 * c], axis=-1).astype(x.dtype)


def _t5_bucket(dist):
    n = jnp.maximum(dist, 0)
    max_exact = REL_BUCKETS // 2
    nf = jnp.maximum(n, max_exact).astype(jnp.float32)
    large = max_exact + (jnp.log(nf / max_exact) / math.log(REL_MAX_DIST / max_exact)
                         * (REL_BUCKETS - max_exact)).astype(jnp.int32)
    large = jnp.minimum(large, REL_BUCKETS - 1)
    return jnp.where(n < max_exact, n, large)


def _dense_causal_attention(q, k, v):
    B, S, H, dq = q.shape
    dv = v.shape[-1]
    nq = S // MLA_Q_BLOCK
    qb = q.reshape(B, nq, MLA_Q_BLOCK, H, dq).transpose(1, 0, 3, 2, 4)
    kpos = jnp.arange(S)

    def one_block(args):
        qi, bi = args
        s = jnp.einsum('bhqd,bkhd->bhqk', qi, k).astype(jnp.float32)
        qpos = bi * MLA_Q_BLOCK + jnp.arange(MLA_Q_BLOCK)
        s = jnp.where(kpos[None, :] <= qpos[:, None], s, -jnp.inf)
        p = jax.nn.softmax(s, axis=-1).astype(v.dtype)
        return jnp.einsum('bhqk,bkhd->bqhd', p, v)

    o = lax.map(one_block, (qb, jnp.arange(nq)))
    return o.transpose(1, 0, 2, 3, 4).reshape(B, S, H * dv)


def _mla_mixer(h, w_in, q_norm, w_uq, kv_norm, w_ukv, w_out):
    B, S, _ = h.shape
    H = MLA_HEADS
    cq, ckv, k_rope = jnp.split(h @ w_in, [MLA_Q_RANK, MLA_Q_RANK + MLA_KV_RANK], axis=-1)
    q = (_rmsnorm(cq, q_norm) @ w_uq).reshape(B, S, H, MLA_NOPE + MLA_ROPE)
    kv = (_rmsnorm(ckv, kv_norm) @ w_ukv).reshape(B, S, H, MLA_NOPE + MLA_V)
    cos, sin = _rope_tables(S, MLA_ROPE)
    q = jnp.concatenate([q[..., :MLA_NOPE], _apply_rope(q[..., MLA_NOPE:], cos, sin)], axis=-1)
    k_rope = jnp.broadcast_to(_apply_rope(k_rope[:, :, None, :], cos, sin), (B, S, H, MLA_ROPE))
    k = jnp.concatenate([kv[..., :MLA_NOPE], k_rope], axis=-1)
    v = kv[..., MLA_NOPE:]
    o = _dense_causal_attention(q * (MLA_NOPE + MLA_ROPE) ** -0.5, k, v)
    return o @ w_out


def _rwkv7_mixer(h, mu, w_rkv, w0, wd1, wd2, a0, wa1, wa2, wg1, wg2, k_k, k_a, r_k, gn_w, gn_b, w_out):
    B, S, D = h.shape
    H, N = RWKV_HEADS, RWKV_HEAD
    xx = jnp.pad(h, ((0, 0), (1, 0), (0, 0)))[:, :-1] - h
    xs = h[:, :, None, :] + xx[:, :, None, :] * mu
    rkv = jnp.einsum('bsnd,nde->bsne', xs[:, :, :3], w_rkv)
    r, k, v = rkv[:, :, 0], rkv[:, :, 1], rkv[:, :, 2]
    xw, xa, xg = xs[:, :, 3], xs[:, :, 4], xs[:, :, 5]
    w = -jax.nn.softplus(-(w0 + jnp.tanh(xw @ wd1) @ wd2)) - 0.5
    decay = jnp.exp(-jnp.exp(w.astype(jnp.float32)))
    a = jax.nn.sigmoid((a0 + (xa @ wa1) @ wa2).astype(jnp.float32))
    g = jax.nn.sigmoid(xg @ wg1) @ wg2
    kk = (k * k_k).astype(jnp.float32).reshape(B, S, H, N)
    kk = kk / jnp.maximum(jnp.sqrt(jnp.sum(kk * kk, axis=-1, keepdims=True)), 1e-12)
    k_mod = k.astype(jnp.float32) * (1 + (a - 1) * k_a.astype(jnp.float32))

    def heads(t):
        return t.astype(jnp.float32).reshape(B, S, H, N)

    r_h, k_h, v_h, a_h, w_h = heads(r), heads(k_mod), heads(v), heads(a), heads(decay)

    def tm(t):
        return t.transpose(1, 0, 2, 3)

    def step(state, inp):
        r_t, w_t, k_t, v_t, ka_t, kb_t = inp
        sa = jnp.einsum('bhvk,bhk->bhv', state, ka_t)
        state = (state * w_t[:, :, None, :] + sa[..., None] * kb_t[:, :, None, :]
                 + v_t[..., None] * k_t[:, :, None, :])
        return state, jnp.einsum('bhvk,bhk->bhv', state, r_t)

    state0 = jnp.zeros((B, H, N, N), jnp.float32)
    _, y = lax.scan(step, state0, (tm(r_h), tm(w_h), tm(k_h), tm(v_h), tm(-kk), tm(kk * a_h)))
    y = y.transpose(1, 0, 2, 3)
    yn = _groupnorm_heads(y, gn_w, gn_b, RWKV_GN_EPS)
    bonus = (jnp.sum(r_h * k_h * r_k.astype(jnp.float32), axis=-1, keepdims=True) * v_h).reshape(B, S, D)
    return ((yn + bonus) * g.astype(jnp.float32)).astype(h.dtype) @ w_out


def _moba_mixer(h, w_in, rel_table, w_out):
    B, S, _ = h.shape
    H, dh, BLK, QC = MOBA_HEADS, MOBA_HEAD_DIM, MOBA_BLOCK, MOBA_Q_CHUNK
    qkv = (h @ w_in).reshape(B, S, 3, H, dh)
    q = qkv[:, :, 0].transpose(0, 2, 1, 3) * dh ** -0.5
    k = qkv[:, :, 1].transpose(0, 2, 1, 3)
    v = qkv[:, :, 2].transpose(0, 2, 1, 3)
    nb = -(-S // BLK)
    K = min(MOBA_TOPK, nb)
    pad = nb * BLK - S
    kb = jnp.pad(k, ((0, 0), (0, 0), (0, pad), (0, 0))).reshape(B, H, nb, BLK, dh)
    vb = jnp.pad(v, ((0, 0), (0, 0), (0, pad), (0, 0))).reshape(B, H, nb, BLK, dh)
    kmean = jnp.mean(kb.astype(jnp.float32), axis=3)
    table_t = rel_table.astype(jnp.float32).T
    nc = S // QC
    qc = q.reshape(B, H, nc, QC, dh).transpose(2, 0, 1, 3, 4)
    b_ix = jnp.arange(B)[:, None, None, None]
    h_ix = jnp.arange(H)[None, :, None, None]
    blk_ids = jnp.arange(nb)
    offs = jnp.arange(BLK)

    def chunk(args):
        qi, ci = args
        start = ci * QC
        qpos = start + jnp.arange(QC)
        own = start // BLK
        gate = jnp.einsum('bhqd,bhnd->bhqn', qi.astype(jnp.float32), kmean)
        gate = jnp.where(blk_ids < own, gate, -jnp.inf)
        top_s, top_i = lax.top_k(gate, K)
        valid = jnp.isfinite(top_s)
        k_sel = kb[b_ix, h_ix, top_i]
        v_sel = vb[b_ix, h_ix, top_i]
        pos_sel = top_i[..., None] * BLK + offs
        bias_sel = table_t[h_ix[..., None], _t5_bucket(qpos[:, None, None] - pos_sel)]
        s_sel = jnp.einsum('bhqd,bhqjkd->bhqjk', qi, k_sel).astype(jnp.float32) + bias_sel
        s_sel = jnp.where(valid[..., None], s_sel, -jnp.inf).reshape(B, H, QC, K * BLK)
        k_own = lax.dynamic_index_in_dim(kb, own, axis=2, keepdims=False)
        v_own = lax.dynamic_index_in_dim(vb, own, axis=2, keepdims=False)
        dist_own = qpos[:, None] - (own * BLK + offs)[None, :]
        s_own = jnp.einsum('bhqd,bhkd->bhqk', qi, k_own).astype(jnp.float32) + table_t[:, _t5_bucket(dist_own)]
        s_own = jnp.where(dist_own >= 0, s_own, -jnp.inf)
        p = jax.nn.softmax(jnp.concatenate([s_sel, s_own], axis=-1), axis=-1).astype(v.dtype)
        p_sel = p[..., :K * BLK].reshape(B, H, QC, K, BLK)
        p_own = p[..., K * BLK:]
        return (jnp.einsum('bhqjk,bhqjkd->bhqd', p_sel, v_sel)
                + jnp.einsum('bhqk,bhkd->bhqd', p_own, v_own))

    o = lax.map(chunk, (qc, jnp.arange(nc)))
    o = o.transpose(1, 0, 3, 2, 4).reshape(B, S, H * dh)
    return o @ w_out


def _retnet_mixer(h, w_in, gn_w, gn_b, w_out):
    B, S, D = h.shape
    H, DK, DV, C = RET_HEADS, RET_DK, RET_DV, RET_CHUNK
    q, k, v, g = jnp.split(h @ w_in, [D, 2 * D, 4 * D], axis=-1)
    cos, sin = _rope_tables(S, DK)
    q = _apply_rope(q.reshape(B, S, H, DK), cos, sin).astype(jnp.float32)
    k = _apply_rope(k.reshape(B, S, H, DK) * DK ** -0.5, cos, sin).astype(jnp.float32)
    v = v.reshape(B, S, H, DV).astype(jnp.float32)
    log_gamma = jnp.log1p(-jnp.exp2(-5.0 - jnp.arange(H, dtype=jnp.float32)))
    idx = jnp.arange(C, dtype=jnp.float32)
    diff = idx[:, None] - idx[None, :]
    dmask = jnp.where(diff >= 0, jnp.exp(jnp.maximum(diff, 0.0)[None] * log_gamma[:, None, None]), 0.0)
    zeta = jnp.exp((C - 1 - idx)[None, :] * log_gamma[:, None])
    xi = jnp.exp((idx + 1)[None, :] * log_gamma[:, None])
    gamma_c = jnp.exp(C * log_gamma)
    nc = S // C

    def chunks(t):
        return t.reshape(B, nc, C, H, t.shape[-1]).transpose(1, 0, 3, 2, 4)

    def step(R, inp):
        qc, kc, vc = inp
        inner = jnp.einsum('bhnd,bhmd->bhnm', qc, kc) * dmask
        o = (jnp.einsum('bhnm,bhme->bhne', inner, vc)
             + jnp.einsum('bhnd,bhde->bhne', qc, R) * xi[None, :, :, None])
        R = gamma_c[None, :, None, None] * R + jnp.einsum('bhmd,hm,bhme->bhde', kc, zeta, vc)
        return R, o

    R0 = jnp.zeros((B, H, DK, DV), jnp.float32)
    _, o = lax.scan(step, R0, (chunks(q), chunks(k), chunks(v)))
    o = o.transpose(1, 0, 3, 2, 4).reshape(B, S, H, DV)
    yn = _groupnorm_heads(o, gn_w, gn_b, RET_GN_EPS)
    return (jax.nn.silu(g.astype(jnp.float32)) * yn).astype(h.dtype) @ w_out


def setup_inputs(seed: int = 0) -> dict:
    key = jax.random.key(seed)
    ks = iter(jax.random.split(key, 48))
    f32 = jnp.float32

    def nrm(shape, scale):
        return scale * jax.random.normal(next(ks), shape, f32)

    def uni(shape, lo, hi):
        return jax.random.uniform(next(ks), shape, f32, lo, hi)

    D = D_MODEL
    n_mla, n_rwkv, n_moba, n_ret = [len(range(m, DEPTH, N_MIXERS)) for m in range(N_MIXERS)]
    gate_offset = jnp.zeros((3,), f32).at[2].set(1.0)
    return {
        'x': nrm((BATCH, SEQ, D), 1.0),
        'c': nrm((BATCH, D), 1.0),
        'ada_w': nrm((DEPTH, D, 9 * D), 0.2 * D ** -0.5),
        'ada_b': (nrm((DEPTH, 3, 3, D), 0.02) + gate_offset[None, None, :, None]).reshape(DEPTH, 9 * D),
        'norm_g': 1.0 + nrm((DEPTH, 3, D), 0.05),
        'ffn_w_in': nrm((DEPTH, 2, D, 2 * D_FF), D ** -0.5),
        'ffn_w_out': nrm((DEPTH, 2, D_FF, D), D_FF ** -0.5),
        'final_g': 1.0 + nrm((D,), 0.05),
        'rel_table': nrm((REL_BUCKETS, MOBA_HEADS), 0.5),
        'mla_w_in': nrm((n_mla, D, MLA_Q_RANK + MLA_KV_RANK + MLA_ROPE), D ** -0.5),
        'mla_q_norm': 1.0 + nrm((n_mla, MLA_Q_RANK), 0.05),
        'mla_w_uq': nrm((n_mla, MLA_Q_RANK, MLA_HEADS * (MLA_NOPE + MLA_ROPE)), MLA_Q_RANK ** -0.5),
        'mla_kv_norm': 1.0 + nrm((n_mla, MLA_KV_RANK), 0.05),
        'mla_w_ukv': nrm((n_mla, MLA_KV_RANK, MLA_HEADS * (MLA_NOPE + MLA_V)), MLA_KV_RANK ** -0.5),
        'mla_w_out': nrm((n_mla, MLA_HEADS * MLA_V, D), (MLA_HEADS * MLA_V) ** -0.5),
        'rwkv_mu': uni((n_rwkv, 6, D), 0.0, 1.0),
        'rwkv_w_rkv': nrm((n_rwkv, 3, D, D), D ** -0.5),
        'rwkv_w0': uni((n_rwkv, D), -6.5, -1.5),
        'rwkv_wd1': nrm((n_rwkv, D, RWKV_DECAY_LORA), D ** -0.5),
        'rwkv_wd2': nrm((n_rwkv, RWKV_DECAY_LORA, D), 0.5 * RWKV_DECAY_LORA ** -0.5),
        'rwkv_a0': nrm((n_rwkv, D), 0.1),
        'rwkv_wa1': nrm((n_rwkv, D, RWKV_A_LORA), D ** -0.5),
        'rwkv_wa2': nrm((n_rwkv, RWKV_A_LORA, D), RWKV_A_LORA ** -0.5),
        'rwkv_wg1': nrm((n_rwkv, D, RWKV_GATE_LORA), D ** -0.5),
        'rwkv_wg2': nrm((n_rwkv, RWKV_GATE_LORA, D), RWKV_GATE_LORA ** -0.5),
        'rwkv_k_k': 0.85 + nrm((n_rwkv, D), 0.05),
        'rwkv_k_a': 1.0 + nrm((n_rwkv, D), 0.05),
        'rwkv_r_k': nrm((n_rwkv, RWKV_HEADS, RWKV_HEAD), 0.1),
        'rwkv_gn_w': 1.0 + nrm((n_rwkv, D), 0.05),
        'rwkv_gn_b': nrm((n_rwkv, D), 0.02),
        'rwkv_w_out': nrm((n_rwkv, D, D), D ** -0.5),
        'moba_w_in': nrm((n_moba, D, 3 * D), D ** -0.5),
        'moba_w_out': nrm((n_moba, D, D), D ** -0.5),
        'ret_w_in': nrm((n_ret, D, 6 * D), D ** -0.5),
        'ret_gn_w': 1.0 + nrm((n_ret, 2 * D), 0.05),
        'ret_gn_b': nrm((n_ret, 2 * D), 0.02),
        'ret_w_out': nrm((n_ret, 2 * D, D), (2 * D) ** -0.5),
    }


def reference(x, c, ada_w, ada_b, norm_g, ffn_w_in, ffn_w_out, final_g, rel_table,
              mla_w_in, mla_q_norm, mla_w_uq, mla_kv_norm, mla_w_ukv, mla_w_out,
              rwkv_mu, rwkv_w_rkv, rwkv_w0, rwkv_wd1, rwkv_wd2, rwkv_a0, rwkv_wa1, rwkv_wa2,
              rwkv_wg1, rwkv_wg2, rwkv_k_k, rwkv_k_a, rwkv_r_k, rwkv_gn_w, rwkv_gn_b, rwkv_w_out,
              moba_w_in, moba_w_out, ret_w_in, ret_gn_w, ret_gn_b, ret_w_out):
    B = x.shape[0]
    c_act = jax.nn.silu(c)
    for i in range(DEPTH):
        mods = (c_act @ ada_w[i] + ada_b[i]).reshape(B, 3, 3, D_MODEL)
        h = _modulate(x, norm_g[i, 0], mods[:, 0, 0], mods[:, 0, 1])
        x = x + MACARON_W * mods[:, 0, 2][:, None, :] * _swiglu(h, ffn_w_in[i, 0], ffn_w_out[i, 0])
        h = _modulate(x, norm_g[i, 1], mods[:, 1, 0], mods[:, 1, 1])
        kind, j = i % N_MIXERS, i // N_MIXERS
        if kind == 0:
            y = _mla_mixer(h, mla_w_in[j], mla_q_norm[j], mla_w_uq[j], mla_kv_norm[j], mla_w_ukv[j], mla_w_out[j])
        elif kind == 1:
            y = _rwkv7_mixer(h, rwkv_mu[j], rwkv_w_rkv[j], rwkv_w0[j], rwkv_wd1[j], rwkv_wd2[j], rwkv_a0[j],
                             rwkv_wa1[j], rwkv_wa2[j], rwkv_wg1[j], rwkv_wg2[j], rwkv_k_k[j], rwkv_k_a[j],
                             rwkv_r_k[j], rwkv_gn_w[j], rwkv_gn_b[j], rwkv_w_out[j])
        elif kind == 2:
            y = _moba_mixer(h, moba_w_in[j], rel_table, moba_w_out[j])
        else:
            y = _retnet_mixer(h, ret_w_in[j], ret_gn_w[j], ret_gn_b[j], ret_w_out[j])
        x = x + mods[:, 1, 2][:, None, :] * y
        h = _modulate(x, norm_g[i, 2], mods[:, 2, 0], mods[:, 2, 1])
        x = x + MACARON_W * mods[:, 2, 2][:, None, :] * _swiglu(h, ffn_w_in[i, 1], ffn_w_out[i, 1])
    return _rmsnorm(x, final_g)
```

```python
import numpy as np
import concourse.bass as bass
import concourse.mybir as mybir
from concourse.bass_utils import run_bass_kernel_spmd
from contextlib import ExitStack

F32 = mybir.dt.float32
BF16 = mybir.dt.bfloat16
ALU = mybir.AluOpType
AF = mybir.ActivationFunctionType
AX = mybir.AxisListType

ENGS = ("pe", "act", "dve", "pool", "sp")
EPOCH = 16000
SAME_ENGINE_SYNC = True
NOSYNC_ENGS = ("act",)
NDMASEM = 24


class Buf:
    __slots__ = ("name", "w", "r", "excl")

    def __init__(self, name="", excl=False):
        self.name = name
        self.w = None
        self.r = []
        self.excl = excl


class Prog:
    def __init__(self, nc):
        self.nc = nc
        self.streams = {e: [] for e in ENGS}
        self.known = {e: {} for e in ENGS}
        self.snap = {e: [] for e in ENGS}
        self.dmacount = {e: 0 for e in ENGS}
        self.dma_snap = {}
        self.stack = ExitStack()
        self.nbuf = 0

    def sbuf(self, name, shape, dtype):
        t = self.stack.enter_context(self.nc.sbuf_tensor(name, list(shape), dtype))
        return t

    def psum(self, name, shape, dtype=F32):
        t = self.stack.enter_context(self.nc.psum_tensor(name, list(shape), dtype))
        return t

    def buf(self, name="", excl=False):
        self.nbuf += 1
        return Buf(name or f"b{self.nbuf}", excl)

    def _deps(self, eng, reads, writes, nosync_same=False):
        deps = {}
        def add(ev):
            if ev is None:
                return
            src, idx = ev
            if src == eng and (nosync_same or not SAME_ENGINE_SYNC or eng in NOSYNC_ENGS):
                return
            if self.known[eng].get(src, 0) >= idx:
                return
            if deps.get(src, 0) < idx:
                deps[src] = idx
        def add_x(ev):
            if ev is not None and ev[0] != eng:
                add(ev)
        for b in reads:
            if b.excl:
                add_x(b.w)
            else:
                add(b.w)
        for b in writes:
            if b.excl:
                add_x(b.w)
                continue
            add(b.w)
            for ev in b.r:
                add(ev)
        return deps

    def _absorb(self, eng, deps):
        k = self.known[eng]
        for src, idx in deps.items():
            if k.get(src, 0) < idx:
                k[src] = idx
            if isinstance(src, str):
                sn = self.snap[src][idx - 1]
            else:
                sn = self.dma_snap.get((src, idx))
            if sn:
                for s2, i2 in sn.items():
                    if k.get(s2, 0) < i2:
                        k[s2] = i2

    def op(self, eng, fn, reads=(), writes=(), nosync_same=False):
        deps = self._deps(eng, reads, writes, nosync_same)
        self._absorb(eng, deps)
        st = self.streams[eng]
        st.append([fn, list(deps.items()), "op"])
        idx = len(st)
        ev = (eng, idx)
        self.snap[eng].append(dict(self.known[eng]))
        for b in reads:
            if b.excl:
                b.w = ev
            else:
                b.r.append(ev)
        for b in writes:
            b.w = ev
            b.r = []
        return ev

    def dma(self, eng, fn, reads=(), writes=()):
        deps = self._deps(eng, reads, writes)
        n = self.dmacount[eng]
        self.dmacount[eng] = n + 1
        slot = n % NDMASEM
        val = 16 * (n // NDMASEM + 1)
        src = ("dma", eng, slot)
        if val > 16 and self.known[eng].get(src, 0) < val - 16 and deps.get(src, 0) < val - 16:
            deps[src] = val - 16
        self._absorb(eng, deps)
        st = self.streams[eng]
        st.append([fn, list(deps.items()), ("dma", slot)])
        self.snap[eng].append(dict(self.known[eng]))
        self.dma_snap[(src, val)] = dict(self.known[eng])
        ev = (src, val)
        for b in reads:
            b.r.append(ev)
        for b in writes:
            b.w = ev
            b.r = []
        return ev

    def barrier(self, bufs):
        for e in ENGS:
            deps = self._deps(e, (), bufs)
            if deps:
                self._absorb(e, deps)
                self.streams[e].append([None, list(deps.items()), "wait"])
                self.snap[e].append(dict(self.known[e]))

    def final_wait(self, eng, bufs):
        deps = self._deps(eng, bufs, ())
        self._absorb(eng, deps)
        self.streams[eng].append([None, list(deps.items()), "wait"])
        self.snap[eng].append(dict(self.known[eng]))

    def emit(self):
        nc = self.nc
        marked = {e: set() for e in ENGS}
        for e in ENGS:
            for fn, waits, kind in self.streams[e]:
                for src, idx in waits:
                    if isinstance(src, str):
                        marked[src].add(idx)
        rank = {}
        nsem = {}
        for e in ENGS:
            r = 0
            for i in sorted(marked[e]):
                r += 1
                rank[(e, i)] = r
            nsem[e] = (r + EPOCH - 1) // EPOCH
        sems = {}
        for e in ENGS:
            for k in range(nsem[e]):
                sems[(e, k)] = self.stack.enter_context(nc.semaphore(f"s_{e}_{k}"))
        dsems = {}
        for e in ENGS:
            if self.dmacount[e]:
                for s in range(min(NDMASEM, self.dmacount[e])):
                    dsems[(e, s)] = self.stack.enter_context(nc.semaphore(f"d_{e}_{s}"))
        block = self.stack.enter_context(nc.Block())
        streams = self.streams

        def run(e, engine):
            for i, (fn, waits, kind) in enumerate(streams[e]):
                for src, idx in waits:
                    if isinstance(src, str):
                        r = rank[(src, idx)] - 1
                        engine.wait_ge(sems[(src, r // EPOCH)], r % EPOCH + 1)
                    else:
                        engine.wait_ge(dsems[(src[1], src[2])], idx)
                if fn is None:
                    continue
                ins = fn(engine)
                if kind == "op":
                    if (i + 1) in marked[e]:
                        r = rank[(e, i + 1)] - 1
                        ins.then_inc(sems[(e, r // EPOCH)], 1)
                else:
                    ins.then_inc(dsems[(e, kind[1])], 16)

        if streams["sp"]:
            @block.sync
            def _(eng):
                run("sp", eng)
        if streams["pe"]:
            @block.tensor
            def _(eng):
                run("pe", eng)
        if streams["act"]:
            @block.scalar
            def _(eng):
                run("act", eng)
        if streams["dve"]:
            @block.vector
            def _(eng):
                run("dve", eng)
        if streams["pool"]:
            @block.gpsimd
            def _(eng):
                run("pool", eng)

    def close(self):
        self.stack.close()


D = 1024
DFF = 2816
NT = 2048
EPS = 1e-6


def build_mods():
    nc = bass.Bass("TRN2", target_bir_lowering=False)
    P = Prog(nc)
    cT = nc.dram_tensor("cT", [128, 8, 2], F32, kind="ExternalInput").ap()
    aw = nc.dram_tensor("aw", [128, 8, 4608], F32, kind="ExternalInput").ap()
    ab = nc.dram_tensor("ab", [128, 36], F32, kind="ExternalInput").ap()
    out = nc.dram_tensor("modsT", [128, 36, 2], F32, kind="ExternalOutput").ap()
    c_sb = P.sbuf("c_sb", [128, 8, 2], F32); bc = P.buf()
    ca = P.sbuf("ca", [128, 8, 2], F32); bca = P.buf()
    ab_sb = P.sbuf("ab_sb", [128, 36], F32); bab = P.buf()
    res = P.sbuf("res", [128, 36, 2], F32); bres = P.buf()
    ps = P.psum("ps", [128, 36, 2]); bps = P.buf()
    w = [P.sbuf(f"w{i}", [128, 8, 1152], F32) for i in range(2)]
    bw = [P.buf() for _ in range(2)]
    P.dma("sp", lambda e: e.dma_start(out=c_sb[:], in_=cT), writes=[bc])
    P.dma("sp", lambda e: e.dma_start(out=ab_sb[:], in_=ab), writes=[bab])
    P.op("act", lambda e: e.activation(out=ca[:], in_=c_sb[:], func=AF.Silu), reads=[bc], writes=[bca])
    for g in range(4):
        k = g % 2
        P.dma("sp", lambda e, g=g, k=k: e.dma_start(out=w[k][:], in_=aw[:, :, g * 1152:(g + 1) * 1152]), writes=[bw[k]])
        for j in range(9):
            jj = g * 9 + j
            for kc in range(8):
                P.op("pe", lambda e, k=k, j=j, jj=jj, kc=kc: e.matmul(
                    ps[:, jj, :], lhsT=w[k][:, kc, j * 128:(j + 1) * 128], rhs=ca[:, kc, :],
                    start=(kc == 0), stop=(kc == 7)), reads=[bw[k], bca], writes=[bps], nosync_same=True)
    for b in range(2):
        P.op("dve", lambda e, b=b: e.tensor_tensor(out=res[:, :, b], in0=ps[:, :, b], in1=ab_sb[:], op=ALU.add),
             reads=[bps, bab], writes=[bres])
    bo = P.buf()
    P.dma("sp", lambda e: e.dma_start(out=out, in_=res[:]), reads=[bres], writes=[bo])
    P.final_wait("sp", [bo])
    P.emit(); P.close()
    return nc


class TokCtx:
    pass


def build_token(F_in, do_next, final):
    nc = bass.Bass("TRN2", target_bir_lowering=False)
    P = Prog(nc)
    dt = nc.dram_tensor
    xT_in = dt("xT_in", [8, 128, NT], F32, kind="ExternalInput").ap()
    xT_out = dt("xT_out", [8, 128, NT], F32, kind="ExternalOutput").ap()
    if F_in:
        KO = F_in // 128
        oT = dt("oT", [KO, 128, NT], BF16, kind="ExternalInput").ap()
        wmo = dt("wmo", [128, KO, D], F32, kind="ExternalInput").ap()
        w_in2 = dt("w_in2", [11, 128, 8, 512], F32, kind="ExternalInput").ap()
        w_out2 = dt("w_out2", [4, 128, 22, 256], F32, kind="ExternalInput").ap()
        modsA = dt("modsA", [128, 72], F32, kind="ExternalInput").ap()
        gA = dt("gA", [128, 3, 8], F32, kind="ExternalInput").ap()
    if do_next:
        w_in1 = dt("w_in1", [11, 128, 8, 512], F32, kind="ExternalInput").ap()
        w_out1 = dt("w_out1", [4, 128, 22, 256], F32, kind="ExternalInput").ap()
        modsB = dt("modsB", [128, 72], F32, kind="ExternalInput").ap()
        gB = dt("gB", [128, 3, 8], F32, kind="ExternalInput").ap()
        hT_out = dt("hT_out", [8, 128, NT], BF16, kind="ExternalOutput").ap()
    if final:
        gF = dt("gF", [128, 8], F32, kind="ExternalInput").ap()

    x = P.sbuf("x", [128, 8, NT], F32)
    bx = [[P.buf(f"x{h}_{c}") for c in range(8)] for h in range(4)]
    ones = P.sbuf("ones", [128, 128], BF16); bones = P.buf()
    hT = P.sbuf("hT", [128, 8, 1024], BF16)
    bh = [[P.buf() for _ in range(8)] for _ in range(2)]
    act = P.sbuf("act", [128, 22, 1024], BF16)
    bact = [[P.buf() for _ in range(22)] for _ in range(2)]
    wi = [P.sbuf(f"wi{i}", [128, 8, 512], BF16) for i in range(3)]
    bwi = [P.buf() for _ in range(3)]
    wo = [P.sbuf(f"wo{i}", [128, 22, 256], BF16) for i in range(2)]
    bwo = [P.buf() for _ in range(2)]
    sq = [P.sbuf(f"sq{i}", [128, 512], BF16) for i in range(2)]
    bsq = [P.buf() for _ in range(2)]
    rstd = P.sbuf("rstd", [128, 512], F32); brstd = P.buf()
    tmp = [P.sbuf(f"tmp{i}", [128, 512], F32) for i in range(2)]
    btmp = [P.buf() for _ in range(2)]
    sg = [P.sbuf(f"sg{i}", [128, 512], F32) for i in range(2)]
    bsg = [P.buf() for _ in range(2)]
    mods = {}
    gsb = {}
    small = P.sbuf("small", [128, 2, 72 + 24 + 72], F32)
    bsmall = [P.buf() for _ in range(2)]
    gf_sb = P.sbuf("gf_sb", [128, 8], F32); bgf = P.buf()
    pg = [P.psum(f"pg{i}", [128, 512]) for i in range(2)]; bpg = [P.buf(excl=True) for _ in range(2)]
    pu = [P.psum(f"pu{i}", [128, 512]) for i in range(2)]; bpu = [P.buf(excl=True) for _ in range(2)]
    po = [P.psum(f"po{i}", [128, 512]) for i in range(2)]; bpo = [P.buf(excl=True) for _ in range(2)]
    pn = P.psum("pn", [128, 512]); bpn = P.buf(excl=True)
    cnt = {"wi": 0, "wo": 0, "sq": 0, "tmp": 0, "sg": 0, "pg": 0, "po": 0}

    P.op("dve", lambda e: e.memset(ones[:], 1.0), writes=[bones])
    for c in range(8):
        for h in range(4):
            P.dma("sp", lambda e, c=c, h=h: e.dma_start(out=x[:, c, h * 512:(h + 1) * 512], in_=xT_in[c, :, h * 512:(h + 1) * 512]),
                  writes=[bx[h][c]])

    def load_small(slot, mods_ap, g_ap):
        P.dma("sp", lambda e: e.dma_start(out=small[:, slot, 0:72], in_=mods_ap), writes=[bsmall[slot]])
        P.dma("sp", lambda e: e.dma_start(out=small[:, slot, 72:96], in_=g_ap.rearrange("p a b -> p (a b)")), writes=[bsmall[slot]])
        for sub in range(3):
            sc = small[:, slot, sub * 24 + 8: sub * 24 + 16]
            gg = small[:, slot, 72 + sub * 8: 72 + sub * 8 + 8]
            dst = small[:, slot, 96 + sub * 8: 96 + sub * 8 + 8]
            P.op("dve", lambda e, sc=sc, gg=gg, dst=dst: e.scalar_tensor_tensor(out=dst, in0=sc, scalar=1.0, in1=gg, op0=ALU.add, op1=ALU.mult),
                 reads=[bsmall[slot]], writes=[bsmall[slot]])
            gt = small[:, slot, sub * 24 + 16: sub * 24 + 24]
            dst2 = small[:, slot, 120 + sub * 8: 120 + sub * 8 + 8]
            P.op("dve", lambda e, gt=gt, dst2=dst2, sub=sub: e.tensor_scalar(out=dst2, in0=gt, scalar1=(1.0 if sub == 1 else 0.5), scalar2=None, op0=ALU.mult),
                 reads=[bsmall[slot]], writes=[bsmall[slot]])

    def shift_ap(slot, sub, c):
        return small[:, slot, sub * 24 + c: sub * 24 + c + 1]

    def gs_ap(slot, sub, c):
        return small[:, slot, 96 + sub * 8 + c: 96 + sub * 8 + c + 1]

    def gm_ap(slot, sub, c):
        return small[:, slot, 120 + sub * 8 + c: 120 + sub * 8 + c + 1]

    def norm_tile(h4, slot, sub, dst_fn, dst_bufs, plain_g=None):
        tsl = slice(h4 * 512, (h4 + 1) * 512)
        for c in range(8):
            k = cnt["sq"] % 2; cnt["sq"] += 1
            P.op("act", lambda e, c=c, k=k: e.activation(out=sq[k][:], in_=x[:, c, tsl], func=AF.Square),
                 reads=[bx[h4][c]], writes=[bsq[k]])
            P.op("pe", lambda e, c=c, k=k: e.matmul(pn[:], lhsT=ones[:], rhs=sq[k][:], start=(c == 0), stop=(c == 7)),
                 reads=[bones, bsq[k]], writes=[bpn], nosync_same=True)
        P.op("act", lambda e: e.activation(out=rstd[:], in_=pn[:], func=AF.Sqrt, bias=EPS, scale=1.0 / D),
             reads=[bpn], writes=[brstd])
        P.op("dve", lambda e: e.reciprocal(out=rstd[:], in_=rstd[:]), reads=[brstd], writes=[brstd])
        for c in range(8):
            k = cnt["tmp"] % 2; cnt["tmp"] += 1
            P.op("dve", lambda e, c=c, k=k: e.tensor_tensor(out=tmp[k][:], in0=x[:, c, tsl], in1=rstd[:], op=ALU.mult),
                 reads=[bx[h4][c], brstd], writes=[btmp[k]])
            if plain_g is None:
                P.op("act", lambda e, c=c, k=k: e.activation(out=dst_fn(c), in_=tmp[k][:], func=AF.Identity,
                                                               bias=shift_ap(slot, sub, c), scale=gs_ap(slot, sub, c)),
                     reads=[btmp[k], bsmall[slot]], writes=dst_bufs(c))
            else:
                P.op("act", lambda e, c=c, k=k: e.activation(out=dst_fn(c), in_=tmp[k][:], func=AF.Identity,
                                                               scale=plain_g[:, c:c + 1]),
                     reads=[btmp[k], bgf], writes=dst_bufs(c))

    def u_norm(u):
        slot, sub, w_in_ap, w_out_ap, half = u
        for s2 in range(2):
            h4 = half * 2 + s2
            norm_tile(h4, slot, sub, lambda c, s2=s2: hT[:, c, s2 * 512:(s2 + 1) * 512], lambda c, s2=s2: [bh[s2][c]])

    def u_inproj(u):
        slot, sub, w_in_ap, w_out_ap, half = u
        for g in range(11):
            k = cnt["wi"] % 3; cnt["wi"] += 1
            P.dma("pool", lambda e, g=g, k=k: e.dma_start(out=wi[k][:], in_=w_in_ap[g]), writes=[bwi[k]])
            for s2 in range(2):
                for cp in range(2):
                    j = g * 2 + cp
                    q = cnt["pg"] % 2; cnt["pg"] += 1
                    for kc in range(8):
                        P.op("pe", lambda e, k=k, q=q, kc=kc, cp=cp, s2=s2: e.matmul(
                            pg[q][:], lhsT=wi[k][:, kc, cp * 128:(cp + 1) * 128], rhs=hT[:, kc, s2 * 512:(s2 + 1) * 512],
                            start=(kc == 0), stop=(kc == 7)), reads=[bwi[k], bh[s2][kc]], writes=[bpg[q]], nosync_same=True)
                    for kc in range(8):
                        P.op("pe", lambda e, k=k, q=q, kc=kc, cp=cp, s2=s2: e.matmul(
                            pu[q][:], lhsT=wi[k][:, kc, 256 + cp * 128:256 + (cp + 1) * 128], rhs=hT[:, kc, s2 * 512:(s2 + 1) * 512],
                            start=(kc == 0), stop=(kc == 7)), reads=[bwi[k], bh[s2][kc]], writes=[bpu[q]], nosync_same=True)
                    r = cnt["sg"] % 2; cnt["sg"] += 1
                    P.op("act", lambda e, q=q, r=r: e.activation(out=sg[r][:], in_=pg[q][:], func=AF.Silu),
                         reads=[bpg[q]], writes=[bsg[r]])
                    P.op("dve", lambda e, q=q, r=r, j=j, s2=s2: e.tensor_tensor(
                        out=act[:, j, s2 * 512:(s2 + 1) * 512], in0=pu[q][:], in1=sg[r][:], op=ALU.mult),
                        reads=[bpu[q], bsg[r]], writes=[bact[s2][j]])

    def u_outproj(u):
        slot, sub, w_in_ap, w_out_ap, half = u
        for og in range(4):
            k = cnt["wo"] % 2; cnt["wo"] += 1
            P.dma("pool", lambda e, og=og, k=k: e.dma_start(out=wo[k][:], in_=w_out_ap[og]), writes=[bwo[k]])
            for s2 in range(2):
                h4 = half * 2 + s2
                for ocl in range(2):
                    oc = og * 2 + ocl
                    q = cnt["po"] % 2; cnt["po"] += 1
                    for kc in range(22):
                        P.op("pe", lambda e, k=k, q=q, kc=kc, ocl=ocl, s2=s2: e.matmul(
                            po[q][:], lhsT=wo[k][:, kc, ocl * 128:(ocl + 1) * 128], rhs=act[:, kc, s2 * 512:(s2 + 1) * 512],
                            start=(kc == 0), stop=(kc == 21)), reads=[bwo[k], bact[s2][kc]], writes=[bpo[q]], nosync_same=True)
                    P.op("dve", lambda e, q=q, oc=oc, h4=h4: e.scalar_tensor_tensor(
                        out=x[:, oc, h4 * 512:(h4 + 1) * 512], in0=po[q][:], scalar=gm_ap(slot, sub, oc),
                        in1=x[:, oc, h4 * 512:(h4 + 1) * 512], op0=ALU.mult, op1=ALU.add),
                        reads=[bpo[q], bsmall[slot]], writes=[bx[h4][oc]])

    def run_units(units):
        if not units:
            return
        u_norm(units[0])
        for i, u in enumerate(units):
            u_inproj(u)
            if i + 1 < len(units):
                u_norm(units[i + 1])
            u_outproj(u)

    if F_in:
        load_small(0, modsA, gA)
        KO = F_in // 128
        otv = hT[:].rearrange("p a (b t) -> p (a b) t", t=512)
        nbuf_o = 16 // KO
        for kc in range(KO):
            P.dma("pool", lambda e, kc=kc: e.dma_start(out=act[:, kc, :], in_=wmo[:, kc, :]), writes=[bact[0][kc], bact[1][kc]])
        for h4 in range(4):
            k = h4 % nbuf_o
            for kc in range(KO):
                c16 = k * KO + kc
                P.dma("sp", lambda e, kc=kc, c16=c16, h4=h4: e.dma_start(out=otv[:, c16, :], in_=oT[kc, :, h4 * 512:(h4 + 1) * 512]),
                      writes=[bh[c16 % 2][c16 // 2]])
            for oc in range(8):
                q = cnt["po"] % 2; cnt["po"] += 1
                for kc in range(KO):
                    c16 = k * KO + kc
                    P.op("pe", lambda e, q=q, kc=kc, oc=oc, c16=c16: e.matmul(
                        po[q][:], lhsT=act[:, kc, oc * 128:(oc + 1) * 128], rhs=otv[:, c16, :],
                        start=(kc == 0), stop=(kc == KO - 1)), reads=[bact[oc // 4][kc], bh[c16 % 2][c16 // 2]], writes=[bpo[q]], nosync_same=True)
                P.op("dve", lambda e, q=q, oc=oc, h4=h4: e.scalar_tensor_tensor(
                    out=x[:, oc, h4 * 512:(h4 + 1) * 512], in0=po[q][:], scalar=gm_ap(0, 1, oc),
                    in1=x[:, oc, h4 * 512:(h4 + 1) * 512], op0=ALU.mult, op1=ALU.add),
                    reads=[bpo[q], bsmall[0]], writes=[bx[h4][oc]])
    outs = []
    units = []
    if F_in:
        units += [(0, 2, w_in2, w_out2, 0), (0, 2, w_in2, w_out2, 1)]
    if do_next:
        load_small(1, modsB, gB)
        units += [(1, 0, w_in1, w_out1, 0), (1, 0, w_in1, w_out1, 1)]
    run_units(units)
    if do_next:
        for h4 in range(4):
            s2 = h4 % 2
            norm_tile(h4, 1, 1, lambda c, s2=s2: hT[:, c, s2 * 512:(s2 + 1) * 512], lambda c, s2=s2: [bh[s2][c]])
            for c in range(8):
                b = P.buf(); outs.append(b)
                P.dma("sp", lambda e, c=c, s2=s2, h4=h4: e.dma_start(out=hT_out[c, :, h4 * 512:(h4 + 1) * 512],
                                                                     in_=hT[:, c, s2 * 512:(s2 + 1) * 512]),
                      reads=[bh[s2][c]], writes=[b])
    if final:
        P.dma("sp", lambda e: e.dma_start(out=gf_sb[:], in_=gF), writes=[bgf])
        for h4 in range(4):
            norm_tile(h4, 0, 0, lambda c, h4=h4: x[:, c, h4 * 512:(h4 + 1) * 512], lambda c, h4=h4: [bx[h4][c]], plain_g=gf_sb)
    for c in range(8):
        for h4 in range(4):
            b = P.buf(); outs.append(b)
            P.dma("sp", lambda e, c=c, h4=h4: e.dma_start(out=xT_out[c, :, h4 * 512:(h4 + 1) * 512], in_=x[:, c, h4 * 512:(h4 + 1) * 512]),
                  reads=[bx[h4][c]], writes=[b])
    P.final_wait("sp", outs)
    P.emit(); P.close()
    return nc


S = 8192
NQT = 16


def attn_core(P, cnt, R, kT, bk, kdim, vaug, bv, q_tile_fn, scale, out_dram, h, col0, masks, bmask, exp_fn=None):
    for j in range(NQT):
        qT, bq = q_tile_fn(j)
        oq = R["cnt_o"] % 2; R["cnt_o"] += 1
        po, bpo = R["po"][oq], R["bpo"][oq]
        nkb = 4 * j + 4

        def emit_S(kb, qT=qT, bq=bq):
            sq = R["cnt_s"] % 3; R["cnt_s"] += 1
            ps, bps = R["ps"][sq], R["bps"][sq]
            P.op("pe", lambda e, kb=kb, ps=ps, qT=qT: e.matmul(ps[:], lhsT=kT[0:kdim, kb * 128:(kb + 1) * 128], rhs=qT, start=True, stop=True),
                 reads=[bk, bq], writes=[bps], nosync_same=True)
            return ps, bps
        pend = [emit_S(0)]
        if nkb > 1:
            pend.append(emit_S(1))
        for kb in range(nkb):
            if kb + 2 < nkb:
                pend.append(emit_S(kb + 2))
            ps, bps = pend.pop(0)
            pq = R["cnt_p"] % 3; R["cnt_p"] += 1
            pt, bpt = R["pt"][pq], R["bpt"][pq]
            d = kb - 4 * j
            if exp_fn is not None:
                exp_fn(j, kb, ps, bps, pt, bpt)
            else:
                P.op("act", lambda e, ps=ps, pt=pt: e.activation(out=pt[:], in_=ps[:], func=AF.Exp, scale=scale), reads=[bps], writes=[bpt])
            if d >= 0:
                P.op("dve", lambda e, pt=pt, d=d: e.tensor_tensor(out=pt[:], in0=pt[:], in1=masks[:, d, :], op=ALU.mult),
                     reads=[bpt, bmask], writes=[bpt])
            for qs in range(4):
                if d > qs:
                    continue
                last = 4 * j + qs
                P.op("pe", lambda e, pt=pt, qs=qs, kb=kb, po=po, last=last: e.matmul(
                    po[:, qs, :], lhsT=pt[:, qs * 128:(qs + 1) * 128], rhs=vaug[:, kb, :], start=(kb == 0 and qs == 0), stop=(kb == last),
                    skip_group_check=True),
                    reads=[bpt, bv], writes=[bpo], nosync_same=True)
        rq = R["cnt_r"] % 2; R["cnt_r"] += 1
        rden, brden = R["rden"][rq], R["brden"][rq]
        ot, bot = R["ot"][rq], R["bot"][rq]
        P.op("dve", lambda e, po=po, rden=rden: e.reciprocal(out=rden[:], in_=po[:, :, 64]), reads=[bpo], writes=[brden])
        for qs in range(4):
            P.op("dve", lambda e, po=po, rden=rden, ot=ot, qs=qs: e.tensor_scalar(
                out=ot[:, qs, :], in0=po[:, qs, 0:64], scalar1=rden[:, qs:qs + 1], scalar2=None, op0=ALU.mult),
                reads=[bpo, brden], writes=[bot])
        b = P.buf(); R["outs"].append(b)
        P.dma("sp", lambda e, ot=ot, j=j: e.dma_start(
            out=out_dram[j * 512:(j + 1) * 512, col0:col0 + 64].rearrange("(a p) c -> p a c", p=128), in_=ot[:]),
            reads=[bot], writes=[b])


def attn_resources(P):
    R = {"cnt_o": 0, "cnt_s": 0, "cnt_p": 0, "cnt_r": 0, "outs": []}
    R["po"] = [P.psum(f"a_po{i}", [128, 4, 128])[:, :, 0:65] for i in range(2)]; R["bpo"] = [P.buf(excl=True) for _ in range(2)]
    R["ps"] = [P.psum(f"a_ps{i}", [128, 512]) for i in range(3)]; R["bps"] = [P.buf(excl=True) for _ in range(3)]
    R["pt"] = [P.sbuf(f"a_pt{i}", [128, 512], BF16) for i in range(3)]; R["bpt"] = [P.buf() for _ in range(3)]
    R["rden"] = [P.sbuf(f"a_rd{i}", [128, 4], F32) for i in range(2)]; R["brden"] = [P.buf() for _ in range(2)]
    R["ot"] = [P.sbuf(f"a_ot{i}", [128, 4, 64], BF16) for i in range(2)]; R["bot"] = [P.buf() for _ in range(2)]
    return R


def causal_masks_np():
    m = np.zeros((128, 4, 512), np.float32)
    p = np.arange(128)[:, None]; f = np.arange(512)[None, :]
    for d in range(4):
        m[:, d, :] = (128 * d + p <= f)
    return m


def build_mla():
    nc = bass.Bass("TRN2", target_bir_lowering=False)
    P = Prog(nc)
    dt = nc.dram_tensor
    hT_d = dt("hT", [8, 128, S], BF16, kind="ExternalInput").ap()
    w_in_d = dt("w_in", [128, 8, 832], F32, kind="ExternalInput").ap()
    wuq_d = dt("wuq", [4, 128, 3, 192], F32, kind="ExternalInput").ap()
    wuk_d = dt("wuk", [4, 128, 2, 64], F32, kind="ExternalInput").ap()
    wuv_d = dt("wuv", [128, 2, 256], F32, kind="ExternalInput").ap()
    qn_d = dt("qn", [128, 3], F32, kind="ExternalInput").ap()
    kvn_d = dt("kvn", [128, 2], F32, kind="ExternalInput").ap()
    cs_d = dt("cs", [96, 2, S], F32, kind="ExternalInput").ap()
    mask_d = dt("mask", [128, 4, 512], F32, kind="ExternalInput").ap()
    o_d = dt("o", [S, 256], BF16, kind="ExternalOutput").ap()

    w_in = P.sbuf("w_in_sb", [128, 8, 832], BF16); bw_in = P.buf()
    wuq = P.sbuf("wuq_sb", [128, 4, 3, 192], BF16); bwuq = P.buf()
    wuk = P.sbuf("wuk_sb", [128, 4, 2, 64], BF16); bwuk = P.buf()
    wuv = P.sbuf("wuv_sb", [128, 2, 256], BF16); bwuv = P.buf()
    qn = P.sbuf("qn_sb", [128, 3], F32); bqn = P.buf()
    kvn = P.sbuf("kvn_sb", [128, 2], F32); bkvn = P.buf()
    masks = P.sbuf("masks_sb", [128, 4, 512], BF16); bmask = P.buf()
    ones = P.sbuf("ones", [128, 128], BF16); bones = P.buf()
    cqn = P.sbuf("cqn", [128, 3, S], BF16); bcqn = [P.buf() for _ in range(NQT)]
    ckvn = P.sbuf("ckvn", [128, 2, S], BF16); bckvn = [P.buf() for _ in range(NQT)]
    kT1 = P.sbuf("kT", [96, S], BF16); bkr = [P.buf() for _ in range(NQT)]; bkT1 = P.buf()
    ht = [P.sbuf(f"ht{i}", [128, 8, 512], BF16) for i in range(2)]; bht = [[P.buf() for _ in range(8)] for _ in range(2)]
    lat = [P.sbuf(f"lat{i}", [128, 512], F32) for i in range(3)]; blat = [P.buf() for _ in range(3)]
    sqt = [P.sbuf(f"sqt{i}", [128, 512], BF16) for i in range(2)]; bsqt = [P.buf() for _ in range(2)]
    rstd = P.sbuf("rstd", [128, 512], F32); brstd = P.buf()
    cst = [P.sbuf(f"cst{i}", [96, 2, 512], F32) for i in range(2)]; bcst = [P.buf() for _ in range(2)]
    t1 = P.sbuf("t1", [96, 512], F32); bt1 = P.buf()
    t2 = P.sbuf("t2", [96, 512], F32); bt2 = P.buf()
    pl = [P.psum(f"pl{i}", [128, 512]) for i in range(2)]; bpl = [P.buf(excl=True) for _ in range(2)]
    pn = P.psum("pn", [128, 512]); bpn = P.buf(excl=True)
    cnt = {"lat": 0, "sq": 0, "pl": 0}

    P.dma("pool", lambda e: e.dma_start(out=w_in[:], in_=w_in_d), writes=[bw_in])
    for h in range(4):
        P.dma("pool", lambda e, h=h: e.dma_start(out=wuq[:, h], in_=wuq_d[h]), writes=[bwuq])
        P.dma("pool", lambda e, h=h: e.dma_start(out=wuk[:, h], in_=wuk_d[h]), writes=[bwuk])
    P.dma("pool", lambda e: e.dma_start(out=wuv[:], in_=wuv_d), writes=[bwuv])
    P.dma("pool", lambda e: e.dma_start(out=masks[:], in_=mask_d), writes=[bmask])
    P.dma("sp", lambda e: e.dma_start(out=qn[:], in_=qn_d), writes=[bqn])
    P.dma("sp", lambda e: e.dma_start(out=kvn[:], in_=kvn_d), writes=[bkvn])
    P.op("dve", lambda e: e.memset(ones[:], 1.0), writes=[bones])

    def proj_chunk(k, col0, M, tile_rhs_bufs):
        q = cnt["pl"] % 2; cnt["pl"] += 1
        for kc in range(8):
            P.op("pe", lambda e, kc=kc, q=q: e.matmul(pl[q][0:M, :], lhsT=w_in[:, kc, col0:col0 + M], rhs=ht[k][:, kc, :],
                                                      start=(kc == 0), stop=(kc == 7)),
                 reads=[bw_in, bht[k][kc]], writes=[bpl[q]], nosync_same=True)
        return q

    for t in range(NQT):
        k = t % 2
        tsl = slice(t * 512, (t + 1) * 512)
        for kc in range(8):
            P.dma("sp", lambda e, kc=kc, k=k, tsl=tsl: e.dma_start(out=ht[k][:, kc, :], in_=hT_d[kc, :, tsl]), writes=[bht[k][kc]])
        for grp, (c0, nch, gain, bgain, dst, bdst, rank) in enumerate([(0, 3, qn, bqn, cqn, bcqn, 384), (384, 2, kvn, bkvn, ckvn, bckvn, 256)]):
            lats = []
            for c in range(nch):
                q = proj_chunk(k, c0 + c * 128, 128, None)
                li = cnt["lat"] % 3; cnt["lat"] += 1
                lats.append(li)
                P.op("act", lambda e, q=q, li=li: e.activation(out=lat[li][:], in_=pl[q][:], func=AF.Identity), reads=[bpl[q]], writes=[blat[li]])
                si = cnt["sq"] % 2; cnt["sq"] += 1
                P.op("dve", lambda e, li=li, si=si: e.tensor_tensor(out=sqt[si][:], in0=lat[li][:], in1=lat[li][:], op=ALU.mult),
                     reads=[blat[li]], writes=[bsqt[si]])
                P.op("pe", lambda e, si=si, c=c, nch=nch: e.matmul(pn[:], lhsT=ones[:], rhs=sqt[si][:], start=(c == 0), stop=(c == nch - 1)),
                     reads=[bones, bsqt[si]], writes=[bpn], nosync_same=True)
            P.op("act", lambda e, rank=rank: e.activation(out=rstd[:], in_=pn[:], func=AF.Sqrt, bias=1e-6, scale=1.0 / rank), reads=[bpn], writes=[brstd])
            P.op("dve", lambda e: e.reciprocal(out=rstd[:], in_=rstd[:]), reads=[brstd], writes=[brstd])
            for c in range(nch):
                li = lats[c]
                P.op("dve", lambda e, li=li, c=c, dst=dst, gain=gain, tsl=tsl: e.scalar_tensor_tensor(
                    out=dst[:, c, tsl], in0=lat[li][:], scalar=gain[:, c:c + 1], in1=rstd[:], op0=ALU.mult, op1=ALU.mult),
                    reads=[blat[li], bgain, brstd], writes=[bdst[t]])
        kc_ = t % 2
        P.dma("sp", lambda e, kc_=kc_, tsl=tsl: e.dma_start(out=cst[kc_][:], in_=cs_d[:, :, tsl]), writes=[bcst[kc_]])
        qa = proj_chunk(k, 640, 96, None)
        P.op("dve", lambda e, qa=qa, kc_=kc_: e.tensor_tensor(out=t1[64:96, :], in0=pl[qa][64:96, :], in1=cst[kc_][64:96, 0, :], op=ALU.mult),
             reads=[bpl[qa], bcst[kc_]], writes=[bt1])
        qb = proj_chunk(k, 736, 96, None)
        P.op("dve", lambda e, qb=qb, kc_=kc_: e.tensor_tensor(out=t2[64:96, :], in0=pl[qb][64:96, :], in1=cst[kc_][64:96, 1, :], op=ALU.mult),
             reads=[bpl[qb], bcst[kc_]], writes=[bt2])
        P.op("dve", lambda e, tsl=tsl: e.tensor_tensor(out=kT1[64:96, tsl], in0=t1[64:96, :], in1=t2[64:96, :], op=ALU.add),
             reads=[bt1, bt2], writes=[bkr[t]])

    R = attn_resources(P)
    kT = [kT1, kT1]; bkT = [bkT1, bkT1]
    va1 = P.sbuf("va", [128, 64, 65], BF16); bva1 = P.buf()
    va = [va1, va1]; bva = [bva1, bva1]
    qt = [P.sbuf(f"qt{i}", [96, 512], BF16) for i in range(2)]; bqt = [P.buf() for _ in range(2)]
    cntq = {"q": 0}
    for h in range(4):
        hb = h % 2
        P.op("pool", lambda e, hb=hb: e.memset(va[hb][:, :, 64:65], 1.0), writes=[bva[hb]])
        for t in range(NQT):
            tsl = slice(t * 512, (t + 1) * 512)
            q = cnt["pl"] % 2; cnt["pl"] += 1
            for kc in range(2):
                P.op("pe", lambda e, kc=kc, q=q, h=h, tsl=tsl: e.matmul(pl[q][0:64, :], lhsT=wuk[:, h, kc, :], rhs=ckvn[:, kc, tsl],
                                                                         start=(kc == 0), stop=(kc == 1)),
                     reads=[bwuk, bckvn[t]], writes=[bpl[q]], nosync_same=True)
            P.op("act", lambda e, q=q, hb=hb, tsl=tsl: e.activation(out=kT[hb][0:64, tsl], in_=pl[q][0:64, :], func=AF.Identity),
                 reads=[bpl[q]] + (bkr if (t == 0) else []), writes=[bkT[hb]])
            for tb in range(4):
                blk = t * 4 + tb
                q = cnt["pl"] % 2; cnt["pl"] += 1
                for kc in range(2):
                    P.op("pe", lambda e, kc=kc, q=q, h=h, blk=blk: e.matmul(
                        pl[q][:, 0:64], lhsT=ckvn[:, kc, blk * 128:(blk + 1) * 128], rhs=wuv[:, kc, h * 64:(h + 1) * 64],
                        start=(kc == 0), stop=(kc == 1)), reads=[bwuv, bckvn[t]], writes=[bpl[q]], nosync_same=True)
                P.op("act", lambda e, q=q, hb=hb, blk=blk: e.activation(out=va[hb][:, blk, 0:64], in_=pl[q][:, 0:64], func=AF.Identity),
                     reads=[bpl[q]], writes=[bva[hb]])

        def q_tile(j, h=h):
            tsl = slice(j * 512, (j + 1) * 512)
            kc_ = cntq["q"] % 2; cntq["q"] += 1
            P.dma("sp", lambda e, kc_=kc_, tsl=tsl: e.dma_start(out=cst[kc_][:], in_=cs_d[:, :, tsl]), writes=[bcst[kc_]])
            qa = cnt["pl"] % 2; cnt["pl"] += 1
            for kc in range(3):
                P.op("pe", lambda e, kc=kc, qa=qa, tsl=tsl: e.matmul(pl[qa][0:96, :], lhsT=wuq[:, h, kc, 0:96], rhs=cqn[:, kc, tsl],
                                                                    start=(kc == 0), stop=(kc == 2)),
                     reads=[bwuq, bcqn[j]], writes=[bpl[qa]], nosync_same=True)
            P.op("dve", lambda e, qa=qa, kc_=kc_: e.tensor_tensor(out=t1[:, :], in0=pl[qa][0:96, :], in1=cst[kc_][:, 0, :], op=ALU.mult),
                 reads=[bpl[qa], bcst[kc_]], writes=[bt1])
            qb = cnt["pl"] % 2; cnt["pl"] += 1
            for kc in range(3):
                P.op("pe", lambda e, kc=kc, qb=qb, tsl=tsl: e.matmul(pl[qb][0:96, :], lhsT=wuq[:, h, kc, 96:192], rhs=cqn[:, kc, tsl],
                                                                    start=(kc == 0), stop=(kc == 2)),
                     reads=[bwuq, bcqn[j]], writes=[bpl[qb]], nosync_same=True)
            P.op("dve", lambda e, qb=qb, kc_=kc_: e.tensor_tensor(out=t2[:, :], in0=pl[qb][0:96, :], in1=cst[kc_][:, 1, :], op=ALU.mult),
                 reads=[bpl[qb], bcst[kc_]], writes=[bt2])
            P.op("dve", lambda e, kc_=kc_: e.tensor_tensor(out=qt[kc_][:, :], in0=t1[:, :], in1=t2[:, :], op=ALU.add),
                 reads=[bt1, bt2], writes=[bqt[kc_]])
            return qt[kc_][:, :], bqt[kc_]

        attn_core(P, cnt, R, kT[hb], bkT[hb], 96, va[hb], bva[hb], q_tile, 96 ** -0.5, o_d, h, h * 64, masks, bmask)
    print('mla sbuf remaining', nc.sbuf_bytes_remaining, {e: len(v) for e, v in P.streams.items()})
    P.final_wait("sp", R["outs"])
    P.emit(); P.close()
    return nc


def mla_host_inputs(inp, hT_full, b, g):
    w_in = inp["mla_w_in"][0]
    kr = w_in[:, 640:672]
    krp = np.concatenate([kr[:, 16:], kr[:, :16]], axis=1)
    z64 = np.zeros((1024, 64), np.float32)
    w_in_l = np.concatenate([w_in[:, :640], z64, kr, z64, krp], axis=1)
    w_in_l = w_in_l.reshape(8, 128, 832).transpose(1, 0, 2)
    wuq = inp["mla_w_uq"][0].reshape(384, 16, 96)
    wukv = inp["mla_w_ukv"][0].reshape(256, 16, 128)
    wq_l = np.zeros((4, 128, 3, 192), np.float32)
    wk_l = np.zeros((4, 128, 2, 64), np.float32)
    wv_l = np.zeros((128, 2, 256), np.float32)
    for hh in range(4):
        H = g * 4 + hh
        wq = wuq[:, H, :]
        wqp = np.concatenate([wq[:, :64], wq[:, 80:96], wq[:, 64:80]], axis=1)
        wq_l[hh] = np.concatenate([wq, wqp], axis=1).reshape(3, 128, 192).transpose(1, 0, 2)
        wk_l[hh] = wukv[:, H, :64].reshape(2, 128, 64).transpose(1, 0, 2)
        wv_l[:, :, hh * 64:(hh + 1) * 64] = wukv[:, H, 64:].reshape(2, 128, 64).transpose(1, 0, 2)
    inv = 10000.0 ** (-np.arange(0, 32, 2, dtype=np.float32) / 32)
    ang = np.arange(S, dtype=np.float32)[:, None] * inv[None, :]
    cos, sin = np.cos(ang).T, np.sin(ang).T
    cs = np.zeros((96, 2, S), np.float32)
    cs[:64, 0] = 1.0
    cs[64:80, 0] = cos; cs[80:96, 0] = cos
    cs[64:80, 1] = -sin; cs[80:96, 1] = sin
    return {"hT": hT_full, "w_in": np.ascontiguousarray(w_in_l),
            "wuq": wq_l, "wuk": wk_l, "wuv": wv_l,
            "qn": np.ascontiguousarray(inp["mla_q_norm"][0].reshape(3, 128).T), "kvn": np.ascontiguousarray(inp["mla_kv_norm"][0].reshape(2, 128).T),
            "cs": cs, "mask": causal_masks_np()}


S = 8192
NCH = 64
NEG = -0.6065306597126334


def build_rwkv():
    nc = bass.Bass("TRN2", target_bir_lowering=False)
    P = Prog(nc)
    dt = nc.dram_tensor
    hT_d = dt("hT", [8, 128, S], BF16, kind="ExternalInput").ap()
    mu_d = dt("mu", [128, 6, 8], F32, kind="ExternalInput").ap()
    wrkv_d = dt("wrkv", [128, 3, 8, 256], F32, kind="ExternalInput").ap()
    wl1_d = dt("wl1", [128, 8, 288], F32, kind="ExternalInput").ap()
    wl2_d = dt("wl2", [128, 4, 256], F32, kind="ExternalInput").ap()
    rows_d = dt("rows", [128, 7, 256], F32, kind="ExternalInput").ap()
    cm_d = dt("cm", [128, 5, 128], F32, kind="ExternalInput").ap()
    o_d = dt("o", [S, 256], BF16, kind="ExternalOutput").ap()

    mu = P.sbuf("mu_sb", [128, 6, 8], F32); bmu = P.buf()
    wrkv = P.sbuf("wrkv_sb", [128, 3, 8, 256], BF16); bwrkv = P.buf()
    wl1 = P.sbuf("wl1_sb", [128, 8, 288], BF16); bwl1 = P.buf()
    wl2 = P.sbuf("wl2_sb", [128, 4, 256], BF16); bwl2 = P.buf()
    rows = P.sbuf("rows_sb", [128, 7, 256], F32); brows = P.buf()
    cm = P.sbuf("cm_sb", [128, 5, 128], F32); bcm = P.buf()
    TriT, ones, mST, mL, ident = (cm[:, i, :] for i in range(5))
    P.dma("sp", lambda e: e.dma_start(out=mu[:], in_=mu_d), writes=[bmu])
    P.dma("pool", lambda e: e.dma_start(out=wl2[:], in_=wl2_d), writes=[bwl2])
    P.dma("sp", lambda e: e.dma_start(out=rows[:], in_=rows_d), writes=[brows])
    P.dma("sp", lambda e: e.dma_start(out=cm[:], in_=cm_d), writes=[bcm])
    omu = P.sbuf("omu", [128, 6, 8], F32); bomu = P.buf()
    P.op("dve", lambda e: e.tensor_scalar(out=omu[:], in0=mu[:], scalar1=-1.0, scalar2=1.0, op0=ALU.mult, op1=ALU.add), reads=[bmu], writes=[bomu])
    wrkvB = P.sbuf("wrkvB_sb", [128, 3, 8, 256], BF16); bwrkvB = P.buf()
    wl1B = P.sbuf("wl1B_sb", [128, 8, 288], BF16); bwl1B = P.buf()
    stg = P.sbuf("stg", [128, 8, 288], F32); bstg = P.buf()
    for n in range(3):
        P.dma("sp", lambda e, n=n: e.dma_start(out=stg[:, :, 0:256], in_=wrkv_d[:, n, :, :]), writes=[bstg])
        for kc in range(8):
            P.op("dve", lambda e, n=n, kc=kc: e.tensor_scalar(out=wrkv[:, n, kc, :], in0=stg[:, kc, 0:256], scalar1=omu[:, n, kc:kc + 1], scalar2=None, op0=ALU.mult),
                 reads=[bstg, bomu], writes=[bwrkv])
            P.op("dve", lambda e, n=n, kc=kc: e.tensor_scalar(out=wrkvB[:, n, kc, :], in0=stg[:, kc, 0:256], scalar1=mu[:, n, kc:kc + 1], scalar2=None, op0=ALU.mult),
                 reads=[bstg, bmu], writes=[bwrkvB])
    P.dma("sp", lambda e: e.dma_start(out=stg[:], in_=wl1_d), writes=[bstg])
    for kc in range(8):
        for (n, c0, c1) in [(3, 0, 64), (4, 64, 128), (5, 128, 288)]:
            P.op("dve", lambda e, n=n, kc=kc, c0=c0, c1=c1: e.tensor_scalar(out=wl1[:, kc, c0:c1], in0=stg[:, kc, c0:c1], scalar1=omu[:, n, kc:kc + 1], scalar2=None, op0=ALU.mult),
                 reads=[bstg, bomu], writes=[bwl1])
            P.op("dve", lambda e, n=n, kc=kc, c0=c0, c1=c1: e.tensor_scalar(out=wl1B[:, kc, c0:c1], in0=stg[:, kc, c0:c1], scalar1=mu[:, n, kc:kc + 1], scalar2=None, op0=ALU.mult),
                 reads=[bstg, bmu], writes=[bwl1B])

    hA = [P.sbuf(f"hA{i}", [128, 8, 128], BF16) for i in range(2)]; bhA = [P.buf() for _ in range(2)]
    hB = [P.sbuf(f"hB{i}", [128, 8, 128], BF16) for i in range(2)]; bhB = [P.buf() for _ in range(2)]
    l1 = P.sbuf("l1", [128, 4, 128], BF16); bl1 = [P.buf() for _ in range(4)]
    F = {}
    def ft(name, shape=(128, 256), dtype=F32):
        F[name] = ([P.sbuf(f"f_{name}{i}", list(shape), dtype) for i in range(2)], [P.buf() for _ in range(2)])
        return F[name]
    for nm in ["r", "kraw", "v", "lw", "a", "gg", "kk", "kmod", "kb", "t0", "t1", "cs", "e1", "e2", "e3", "e4",
               "kat", "rt", "kbh", "kh", "kg", "kbg", "y", "cen", "yn"]:
        ft(nm)
    ft("ss", (128, 4)); ft("rn", (128, 4)); ft("bc", (128, 4)); ft("mean", (128, 4)); ft("var", (128, 4)); ft("rstd", (128, 4))
    ft("gcol", (64, 4)); ft("z", (128, 256), BF16)
    H = P.sbuf("H", [64, 4, 64], F32); bH = [P.buf() for _ in range(4)]
    featT2 = [[P.sbuf(f"featT{j}_{i}", [64, 4, 128], F32) for i in range(4)] for j in range(2)]; bfeat2 = [[P.buf() for _ in range(4)] for _ in range(2)]
    def sq2(nm):
        return ([[P.sbuf(f"{nm}{h}_{i}", [128, 128], F32) for i in range(2)] for h in range(4)], [[P.buf() for _ in range(2)] for _ in range(4)])
    def sq1(nm, w=128):
        return ([P.sbuf(f"{nm}{h}", [128, w], F32) for h in range(4)], [P.buf() for _ in range(4)])
    A_, bA = sq2("A"); Q_, bQ = sq2("Q"); Tt_, bTt = sq2("Tt"); Tm_, bTm = sq2("Tm")
    MbT, bMbT = sq1("MbT"); LkT, bLkT = sq1("LkT"); MkT, bMkT = sq1("MkT")
    W1s, bW1s = sq1("W1s", 64); Us, bUs = sq1("Us", 64)
    bank = [P.psum(f"bk{i}", [128, 512]) for i in range(8)]
    bb = [[P.buf(excl=True)] * 4 for _ in range(8)]

    for hh in range(4):
        P.op("dve", lambda e, hh=hh: e.memset(H[:, hh, :], 0.0), writes=[bH[hh]])

    def dve(fn, reads, writes): P.op("dve", fn, reads=reads, writes=writes)
    def act(fn, reads, writes): P.op("act", fn, reads=reads, writes=writes)
    def mm(out, lhsT, rhs, start, stop, reads, writes): P.op("pe", lambda e: e.matmul(out, lhsT=lhsT, rhs=rhs, start=start, stop=stop), reads=reads, writes=writes, nosync_same=True)

    def stageA(c):
        k = c % 2
        lo = c * 128
        par = c % 2
        def T(name): return F[name][0][par]
        def B(name): return F[name][1][par]
        featT = featT2[par]; bfeat = bfeat2[par]
        P.dma("sp", lambda e, k=k, lo=lo: e.dma_start(out=hA[k][:], in_=hT_d[:, :, lo:lo + 128].rearrange("k p t -> p k t")), writes=[bhA[k]])
        if c == 0:
            P.op("pool", lambda e, k=k: e.memset(hB[k][:, :, 0:1], 0.0), writes=[bhB[k]])
            P.dma("sp", lambda e, k=k: e.dma_start(out=hB[k][:, :, 1:128], in_=hT_d[:, :, 0:127].rearrange("k p t -> p k t")), writes=[bhB[k]])
        else:
            P.dma("sp", lambda e, k=k, lo=lo: e.dma_start(out=hB[k][:], in_=hT_d[:, :, lo - 1:lo + 127].rearrange("k p t -> p k t")), writes=[bhB[k]])
        yield
        regs = [(0, 0, 0), (0, 1, 1), (1, 0, 2)]
        for n, (bk, half, _) in enumerate(regs):
            for kc in range(8):
                mm(bank[bk][:, half * 256:(half + 1) * 256], hA[k][:, kc, :], wrkv[:, n, kc, :], kc == 0, False, [bhA[k], bwrkv], [bb[bk][half]])
                mm(bank[bk][:, half * 256:(half + 1) * 256], hB[k][:, kc, :], wrkvB[:, n, kc, :], False, kc == 7, [bhB[k], bwrkvB], [bb[bk][half]])
        yield
        for (o_ap, c0, c1, reg) in [(bank[3][0:64, 0:128], 0, 64, 0), (bank[3][0:64, 128:256], 64, 128, 1),
                                     (bank[3][:, 256:384], 128, 256, 2), (bank[3][0:32, 384:512], 256, 288, 3)]:
            for kc in range(8):
                mm(o_ap, wl1[:, kc, c0:c1], hA[k][:, kc, :], kc == 0, False, [bwl1, bhA[k]], [bb[3][reg]])
                mm(o_ap, wl1B[:, kc, c0:c1], hB[k][:, kc, :], False, kc == 7, [bwl1B, bhB[k]], [bb[3][reg]])
        act(lambda e: e.activation(out=l1[0:64, 0, :], in_=bank[3][0:64, 0:128], func=AF.Tanh), [bb[3][0]], [bl1[0]])
        act(lambda e: e.activation(out=l1[0:64, 1, :], in_=bank[3][0:64, 128:256], func=AF.Identity), [bb[3][1]], [bl1[1]])
        act(lambda e: e.activation(out=l1[:, 2, :], in_=bank[3][:, 256:384], func=AF.Sigmoid), [bb[3][2]], [bl1[2]])
        act(lambda e: e.activation(out=l1[0:32, 3, :], in_=bank[3][0:32, 384:512], func=AF.Sigmoid), [bb[3][3]], [bl1[3]])
        yield
        act(lambda e: e.activation(out=T("r")[:], in_=bank[0][:, 0:256], func=AF.Identity), [bb[0][0]], [B("r")])
        act(lambda e: e.activation(out=T("kraw")[:], in_=bank[0][:, 256:512], func=AF.Identity), [bb[0][1]], [B("kraw")])
        act(lambda e: e.activation(out=T("v")[:], in_=bank[1][:, 0:256], func=AF.Identity), [bb[1][0]], [B("v")])
        yield
        mm(bank[1][:, 256:512], l1[0:64, 0, :], wl2[0:64, 0, :], True, True, [bl1[0], bwl2], [bb[1][1]])
        mm(bank[2][:, 0:256], l1[0:64, 1, :], wl2[0:64, 1, :], True, True, [bl1[1], bwl2], [bb[2][0]])
        mm(bank[2][:, 256:512], l1[:, 2, :], wl2[:, 2, :], True, False, [bl1[2], bwl2], [bb[2][1]])
        mm(bank[2][:, 256:512], l1[0:32, 3, :], wl2[0:32, 3, :], False, True, [bl1[3], bwl2], [bb[2][1]])
        dve(lambda e: e.tensor_tensor(out=T("lw")[:], in0=bank[1][:, 256:512], in1=rows[:, 0, :], op=ALU.add), [bb[1][1], brows], [B("lw")])
        act(lambda e: e.activation(out=T("lw")[:], in_=T("lw")[:], func=AF.Sigmoid), [B("lw")], [B("lw")])
        dve(lambda e: e.tensor_scalar(out=T("lw")[:], in0=T("lw")[:], scalar1=NEG, scalar2=None, op0=ALU.mult), [B("lw")], [B("lw")])
        dve(lambda e: e.tensor_tensor(out=T("a")[:], in0=bank[2][:, 0:256], in1=rows[:, 1, :], op=ALU.add), [bb[2][0], brows], [B("a")])
        act(lambda e: e.activation(out=T("a")[:], in_=T("a")[:], func=AF.Sigmoid), [B("a")], [B("a")])
        act(lambda e: e.activation(out=T("gg")[:], in_=bank[2][:, 256:512], func=AF.Identity), [bb[2][1]], [B("gg")])
        yield
        dve(lambda e: e.tensor_tensor(out=T("kk")[:], in0=T("kraw")[:], in1=rows[:, 2, :], op=ALU.mult), [B("kraw"), brows], [B("kk")])
        dve(lambda e: e.tensor_tensor(out=T("t0")[:], in0=T("kk")[:], in1=T("kk")[:], op=ALU.mult), [B("kk")], [B("t0")])
        dve(lambda e: e.tensor_reduce(out=T("ss")[:], in_=T("t0")[:].rearrange("p (h n) -> p h n", h=4), axis=AX.X, op=ALU.add), [B("t0")], [B("ss")])
        act(lambda e: e.activation(out=T("rn")[:], in_=T("ss")[:], func=AF.Sqrt), [B("ss")], [B("rn")])
        dve(lambda e: e.tensor_scalar(out=T("rn")[:], in0=T("rn")[:], scalar1=1e-12, scalar2=None, op0=ALU.max), [B("rn")], [B("rn")])
        dve(lambda e: e.reciprocal(out=T("rn")[:], in_=T("rn")[:]), [B("rn")], [B("rn")])
        for hh in range(4):
            sl = slice(hh * 64, (hh + 1) * 64)
            dve(lambda e, hh=hh, sl=sl: e.tensor_scalar(out=T("kk")[:, sl], in0=T("kk")[:, sl], scalar1=T("rn")[:, hh:hh + 1], scalar2=None, op0=ALU.mult),
                [B("kk"), B("rn")], [B("kk")])
        dve(lambda e: e.scalar_tensor_tensor(out=T("t1")[:], in0=T("a")[:], scalar=-1.0, in1=rows[:, 3, :], op0=ALU.add, op1=ALU.mult), [B("a"), brows], [B("t1")])
        dve(lambda e: e.scalar_tensor_tensor(out=T("kmod")[:], in0=T("t1")[:], scalar=1.0, in1=T("kraw")[:], op0=ALU.add, op1=ALU.mult), [B("t1"), B("kraw")], [B("kmod")])
        dve(lambda e: e.tensor_tensor(out=T("kb")[:], in0=T("kk")[:], in1=T("a")[:], op=ALU.mult), [B("kk"), B("a")], [B("kb")])
        yield
        dve(lambda e: e.tensor_tensor(out=T("t0")[:], in0=T("r")[:], in1=T("kmod")[:], op=ALU.mult), [B("r"), B("kmod")], [B("t0")])
        dve(lambda e: e.tensor_tensor(out=T("t0")[:], in0=T("t0")[:], in1=rows[:, 4, :], op=ALU.mult), [B("t0"), brows], [B("t0")])
        dve(lambda e: e.tensor_reduce(out=T("bc")[:], in_=T("t0")[:].rearrange("p (h n) -> p h n", h=4), axis=AX.X, op=ALU.add), [B("t0")], [B("bc")])
        yield
        mm(bank[0][:, 0:256], TriT, T("lw")[:], True, True, [bcm, B("lw")], [bb[0][0]])
        mm(bank[0][:, 256:512], ones, T("lw")[:], True, True, [bcm, B("lw")], [bb[0][1]])
        act(lambda e: e.activation(out=T("cs")[:], in_=bank[0][:, 0:256], func=AF.Identity), [bb[0][0]], [B("cs")])
        act(lambda e: e.activation(out=T("e2")[:], in_=T("cs")[:], func=AF.Exp), [B("cs")], [B("e2")])
        act(lambda e: e.activation(out=T("e3")[:], in_=T("cs")[:], func=AF.Exp, scale=-1.0), [B("cs")], [B("e3")])
        dve(lambda e: e.tensor_tensor(out=T("e1")[:], in0=T("cs")[:], in1=T("lw")[:], op=ALU.subtract), [B("cs"), B("lw")], [B("e1")])
        act(lambda e: e.activation(out=T("e1")[:], in_=T("e1")[:], func=AF.Exp), [B("e1")], [B("e1")])
        dve(lambda e: e.tensor_tensor(out=T("e4")[:], in0=bank[0][:, 256:512], in1=T("cs")[:], op=ALU.subtract), [bb[0][1], B("cs")], [B("e4")])
        act(lambda e: e.activation(out=T("e4")[:], in_=T("e4")[:], func=AF.Exp), [B("e4")], [B("e4")])
        dve(lambda e: e.scalar_tensor_tensor(out=T("kat")[:], in0=T("kk")[:], scalar=-1.0, in1=T("e1")[:], op0=ALU.mult, op1=ALU.mult), [B("kk"), B("e1")], [B("kat")])
        dve(lambda e: e.tensor_tensor(out=T("rt")[:], in0=T("r")[:], in1=T("e2")[:], op=ALU.mult), [B("r"), B("e2")], [B("rt")])
        dve(lambda e: e.tensor_tensor(out=T("kbh")[:], in0=T("kb")[:], in1=T("e3")[:], op=ALU.mult), [B("kb"), B("e3")], [B("kbh")])
        dve(lambda e: e.tensor_tensor(out=T("kh")[:], in0=T("kmod")[:], in1=T("e3")[:], op=ALU.mult), [B("kmod"), B("e3")], [B("kh")])
        dve(lambda e: e.tensor_tensor(out=T("kg")[:], in0=T("kmod")[:], in1=T("e4")[:], op=ALU.mult), [B("kmod"), B("e4")], [B("kg")])
        dve(lambda e: e.tensor_tensor(out=T("kbg")[:], in0=T("kb")[:], in1=T("e4")[:], op=ALU.mult), [B("kb"), B("e4")], [B("kbg")])
        yield
        for hh in range(4):
            mm(bank[1][0:64, 2 * hh:2 + 2 * hh], T("lw")[:, hh * 64:(hh + 1) * 64], ones[:, 0:2], True, True, [B("lw"), bcm], [bb[1][3]])
        act(lambda e: e.activation(out=T("gcol")[:], in_=bank[1][0:64, 0:8].rearrange("p (h t) -> p h t", t=2)[:, :, 0], func=AF.Exp), [bb[1][3]], [B("gcol")])

    def stageB(c):
        k = c % 2
        lo = c * 128
        par = c % 2
        def T(name): return F[name][0][par]
        def B(name): return F[name][1][par]
        featT = featT2[par]; bfeat = bfeat2[par]
        HS = range(4)
        sls = [slice(hh * 64, (hh + 1) * 64) for hh in HS]
        yield
        for hh in HS:
            tb = 2 + hh % 2
            for xi, nm in enumerate(["kat", "rt", "kbh", "kh"]):
                P.op("pe", lambda e, xi=xi, nm=nm, tb=tb, hh=hh: e.transpose(bank[tb][0:64, xi * 128:(xi + 1) * 128], T(nm)[:, sls[hh]], ident),
                     reads=[B(nm), bcm], writes=[bb[tb][0]], nosync_same=True)
            act(lambda e, hh=hh, tb=tb: e.activation(out=featT[hh][:].rearrange("p a b -> p (a b)"), in_=bank[tb][0:64, :], func=AF.Identity), [bb[tb][0]], [bfeat[hh]])
        kaT = [featT[hh][:, 0, :] for hh in HS]; rT = [featT[hh][:, 1, :] for hh in HS]
        kbT = [featT[hh][:, 2, :] for hh in HS]; khT = [featT[hh][:, 3, :] for hh in HS]
        karT = [featT[hh][:, 0:2, :].rearrange("p a b -> p (a b)") for hh in HS]
        SB = [bank[4 + hh] for hh in HS]; bSB = [bb[4 + hh][0] for hh in HS]
        yield
        for hh in HS:
            mm(SB[hh][:, 0:256], kbT[hh], karT[hh], True, True, [bfeat[hh]], [bSB[hh]])
        yield
        for hh in HS:
            dve(lambda e, hh=hh: e.tensor_tensor(out=A_[hh][0][:], in0=SB[hh][:, 0:128], in1=mST, op=ALU.mult), [bSB[hh], bcm], [bA[hh][0]])
            dve(lambda e, hh=hh: e.tensor_tensor(out=MbT[hh][:], in0=SB[hh][:, 128:256], in1=TriT, op=ALU.mult), [bSB[hh], bcm], [bMbT[hh]])
        yield
        for hh in HS:
            mm(SB[hh][:, 0:256], khT[hh], karT[hh], True, True, [bfeat[hh]], [bSB[hh]])
        yield
        for hh in HS:
            dve(lambda e, hh=hh: e.tensor_tensor(out=LkT[hh][:], in0=SB[hh][:, 0:128], in1=mST, op=ALU.mult), [bSB[hh], bcm], [bLkT[hh]])
            dve(lambda e, hh=hh: e.tensor_tensor(out=MkT[hh][:], in0=SB[hh][:, 128:256], in1=TriT, op=ALU.mult), [bSB[hh], bcm], [bMkT[hh]])
        yield
        for hh in HS:
            mm(SB[hh][:, 0:128], kaT[hh], kbT[hh], True, True, [bfeat[hh]], [bSB[hh]])
        yield
        for hh in HS:
            dve(lambda e, hh=hh: e.tensor_tensor(out=Q_[hh][0][:], in0=SB[hh][:, 0:128], in1=mL, op=ALU.mult), [bSB[hh], bcm], [bQ[hh][0]])
            dve(lambda e, hh=hh: e.tensor_tensor(out=Tt_[hh][0][:], in0=A_[hh][0][:], in1=ident, op=ALU.add), [bA[hh][0], bcm], [bTt[hh][0]])
            dve(lambda e, hh=hh: e.tensor_tensor(out=Tm_[hh][0][:], in0=Q_[hh][0][:], in1=ident, op=ALU.add), [bQ[hh][0], bcm], [bTm[hh][0]])
        cur = 0
        yield
        for s_ in range(1, 7):
            nx = 1 - cur
            yield
            for hh in HS:
                mm(SB[hh][:, 0:128], Q_[hh][cur][:], A_[hh][cur][:], True, True, [bQ[hh][cur], bA[hh][cur]], [bSB[hh]])
            yield
            for hh in HS:
                act(lambda e, hh=hh, nx=nx: e.activation(out=A_[hh][nx][:], in_=SB[hh][:, 0:128], func=AF.Identity), [bSB[hh]], [bA[hh][nx]])
            if s_ < 6:
                yield
                for hh in HS:
                    mm(SB[hh][:, 128:256], A_[hh][cur][:], Q_[hh][cur][:], True, True, [bQ[hh][cur], bA[hh][cur]], [bSB[hh]])
                yield
                for hh in HS:
                    act(lambda e, hh=hh, nx=nx: e.activation(out=Q_[hh][nx][:], in_=SB[hh][:, 128:256], func=AF.Identity), [bSB[hh]], [bQ[hh][nx]])
            yield
            for hh in HS:
                mm(SB[hh][:, 256:384], Tm_[hh][cur][:], A_[hh][nx][:], True, True, [bTm[hh][cur], bA[hh][nx]], [bSB[hh]])
            yield
            for hh in HS:
                dve(lambda e, hh=hh, nx=nx, cur=cur: e.tensor_tensor(out=Tt_[hh][nx][:], in0=SB[hh][:, 256:384], in1=Tt_[hh][cur][:], op=ALU.add),
                    [bSB[hh], bTt[hh][cur]], [bTt[hh][nx]])
            if s_ < 6:
                yield
                for hh in HS:
                    mm(SB[hh][:, 384:512], Tt_[hh][cur][:], Q_[hh][nx][:], True, True, [bTt[hh][cur], bQ[hh][nx]], [bSB[hh]])
                yield
                for hh in HS:
                    dve(lambda e, hh=hh, nx=nx, cur=cur: e.tensor_tensor(out=Tm_[hh][nx][:], in0=SB[hh][:, 384:512], in1=Tm_[hh][cur][:], op=ALU.add),
                        [bSB[hh], bTm[hh][cur]], [bTm[hh][nx]])
            cur = nx
        Vh = [T("v")[:, sls[hh]] for hh in HS]
        yield
        for hh in HS:
            mm(SB[hh][:, 0:64], LkT[hh][:], Vh[hh], True, False, [bLkT[hh], B("v")], [bSB[hh]])
            mm(SB[hh][:, 0:64], kaT[hh], H[:, hh, :], False, True, [bfeat[hh], bH[hh]], [bSB[hh]])
        yield
        for hh in HS:
            act(lambda e, hh=hh: e.activation(out=W1s[hh][:], in_=SB[hh][:, 0:64], func=AF.Identity), [bSB[hh]], [bW1s[hh]])
        yield
        for hh in HS:
            mm(SB[hh][:, 64:128], Tt_[hh][cur][:], W1s[hh][:], True, True, [bTt[hh][cur], bW1s[hh]], [bSB[hh]])
        yield
        for hh in HS:
            act(lambda e, hh=hh: e.activation(out=Us[hh][:], in_=SB[hh][:, 64:128], func=AF.Identity), [bSB[hh]], [bUs[hh]])
        yield
        for hh in HS:
            mm(SB[hh][:, 128:192], rT[hh], H[:, hh, :], True, False, [bfeat[hh], bH[hh]], [bSB[hh]])
            mm(SB[hh][:, 128:192], MkT[hh][:], Vh[hh], False, False, [bMkT[hh], B("v")], [bSB[hh]])
            mm(SB[hh][:, 128:192], MbT[hh][:], Us[hh][:], False, True, [bMbT[hh], bUs[hh]], [bSB[hh]])
        yield
        for hh in HS:
            act(lambda e, hh=hh: e.activation(out=T("y")[:, sls[hh]], in_=SB[hh][:, 128:192], func=AF.Identity), [bSB[hh]], [B("y")])
        yield
        for hh in HS:
            mm(SB[hh][0:64, 192:256], T("kg")[:, sls[hh]], Vh[hh], True, False, [B("kg"), B("v")], [bSB[hh]])
            mm(SB[hh][0:64, 192:256], T("kbg")[:, sls[hh]], Us[hh][:], False, True, [B("kbg"), bUs[hh]], [bSB[hh]])
        yield
        for hh in HS:
            dve(lambda e, hh=hh: e.scalar_tensor_tensor(out=H[:, hh, :], in0=H[:, hh, :], scalar=T("gcol")[:, hh:hh + 1], in1=SB[hh][0:64, 192:256],
                                                        op0=ALU.mult, op1=ALU.add), [bH[hh], B("gcol"), bSB[hh]], [bH[hh]])
        yield
        v3 = lambda nm: T(nm)[:].rearrange("p (h n) -> p h n", h=4)
        dve(lambda e: e.tensor_reduce(out=T("mean")[:], in_=v3("y"), axis=AX.X, op=ALU.add), [B("y")], [B("mean")])
        dve(lambda e: e.tensor_scalar(out=T("mean")[:], in0=T("mean")[:], scalar1=1.0 / 64, scalar2=None, op0=ALU.mult), [B("mean")], [B("mean")])
        for hh in range(4):
            sl = slice(hh * 64, (hh + 1) * 64)
            dve(lambda e, hh=hh, sl=sl: e.tensor_scalar(out=T("cen")[:, sl], in0=T("y")[:, sl], scalar1=T("mean")[:, hh:hh + 1], scalar2=None, op0=ALU.subtract),
                [B("y"), B("mean")], [B("cen")])
        dve(lambda e: e.tensor_tensor(out=T("t0")[:], in0=T("cen")[:], in1=T("cen")[:], op=ALU.mult), [B("cen")], [B("t0")])
        dve(lambda e: e.tensor_reduce(out=T("var")[:], in_=v3("t0"), axis=AX.X, op=ALU.add), [B("t0")], [B("var")])
        act(lambda e: e.activation(out=T("rstd")[:], in_=T("var")[:], func=AF.Sqrt, bias=64e-5, scale=1.0 / 64), [B("var")], [B("rstd")])
        dve(lambda e: e.reciprocal(out=T("rstd")[:], in_=T("rstd")[:]), [B("rstd")], [B("rstd")])
        for hh in range(4):
            sl = slice(hh * 64, (hh + 1) * 64)
            dve(lambda e, hh=hh, sl=sl: e.scalar_tensor_tensor(out=T("yn")[:, sl], in0=T("cen")[:, sl], scalar=T("rstd")[:, hh:hh + 1], in1=rows[:, 5, sl],
                                                               op0=ALU.mult, op1=ALU.mult), [B("cen"), B("rstd"), brows], [B("yn")])
        dve(lambda e: e.tensor_tensor(out=T("yn")[:], in0=T("yn")[:], in1=rows[:, 6, :], op=ALU.add), [B("yn"), brows], [B("yn")])
        for hh in range(4):
            sl = slice(hh * 64, (hh + 1) * 64)
            dve(lambda e, hh=hh, sl=sl: e.scalar_tensor_tensor(out=T("yn")[:, sl], in0=T("v")[:, sl], scalar=T("bc")[:, hh:hh + 1], in1=T("yn")[:, sl],
                                                               op0=ALU.mult, op1=ALU.add), [B("v"), B("bc"), B("yn")], [B("yn")])
        dve(lambda e: e.tensor_tensor(out=T("z")[:], in0=T("yn")[:], in1=T("gg")[:], op=ALU.mult), [B("yn"), B("gg")], [B("z")])
        bo = P.buf()
        P.dma("sp", lambda e, lo=lo: e.dma_start(out=o_d[lo:lo + 128, :], in_=T("z")[:]), reads=[B("z")], writes=[bo])
        OUTS.append(bo)
        yield
    OUTS = []
    def drain(g):
        for _ in g:
            pass
    drain(stageA(0))
    for c in range(NCH):
        gB = stageB(c)
        gA = stageA(c + 1) if c + 1 < NCH else None
        while gB is not None or gA is not None:
            if gB is not None:
                try:
                    next(gB)
                except StopIteration:
                    gB = None
            if gA is not None:
                try:
                    next(gA)
                except StopIteration:
                    gA = None
    print('rwkv sbuf remaining', nc.sbuf_bytes_remaining, {e: len(v) for e, v in P.streams.items()})
    P.final_wait("sp", OUTS)
    P.emit(); P.close()
    return nc


def rwkv_host_inputs(inp, hT_full, g):
    cs = slice(g * 256, (g + 1) * 256)
    def kp(w): return np.ascontiguousarray(w.reshape(8, 128, -1).transpose(1, 0, 2))
    mu = np.ascontiguousarray(inp["rwkv_mu"][0].reshape(6, 8, 128).transpose(2, 0, 1))
    wrkv = np.stack([kp(inp["rwkv_w_rkv"][0, n][:, cs]) for n in range(3)], axis=1)
    wl1 = kp(np.concatenate([inp["rwkv_wd1"][0], inp["rwkv_wa1"][0], inp["rwkv_wg1"][0]], axis=1))
    wl2 = np.zeros((128, 4, 256), np.float32)
    wl2[0:64, 0] = inp["rwkv_wd2"][0][:, cs]
    wl2[0:64, 1] = inp["rwkv_wa2"][0][:, cs]
    wl2[:, 2] = inp["rwkv_wg2"][0][0:128, cs]
    wl2[0:32, 3] = inp["rwkv_wg2"][0][128:160, cs]
    rws = np.stack([inp["rwkv_w0"][0][cs], inp["rwkv_a0"][0][cs], inp["rwkv_k_k"][0][cs], inp["rwkv_k_a"][0][cs],
                    inp["rwkv_r_k"][0].reshape(-1)[cs], inp["rwkv_gn_w"][0][cs], inp["rwkv_gn_b"][0][cs]], axis=0)
    rows = np.ascontiguousarray(np.broadcast_to(rws[None], (128, 7, 256))).astype(np.float32)
    i = np.arange(128)
    TriT = (i[:, None] <= i[None, :]).astype(np.float32)
    mST = (i[None, :] > i[:, None]).astype(np.float32)
    mL = (i[None, :] < i[:, None]).astype(np.float32)
    cm = np.stack([TriT, np.ones((128, 128), np.float32), mST, mL, np.eye(128, dtype=np.float32)], axis=1)
    return {"hT": hT_full, "mu": mu, "wrkv": np.ascontiguousarray(wrkv), "wl1": wl1, "wl2": wl2, "rows": rows, "cm": np.ascontiguousarray(cm)}


S = 8192
NQT = 16


def build_ret():
    nc = bass.Bass("TRN2", target_bir_lowering=False)
    P = Prog(nc)
    dt = nc.dram_tensor
    hT_d = dt("hT", [8, 128, S], BF16, kind="ExternalInput").ap()
    wqk_d = dt("wqk", [128, 8, 512], F32, kind="ExternalInput").ap()
    wv_d = dt("wv", [128, 8, 512], F32, kind="ExternalInput").ap()
    wg_d = dt("wg", [128, 8, 512], F32, kind="ExternalInput").ap()
    cs_d = dt("cs", [128, 2, S], F32, kind="ExternalInput").ap()
    dec_d = dt("dec", [128, 5, 512], F32, kind="ExternalInput").ap()
    gn_d = dt("gn", [128, 2, 512], F32, kind="ExternalInput").ap()
    gpow_d = dt("gpow", [128, 64], F32, kind="ExternalInput").ap()
    z_d = dt("z", [S, 512], BF16, kind="ExternalOutput").ap()
    DBG = None
    if DBG:
        qk_d = dt("qk_dbg", [128, 4, S], BF16, kind="ExternalOutput").ap()
        v_d = dt("v_dbg", [128, 64, 512], BF16, kind="ExternalOutput").ap()

    wqk = P.sbuf("wqk_sb", [128, 8, 512], BF16); bwqk = P.buf()
    wv = P.sbuf("wv_sb", [128, 8, 512], BF16); bwv = P.buf()
    wg = P.sbuf("wg_sb", [128, 8, 512], BF16); bwg = P.buf()
    dec = P.sbuf("dec_sb", [128, 5, 512], F32); bdec = P.buf()
    gn = P.sbuf("gn_sb", [128, 2, 512], F32); bgn = P.buf()
    gpow = P.sbuf("gpow_sb", [128, 64], F32); bgpow = P.buf()
    qT = P.sbuf("qT", [128, 2, S], BF16); bqT = [P.buf() for _ in range(NQT)]
    kT = P.sbuf("kT", [128, 2, S], BF16); bkT = [P.buf() for _ in range(NQT)]
    v = P.sbuf("v", [128, 64, 512], BF16); bv = [P.buf() for _ in range(64)]
    ht = P.sbuf("ht", [128, 8, 512], BF16); bht = [P.buf() for _ in range(8)]
    cst = P.sbuf("cst", [128, 2, 512], F32); bcst = P.buf()
    x1 = P.sbuf("x1", [128, 512], F32); bx1 = P.buf()
    ta = P.sbuf("ta", [128, 512], F32); bta = P.buf()
    tb = P.sbuf("tb", [128, 512], F32); btb = P.buf()
    pt = [P.sbuf(f"pt{i}", [128, 512], BF16) for i in range(3)]; bpt = [P.buf() for _ in range(3)]
    cen = P.sbuf("cen", [128, 512], F32); bcen = P.buf()
    sgt = P.sbuf("sgt", [128, 512], F32); bsgt = P.buf()
    zt = [P.sbuf(f"zt{i}", [128, 512], BF16) for i in range(2)]; bzt = [P.buf() for _ in range(2)]
    st = P.sbuf("st", [128, 4], F32); bst = P.buf()
    pl = [P.psum(f"pl{i}", [128, 512]) for i in range(2)]; bpl = [P.buf(excl=True) for _ in range(2)]
    ps = [P.psum(f"ps{i}", [128, 512]) for i in range(2)]; bps = [P.buf(excl=True) for _ in range(2)]
    po = [P.psum(f"po{i}", [128, 512]) for i in range(4)]; bpo = [P.buf(excl=True) for _ in range(4)]
    cnt = {"pl": 0, "ps": 0, "pt": 0, "z": 0}

    P.dma("pool", lambda e: e.dma_start(out=wqk[:], in_=wqk_d), writes=[bwqk])
    P.dma("pool", lambda e: e.dma_start(out=wv[:], in_=wv_d), writes=[bwv])
    P.dma("pool", lambda e: e.dma_start(out=wg[:], in_=wg_d), writes=[bwg])
    P.dma("sp", lambda e: e.dma_start(out=dec[:], in_=dec_d), writes=[bdec])
    P.dma("sp", lambda e: e.dma_start(out=gn[:], in_=gn_d), writes=[bgn])
    P.dma("sp", lambda e: e.dma_start(out=gpow[:], in_=gpow_d), writes=[bgpow])

    def load_ht(t):
        for kc in range(8):
            P.dma("sp", lambda e, kc=kc, t=t: e.dma_start(out=ht[:, kc, :], in_=hT_d[kc, :, t * 512:(t + 1) * 512]), writes=[bht[kc]])

    for t in range(NQT):
        tsl = slice(t * 512, (t + 1) * 512)
        load_ht(t)
        P.dma("sp", lambda e, tsl=tsl: e.dma_start(out=cst[:], in_=cs_d[:, :, tsl]), writes=[bcst])
        for which, (dst, bdst) in enumerate([(qT, bqT), (kT, bkT)]):
            qa = cnt["pl"] % 2; cnt["pl"] += 1
            for kc in range(8):
                P.op("pe", lambda e, kc=kc, qa=qa, which=which: e.matmul(pl[qa][:], lhsT=wqk[:, kc, which * 256:which * 256 + 128], rhs=ht[:, kc, :],
                                                                         start=(kc == 0), stop=(kc == 7)), reads=[bwqk, bht[kc]], writes=[bpl[qa]], nosync_same=True)
            P.op("act", lambda e, qa=qa: e.activation(out=x1[:], in_=pl[qa][:], func=AF.Identity), reads=[bpl[qa]], writes=[bx1])
            qb = cnt["pl"] % 2; cnt["pl"] += 1
            for kc in range(8):
                P.op("pe", lambda e, kc=kc, qb=qb, which=which: e.matmul(pl[qb][:], lhsT=wqk[:, kc, which * 256 + 128:which * 256 + 256], rhs=ht[:, kc, :],
                                                                         start=(kc == 0), stop=(kc == 7)), reads=[bwqk, bht[kc]], writes=[bpl[qb]], nosync_same=True)
            P.op("dve", lambda e: e.tensor_tensor(out=ta[:], in0=x1[:], in1=cst[:, 0, :], op=ALU.mult), reads=[bx1, bcst], writes=[bta])
            P.op("dve", lambda e, qb=qb: e.tensor_tensor(out=tb[:], in0=pl[qb][:], in1=cst[:, 1, :], op=ALU.mult), reads=[bpl[qb], bcst], writes=[btb])
            P.op("dve", lambda e, dst=dst, tsl=tsl: e.tensor_tensor(out=dst[:, 0, tsl], in0=ta[:], in1=tb[:], op=ALU.subtract), reads=[bta, btb], writes=[bdst[t]])
            P.op("dve", lambda e: e.tensor_tensor(out=ta[:], in0=x1[:], in1=cst[:, 1, :], op=ALU.mult), reads=[bx1, bcst], writes=[bta])
            P.op("dve", lambda e, qb=qb: e.tensor_tensor(out=tb[:], in0=pl[qb][:], in1=cst[:, 0, :], op=ALU.mult), reads=[bpl[qb], bcst], writes=[btb])
            P.op("dve", lambda e, dst=dst, tsl=tsl: e.tensor_tensor(out=dst[:, 1, tsl], in0=ta[:], in1=tb[:], op=ALU.add), reads=[bta, btb], writes=[bdst[t]])
        for tb4 in range(4):
            blk = t * 4 + tb4
            qa = cnt["pl"] % 2; cnt["pl"] += 1
            for kc in range(8):
                P.op("pe", lambda e, kc=kc, qa=qa, tb4=tb4: e.matmul(pl[qa][:], lhsT=ht[:, kc, tb4 * 128:(tb4 + 1) * 128], rhs=wv[:, kc, :],
                                                                     start=(kc == 0), stop=(kc == 7)), reads=[bwv, bht[kc]], writes=[bpl[qa]], nosync_same=True)
            P.op("act", lambda e, qa=qa, blk=blk: e.activation(out=v[:, blk, :], in_=pl[qa][:], func=AF.Identity), reads=[bpl[qa]], writes=[bv[blk]])

    outs = []
    if DBG:
        for (src, bsrc, o0) in [(qT, bqT, 0), (kT, bkT, 2)]:
            for c in range(2):
                b = P.buf(); outs.append(b)
                P.dma('sp', lambda e, src=src, c=c, o0=o0: e.dma_start(out=qk_d[:, o0 + c, :], in_=src[:, c, :]), reads=bsrc, writes=[b])
        b = P.buf(); outs.append(b)
        P.dma('sp', lambda e: e.dma_start(out=v_d, in_=v[:]), reads=bv, writes=[b])
    for j in range(NQT):
        tsl = slice(j * 512, (j + 1) * 512)
        load_ht(j)
        nkb = 4 * j + 4

        def emit_S(kb):
            sq = cnt["ps"] % 2; cnt["ps"] += 1
            for c in range(2):
                P.op("pe", lambda e, c=c, sq=sq, kb=kb, tsl=tsl: e.matmul(ps[sq][:], lhsT=kT[:, c, kb * 128:(kb + 1) * 128], rhs=qT[:, c, tsl],
                                                                 start=(c == 0), stop=(c == 1)), reads=[bkT[kb // 4], bqT[j]], writes=[bps[sq]], nosync_same=True)
            return sq
        cur = emit_S(0)
        for kb in range(nkb):
            nxt = emit_S(kb + 1) if kb + 1 < nkb else None
            d = kb - 4 * j
            pq = cnt["pt"] % 3; cnt["pt"] += 1
            if d >= 0:
                P.op("dve", lambda e, cur=cur, pq=pq, d=d: e.tensor_tensor(out=pt[pq][:], in0=ps[cur][:], in1=dec[:, 1 + d, :], op=ALU.mult),
                     reads=[bps[cur], bdec], writes=[bpt[pq]])
            else:
                i = (512 * j - 128 * kb) // 128
                P.op("dve", lambda e, cur=cur, pq=pq, i=i: e.scalar_tensor_tensor(out=pt[pq][:], in0=ps[cur][:], scalar=gpow[:, i:i + 1], in1=dec[:, 0, :],
                                                                                  op0=ALU.mult, op1=ALU.mult), reads=[bps[cur], bdec, bgpow], writes=[bpt[pq]])
            for qs in range(4):
                if d > qs:
                    continue
                last = 4 * j + qs
                P.op("pe", lambda e, pq=pq, qs=qs, kb=kb, last=last: e.matmul(po[qs][:], lhsT=pt[pq][:, qs * 128:(qs + 1) * 128], rhs=v[:, kb, :],
                                                                              start=(kb == 0), stop=(kb == last)), reads=[bpt[pq], bv[kb]], writes=[bpo[qs]], nosync_same=True)
            cur = nxt
        for qs in range(4):
            qa = cnt["pl"] % 2; cnt["pl"] += 1
            for kc in range(8):
                P.op("pe", lambda e, kc=kc, qa=qa, qs=qs: e.matmul(pl[qa][:], lhsT=ht[:, kc, qs * 128:(qs + 1) * 128], rhs=wg[:, kc, :],
                                                                   start=(kc == 0), stop=(kc == 7)), reads=[bwg, bht[kc]], writes=[bpl[qa]], nosync_same=True)
            P.op("act", lambda e, qa=qa: e.activation(out=sgt[:], in_=pl[qa][:], func=AF.Silu), reads=[bpl[qa]], writes=[bsgt])
            P.op("dve", lambda e, qs=qs: e.tensor_reduce(out=st[:, 0:1], in_=po[qs][:], axis=AX.X, op=ALU.add), reads=[bpo[qs]], writes=[bst])
            P.op("dve", lambda e: e.tensor_scalar(out=st[:, 1:2], in0=st[:, 0:1], scalar1=-1.0 / 512, scalar2=None, op0=ALU.mult), reads=[bst], writes=[bst])
            P.op("act", lambda e, qs=qs: e.activation(out=cen[:], in_=po[qs][:], func=AF.Identity, bias=st[:, 1:2]), reads=[bpo[qs], bst], writes=[bcen])
            P.op("dve", lambda e: e.tensor_tensor(out=ta[:], in0=cen[:], in1=cen[:], op=ALU.mult), reads=[bcen], writes=[bta])
            P.op("dve", lambda e: e.tensor_reduce(out=st[:, 2:3], in_=ta[:], axis=AX.X, op=ALU.add), reads=[bta], writes=[bst])
            P.op("act", lambda e: e.activation(out=st[:, 3:4], in_=st[:, 2:3], func=AF.Sqrt, bias=1e-5, scale=1.0 / 512), reads=[bst], writes=[bst])
            P.op("dve", lambda e: e.reciprocal(out=st[:, 3:4], in_=st[:, 3:4]), reads=[bst], writes=[bst])
            P.op("dve", lambda e: e.scalar_tensor_tensor(out=tb[:], in0=cen[:], scalar=st[:, 3:4], in1=gn[:, 0, :], op0=ALU.mult, op1=ALU.mult),
                 reads=[bcen, bst, bgn], writes=[btb])
            P.op("dve", lambda e: e.tensor_tensor(out=tb[:], in0=tb[:], in1=gn[:, 1, :], op=ALU.add), reads=[btb, bgn], writes=[btb])
            zi = cnt["z"] % 2; cnt["z"] += 1
            P.op("dve", lambda e, zi=zi: e.tensor_tensor(out=zt[zi][:], in0=tb[:], in1=sgt[:], op=ALU.mult), reads=[btb, bsgt], writes=[bzt[zi]])
            b = P.buf(); outs.append(b)
            r0 = j * 512 + qs * 128
            P.dma("sp", lambda e, zi=zi, r0=r0: e.dma_start(out=z_d[r0:r0 + 128, :], in_=zt[zi][:]), reads=[bzt[zi]], writes=[b])
    print('ret sbuf remaining', nc.sbuf_bytes_remaining, {e: len(v_) for e, v_ in P.streams.items()})
    P.final_wait("sp", outs)
    P.emit(); P.close()
    return nc


def ret_host_inputs(inp, hT_full, h):
    D = 1024
    w = inp["ret_w_in"][0]
    def kp(a): return np.ascontiguousarray(a.reshape(8, 128, -1).transpose(1, 0, 2))
    wq = w[:, h * 256:(h + 1) * 256]; wk = w[:, D + h * 256:D + (h + 1) * 256]
    wv = w[:, 2 * D + h * 512:2 * D + (h + 1) * 512]; wg = w[:, 4 * D + h * 512:4 * D + (h + 1) * 512]
    inv = 10000.0 ** (-np.arange(0, 256, 2, dtype=np.float32) / 256)
    ang = np.arange(S, dtype=np.float32)[:, None] * inv[None, :]
    cs = np.ascontiguousarray(np.stack([np.cos(ang).T, np.sin(ang).T], axis=1)).astype(np.float32)
    lg = np.log1p(-np.exp2(-5.0 - h))
    p = np.arange(128, dtype=np.float64)[:, None]; f = np.arange(512, dtype=np.float64)[None, :]
    dec = np.zeros((128, 5, 512), np.float64)
    dec[:, 0] = np.exp(lg * (f - p))
    for d in range(4):
        e_ = f - p - 128 * d
        dec[:, 1 + d] = np.where(e_ >= 0, np.exp(lg * np.maximum(e_, 0)), 0.0)
    dec *= 256 ** -0.5
    gpow = np.broadcast_to(np.exp(lg * 128.0 * np.arange(64, dtype=np.float64))[None, :], (128, 64))
    gn = np.stack([np.broadcast_to(inp["ret_gn_w"][0][h * 512:(h + 1) * 512][None], (128, 512)),
                   np.broadcast_to(inp["ret_gn_b"][0][h * 512:(h + 1) * 512][None], (128, 512))], axis=1)
    return {"hT": hT_full, "wqk": kp(np.concatenate([wq, wk], axis=1)), "wv": kp(wv), "wg": kp(wg), "cs": cs,
            "dec": dec.astype(np.float32), "gn": np.ascontiguousarray(gn).astype(np.float32),
            "gpow": np.ascontiguousarray(gpow).astype(np.float32)}


S = 8192


def build_moba():
    nc = bass.Bass("TRN2", target_bir_lowering=False)
    P = Prog(nc)
    dt = nc.dram_tensor
    hT_d = dt("hT", [8, 128, S], BF16, kind="ExternalInput").ap()
    w_d = dt("wqkv", [128, 8, 768], F32, kind="ExternalInput").ap()
    tab_d = dt("tab", [32, 4], F32, kind="ExternalInput").ap()
    oh_d = dt("oh", [32, 1152 + 128], F32, kind="ExternalInput").ap()
    cm_d = dt("cm", [128, 2, 128], F32, kind="ExternalInput").ap()
    kblk_d = dt("kblk", [96, S], F32, kind="ExternalInput").ap()
    sel_d = dt("selc", [128, 2, 64, 32], F32, kind="ExternalInput").ap()
    mask_d = dt("mask", [128, 4, 512], F32, kind="ExternalInput").ap()
    tv_d = dt("tv_scratch", [4, 1152], F32, kind="Internal").ap()
    o_d = dt("o", [S, 256], BF16, kind="ExternalOutput").ap()

    w = P.sbuf("w_sb", [128, 8, 768], BF16); bw = P.buf()
    tab = P.sbuf("tab_sb", [32, 4], F32); btab = P.buf()
    oh = P.sbuf("oh_sb", [32, 1280], F32); boh = P.buf()
    cm = P.sbuf("cm_sb", [128, 2, 128], F32); bcm = P.buf()
    selc = P.sbuf("selc_sb", [128, 2, 64, 32], F32); bselc = P.buf()
    masks = P.sbuf("masks_sb", [128, 4, 512], BF16); bmask = P.buf()
    tv = P.sbuf("tv_sb", [4, 1152], F32); btv = P.buf()
    b31 = P.sbuf("b31", [128, 4], F32); bb31 = P.buf()
    hk = P.sbuf("hk", [128, 512], F32); bhk = P.buf()
    bt = P.sbuf("bt", [128, 5, 512], F32); bbt = P.buf()
    kT = P.sbuf("kT", [96, S], BF16); bkT = P.buf(); bkrow = P.buf()
    qT = P.sbuf("qT", [96, S], BF16); bqT = [P.buf() for _ in range(NQT)]
    va = P.sbuf("va", [128, 64, 65], BF16); bva = P.buf()
    kmT = P.sbuf("kmT", [64, 32], F32); bkm = P.buf()
    ht = [P.sbuf(f"ht{i}", [128, 8, 512], BF16) for i in range(2)]; bht = [[P.buf() for _ in range(8)] for _ in range(2)]
    qf = P.sbuf("qf", [64, 512], F32); bqf = P.buf()
    gm = [P.sbuf(f"gm{i}", [128, 4, 32], F32) for i in range(2)]; bgm = [P.buf() for _ in range(2)]
    top8 = [P.sbuf(f"top8{i}", [128, 4, 8], F32) for i in range(2)]; btop = [P.buf() for _ in range(2)]
    pen = [P.sbuf(f"pen{i}", [128, 4, 96], F32) for i in range(2)]; bpen = [P.buf() for _ in range(2)]
    stmp = [P.sbuf(f"stmp{i}", [128, 512], F32) for i in range(2)]; bstmp = [P.buf() for _ in range(2)]
    pl = [P.psum(f"pl{i}", [128, 512]) for i in range(2)]; bpl = [P.buf(excl=True) for _ in range(2)]
    pg = P.psum("pg", [128, 512]); bpg = P.buf(excl=True)
    cnt = {"pl": 0, "ht": 0, "st": 0}
    anti, ident = cm[:, 0, :], cm[:, 1, :]

    P.dma("pool", lambda e: e.dma_start(out=w[:], in_=w_d), writes=[bw])
    P.dma("pool", lambda e: e.dma_start(out=masks[:], in_=mask_d), writes=[bmask])
    P.dma("pool", lambda e: e.dma_start(out=kT[64:96, :], in_=kblk_d[64:96, :]), writes=[bkrow])
    P.dma("sp", lambda e: e.dma_start(out=tab[:], in_=tab_d), writes=[btab])
    P.dma("sp", lambda e: e.dma_start(out=oh[:], in_=oh_d), writes=[boh])
    P.dma("sp", lambda e: e.dma_start(out=cm[:], in_=cm_d), writes=[bcm])
    P.dma("sp", lambda e: e.dma_start(out=selc[:], in_=sel_d), writes=[bselc])
    for i_ in range(2):
        P.op("dve", lambda e, i_=i_: e.memset(pen[i_][:], 0.0), writes=[bpen[i_]])
    for c3 in range(3):
        n0 = c3 * 384
        P.op("pe", lambda e, n0=n0: e.matmul(pg[0:4, 0:384], lhsT=tab[:], rhs=oh[:, n0:n0 + 384], start=True, stop=True),
             reads=[btab, boh], writes=[bpg], nosync_same=True)
        P.op("act", lambda e, n0=n0: e.activation(out=tv[:, n0:n0 + 384], in_=pg[0:4, 0:384], func=AF.Identity), reads=[bpg], writes=[btv])
    P.op("pe", lambda e: e.matmul(pg[:, 0:4], lhsT=oh[:, 1152:1280], rhs=tab[:], start=True, stop=True), reads=[btab, boh], writes=[bpg], nosync_same=True)
    P.op("act", lambda e: e.activation(out=b31[:], in_=pg[:, 0:4], func=AF.Identity), reads=[bpg], writes=[bb31])
    btvd = P.buf()
    P.dma("sp", lambda e: e.dma_start(out=tv_d, in_=tv[:]), reads=[btv], writes=[btvd])

    R = attn_resources(P)
    for hh in range(4):
        for x in range(5):
            src = bass.AP(tv_d.tensor, hh * 1152 + 128 * x, [[1, 128], [1, 512]])
            P.dma("sp", lambda e, src=src: e.dma_start(out=hk[:], in_=src), reads=[btvd], writes=[bhk])
            P.op("pe", lambda e: e.matmul(pg[:], lhsT=anti, rhs=hk[:], start=True, stop=True), reads=[bcm, bhk], writes=[bpg], nosync_same=True)
            P.op("act", lambda e, x=x: e.activation(out=bt[:, x, :], in_=pg[:], func=AF.Identity), reads=[bpg], writes=[bbt])
        P.op("pool", lambda e: e.memset(va[:, :, 64:65], 1.0), writes=[bva])
        for t in range(NQT):
            tsl = slice(t * 512, (t + 1) * 512)
            k = cnt["ht"] % 2; cnt["ht"] += 1
            for kc in range(8):
                P.dma("sp", lambda e, kc=kc, k=k, tsl=tsl: e.dma_start(out=ht[k][:, kc, :], in_=hT_d[kc, :, tsl]), writes=[bht[k][kc]])
            q_ = cnt["pl"] % 2; cnt["pl"] += 1
            for kc in range(8):
                P.op("pe", lambda e, kc=kc, q_=q_, k=k, hh=hh: e.matmul(pl[q_][0:64, :], lhsT=w[:, kc, 256 + hh * 64:256 + (hh + 1) * 64], rhs=ht[k][:, kc, :],
                                                                        start=(kc == 0), stop=(kc == 7)), reads=[bw, bht[k][kc]], writes=[bpl[q_]], nosync_same=True)
            P.op("act", lambda e, q_=q_, tsl=tsl: e.activation(out=kT[0:64, tsl], in_=pl[q_][0:64, :], func=AF.Identity), reads=[bpl[q_]], writes=[bkT])
            P.op("dve", lambda e, q_=q_, t=t: e.tensor_reduce(out=kmT[:, 2 * t:2 * t + 2], in_=pl[q_][0:64, :].rearrange("p (a b) -> p a b", a=2),
                                                             axis=AX.X, op=ALU.add), reads=[bpl[q_]], writes=[bkm])
            P.op("dve", lambda e, t=t: e.tensor_scalar(out=kmT[:, 2 * t:2 * t + 2], in0=kmT[:, 2 * t:2 * t + 2], scalar1=1.0 / 256, scalar2=None, op0=ALU.mult),
                 reads=[bkm], writes=[bkm])
            for tb4 in range(4):
                blk = t * 4 + tb4
                q_ = cnt["pl"] % 2; cnt["pl"] += 1
                for kc in range(8):
                    P.op("pe", lambda e, kc=kc, q_=q_, k=k, hh=hh, tb4=tb4: e.matmul(pl[q_][:, 0:64], lhsT=ht[k][:, kc, tb4 * 128:(tb4 + 1) * 128],
                                                                                     rhs=w[:, kc, 512 + hh * 64:512 + (hh + 1) * 64], start=(kc == 0), stop=(kc == 7)),
                         reads=[bw, bht[k][kc]], writes=[bpl[q_]], nosync_same=True)
                P.op("act", lambda e, q_=q_, blk=blk: e.activation(out=va[:, blk, 0:64], in_=pl[q_][:, 0:64], func=AF.Identity), reads=[bpl[q_]], writes=[bva])
            q_ = cnt["pl"] % 2; cnt["pl"] += 1
            for kc in range(8):
                P.op("pe", lambda e, kc=kc, q_=q_, k=k, hh=hh: e.matmul(pl[q_][0:64, :], lhsT=w[:, kc, hh * 64:(hh + 1) * 64], rhs=ht[k][:, kc, :],
                                                                        start=(kc == 0), stop=(kc == 7)), reads=[bw, bht[k][kc]], writes=[bpl[q_]], nosync_same=True)
            P.op("act", lambda e, q_=q_: e.activation(out=qf[:], in_=pl[q_][0:64, :], func=AF.Identity, scale=0.125), reads=[bpl[q_]], writes=[bqf])
            P.op("dve", lambda e, tsl=tsl: e.tensor_copy(out=qT[0:64, tsl], in_=qf[:]), reads=[bqf], writes=[bqT[t]])
            gi = t % 2
            q_ = cnt["pl"] % 2; cnt["pl"] += 1
            for qb in range(4):
                P.op("pe", lambda e, qb=qb, q_=q_: e.matmul(pl[q_][:, qb * 32:(qb + 1) * 32], lhsT=qf[:, qb * 128:(qb + 1) * 128], rhs=kmT[:], start=True, stop=True),
                     reads=[bqf, bkm], writes=[bpl[q_]], nosync_same=True)
            P.op("dve", lambda e, q_=q_, gi=gi, t=t: e.tensor_tensor(out=gm[gi][:], in0=pl[q_][:, 0:128].rearrange("p (a b) -> p a b", a=4),
                                                                   in1=selc[:, 0, 4 * t:4 * t + 4, :], op=ALU.add), reads=[bpl[q_], bselc], writes=[bgm[gi]])
            for qb in range(4):
                P.op("dve", lambda e, qb=qb, gi=gi: e.max(out=top8[gi][:, qb, :], in_=gm[gi][:, qb, :]), reads=[bgm[gi]], writes=[btop[gi]])
            for qb in range(4):
                P.op("dve", lambda e, qb=qb, gi=gi: e.tensor_scalar(out=gm[gi][:, qb, :], in0=gm[gi][:, qb, :], scalar1=top8[gi][:, qb, 2:3], scalar2=-1.0,
                                                                   op0=ALU.is_ge, op1=ALU.add), reads=[bgm[gi], btop[gi]], writes=[bgm[gi]])
            P.op("dve", lambda e, gi=gi, t=t: e.scalar_tensor_tensor(out=pen[gi][:, :, 64:96], in0=gm[gi][:], scalar=30000.0, in1=selc[:, 1, 4 * t:4 * t + 4, :],
                                                                    op0=ALU.mult, op1=ALU.mult), reads=[bgm[gi], bselc], writes=[bpen[gi]])
            for qb in range(4):
                P.op("pe", lambda e, qb=qb, gi=gi: e.transpose(pg[0:96, qb * 128:(qb + 1) * 128], pen[gi][:, qb, :], ident), reads=[bpen[gi], bcm], writes=[bpg], nosync_same=True)
            P.op("act", lambda e, tsl=tsl: e.activation(out=qT[64:96, tsl], in_=pg[64:96, :], func=AF.Identity), reads=[bpg], writes=[bqT[t]])

        def q_tile(j):
            return qT[0:96, j * 512:(j + 1) * 512], bqT[j]

        def exp_fn(j, kb, ps, bps, pt, bpt, hh=hh):
            d = kb - 4 * j
            if d >= -1:
                x = 3 - d
                si = cnt["st"] % 2; cnt["st"] += 1
                P.op("dve", lambda e, ps=ps, si=si, x=x: e.tensor_tensor(out=stmp[si][:], in0=ps[:], in1=bt[:, x, :], op=ALU.add), reads=[bps, bbt], writes=[bstmp[si]])
                P.op("act", lambda e, pt=pt, si=si: e.activation(out=pt[:], in_=stmp[si][:], func=AF.Exp), reads=[bstmp[si]], writes=[bpt])
            else:
                P.op("act", lambda e, ps=ps, pt=pt, hh=hh: e.activation(out=pt[:], in_=ps[:], func=AF.Exp, bias=b31[:, hh:hh + 1]), reads=[bps, bb31], writes=[bpt])

        class _KB:
            pass
        attn_core(P, cnt, R, kT, bkT, 96, va, bva, q_tile, 1.0, o_d, hh, hh * 64, masks, bmask, exp_fn=exp_fn)
    print('moba sbuf remaining', nc.sbuf_bytes_remaining, {e: len(v_) for e, v_ in P.streams.items()})
    P.final_wait("sp", R["outs"])
    P.emit(); P.close()
    return nc


def _t5_bucket_np(dist):
    n = np.maximum(dist, 0)
    nf = np.maximum(n, 16).astype(np.float32)
    large = 16 + (np.log(nf / np.float32(16)) / np.float32(np.log(128 / 16)) * np.float32(16)).astype(np.int32)
    large = np.minimum(large, 31)
    return np.where(n < 16, n, large)


def moba_host_inputs(inp, hT_full, g):
    w = inp["moba_w_in"][0]
    cols = []
    for part in range(3):
        cols.append(w[:, part * 1024 + g * 256: part * 1024 + (g + 1) * 256])
    wl = np.concatenate(cols, axis=1)
    wl = np.ascontiguousarray(wl.reshape(8, 128, 768).transpose(1, 0, 2))
    tab = np.ascontiguousarray(inp["rel_table"][:, g * 4:(g + 1) * 4]).astype(np.float32)
    bk = _t5_bucket_np(np.arange(1152) - 511)
    oh = np.zeros((32, 1280), np.float32)
    oh[bk, np.arange(1152)] = 1.0
    oh[31, 1152:] = 1.0
    i = np.arange(128)
    cm = np.stack([(i[:, None] + i[None, :] == 127).astype(np.float32), np.eye(128, dtype=np.float32)], axis=1)
    kblk = np.zeros((96, S), np.float32)
    kblk[64 + np.arange(S) // 256, np.arange(S)] = 1.0
    n = np.arange(32)
    own = (np.arange(64) // 2)[:, None]
    negm = np.where(n[None, :] < own, 0.0, -1e30).astype(np.float32)
    valid = (n[None, :] < own).astype(np.float32)
    selc = np.ascontiguousarray(np.broadcast_to(np.stack([negm, valid], axis=0)[None], (128, 2, 64, 32))).astype(np.float32)
    return {"hT": hT_full, "wqkv": wl, "tab": tab, "oh": oh, "cm": np.ascontiguousarray(cm), "kblk": kblk, "selc": selc,
            "mask": causal_masks_np()}


def lay_w_in(w):
    gate = w[:, :2816].reshape(8, 128, 11, 256)
    up = w[:, 2816:].reshape(8, 128, 11, 256)
    t = np.concatenate([gate, up], axis=3)
    return np.ascontiguousarray(t.transpose(2, 1, 0, 3))
def lay_w_out(w):
    t = w.reshape(22, 128, 4, 256)
    return np.ascontiguousarray(t.transpose(2, 1, 0, 3))
def lay_xT(xs):
    return np.ascontiguousarray(xs.T.reshape(8, 128, xs.shape[0]))
def lay_kp(w):
    K, N = w.shape
    return np.ascontiguousarray(w.reshape(K // 128, 128, N).transpose(1, 0, 2))
def mods_inputs(c, ada_w, ada_b):
    W = np.concatenate([ada_w[i] for i in range(4)], axis=1)
    bflat = ada_b.reshape(-1)
    cT = np.ascontiguousarray(c.T.reshape(8, 128, 2).transpose(1, 0, 2))
    maps = []
    for k in range(8):
        sl = W[:, k * 4608:(k + 1) * 4608]
        maps.append({"cT": cT, "aw": lay_kp(sl), "ab": np.ascontiguousarray(bflat[k * 4608:(k + 1) * 4608].reshape(36, 128).T)})
    return maps
def mods_assemble(results):
    allm = np.concatenate([r["modsT"] for r in results], axis=1)
    return [[np.ascontiguousarray(allm[:, l * 72:(l + 1) * 72, b]) for b in range(2)] for l in range(4)]


import ml_dtypes

_CORES = list(range(8))


def _gather_hT(results):
    return [np.ascontiguousarray(np.concatenate([results[b * 4 + q]["hT_out"] for q in range(4)], axis=2)) for b in range(2)]


def _oT_for_tokens(o_tok, b, q):
    sl = o_tok[b][q * 2048:(q + 1) * 2048]
    return np.ascontiguousarray(sl.T.reshape(-1, 128, 2048))


def kernel(**inp):
    inp = {k: np.asarray(v) for k, v in inp.items()}
    res = run_bass_kernel_spmd(build_mods(), mods_inputs(inp["c"], inp["ada_w"], inp["ada_b"]), core_ids=_CORES)
    modsT = mods_assemble(res.results)
    gT = [np.ascontiguousarray(inp["norm_g"][l].reshape(3, 8, 128).transpose(2, 0, 1)) for l in range(4)]
    x = inp["x"]
    maps = []
    for cid in _CORES:
        b, q = cid // 4, cid % 4
        maps.append({"xT_in": lay_xT(x[b, q * 2048:(q + 1) * 2048]), "w_in1": lay_w_in(inp["ffn_w_in"][0, 0]),
                     "w_out1": lay_w_out(inp["ffn_w_out"][0, 0]), "modsB": modsT[0][b], "gB": gT[0]})
    res = run_bass_kernel_spmd(build_token(0, True, False), maps, core_ids=_CORES)
    xT = [r["xT_out"] for r in res.results]
    hT = _gather_hT(res.results)
    w_mix_out = [inp["mla_w_out"][0], inp["rwkv_w_out"][0], inp["moba_w_out"][0], inp["ret_w_out"][0]]
    for l in range(4):
        if l == 0:
            res = run_bass_kernel_spmd(build_mla(), [mla_host_inputs(inp, hT[c // 4], c // 4, c % 4) for c in _CORES], core_ids=_CORES)
            key = "o"
        elif l == 1:
            res = run_bass_kernel_spmd(build_rwkv(), [rwkv_host_inputs(inp, hT[c // 4], c % 4) for c in _CORES], core_ids=_CORES)
            key = "o"
        elif l == 2:
            res = run_bass_kernel_spmd(build_moba(), [moba_host_inputs(inp, hT[c // 4], c % 4) for c in _CORES], core_ids=_CORES)
            key = "o"
        else:
            res = run_bass_kernel_spmd(build_ret(), [ret_host_inputs(inp, hT[c // 4], c % 4) for c in _CORES], core_ids=_CORES)
            key = "z"
        o_tok = [np.concatenate([res.results[b * 4 + g][key] for g in range(4)], axis=1) for b in range(2)]
        F_in = o_tok[0].shape[1]
        last = (l == 3)
        maps = []
        for cid in _CORES:
            b, q = cid // 4, cid % 4
            m = {"xT_in": xT[cid], "oT": _oT_for_tokens(o_tok, b, q), "wmo": lay_kp(w_mix_out[l]),
                 "w_in2": lay_w_in(inp["ffn_w_in"][l, 1]), "w_out2": lay_w_out(inp["ffn_w_out"][l, 1]),
                 "modsA": modsT[l][b], "gA": gT[l]}
            if not last:
                m.update({"w_in1": lay_w_in(inp["ffn_w_in"][l + 1, 0]), "w_out1": lay_w_out(inp["ffn_w_out"][l + 1, 0]),
                          "modsB": modsT[l + 1][b], "gB": gT[l + 1]})
            else:
                m["gF"] = np.ascontiguousarray(inp["final_g"].reshape(8, 128).T)
            maps.append(m)
        res = run_bass_kernel_spmd(build_token(F_in, not last, last), maps, core_ids=_CORES)
        xT = [r["xT_out"] for r in res.results]
        if not last:
            hT = _gather_hT(res.results)
    out = np.empty((2, 8192, 1024), np.float32)
    for cid in _CORES:
        b, q = cid // 4, cid % 4
        out[b, q * 2048:(q + 1) * 2048] = xT[cid].reshape(1024, 2048).T
    return out
```

```python
import numpy as np
import concourse.bass as bass
import concourse.mybir as mybir
from concourse.bass_utils import run_bass_kernel_spmd
from contextlib import ExitStack

F32 = mybir.dt.float32
BF16 = mybir.dt.bfloat16
ALU = mybir.AluOpType
AF = mybir.ActivationFunctionType
AX = mybir.AxisListType

ENGS = ("pe", "act", "dve", "pool", "sp")
EPOCH = 16000
SAME_ENGINE_SYNC = True
NOSYNC_ENGS = ("act",)
NDMASEM = 24


class Buf:
    __slots__ = ("name", "w", "r", "excl")

    def __init__(self, name="", excl=False):
        self.name = name
        self.w = None
        self.r = []
        self.excl = excl


class Prog:
    def __init__(self, nc):
        self.nc = nc
        self.streams = {e: [] for e in ENGS}
        self.known = {e: {} for e in ENGS}
        self.snap = {e: [] for e in ENGS}
        self.dmacount = {e: 0 for e in ENGS}
        self.dma_snap = {}
        self.stack = ExitStack()
        self.nbuf = 0

    def sbuf(self, name, shape, dtype):
        t = self.stack.enter_context(self.nc.sbuf_tensor(name, list(shape), dtype))
        return t

    def psum(self, name, shape, dtype=F32):
        t = self.stack.enter_context(self.nc.psum_tensor(name, list(shape), dtype))
        return t

    def buf(self, name="", excl=False):
        self.nbuf += 1
        return Buf(name or f"b{self.nbuf}", excl)

    def _deps(self, eng, reads, writes, nosync_same=False):
        deps = {}
        def add(ev):
            if ev is None:
                return
            src, idx = ev
            if src == eng and (nosync_same or not SAME_ENGINE_SYNC or eng in NOSYNC_ENGS):
                return
            if self.known[eng].get(src, 0) >= idx:
                return
            if deps.get(src, 0) < idx:
                deps[src] = idx
        def add_x(ev):
            if ev is not None and ev[0] != eng:
                add(ev)
        for b in reads:
            if b.excl:
                add_x(b.w)
            else:
                add(b.w)
        for b in writes:
            if b.excl:
                add_x(b.w)
                continue
            add(b.w)
            for ev in b.r:
                add(ev)
        return deps

    def _absorb(self, eng, deps):
        k = self.known[eng]
        for src, idx in deps.items():
            if k.get(src, 0) < idx:
                k[src] = idx
            if isinstance(src, str):
                sn = self.snap[src][idx - 1]
            else:
                sn = self.dma_snap.get((src, idx))
            if sn:
                for s2, i2 in sn.items():
                    if k.get(s2, 0) < i2:
                        k[s2] = i2

    def op(self, eng, fn, reads=(), writes=(), nosync_same=False):
        deps = self._deps(eng, reads, writes, nosync_same)
        self._absorb(eng, deps)
        st = self.streams[eng]
        st.append([fn, list(deps.items()), "op"])
        idx = len(st)
        ev = (eng, idx)
        self.snap[eng].append(dict(self.known[eng]))
        for b in reads:
            if b.excl:
                b.w = ev
            else:
                b.r.append(ev)
        for b in writes:
            b.w = ev
            b.r = []
        return ev

    def dma(self, eng, fn, reads=(), writes=()):
        deps = self._deps(eng, reads, writes)
        n = self.dmacount[eng]
        self.dmacount[eng] = n + 1
        slot = n % NDMASEM
        val = 16 * (n // NDMASEM + 1)
        src = ("dma", eng, slot)
        if val > 16 and self.known[eng].get(src, 0) < val - 16 and deps.get(src, 0) < val - 16:
            deps[src] = val - 16
        self._absorb(eng, deps)
        st = self.streams[eng]
        st.append([fn, list(deps.items()), ("dma", slot)])
        self.snap[eng].append(dict(self.known[eng]))
        self.dma_snap[(src, val)] = dict(self.known[eng])
        ev = (src, val)
        for b in reads:
            b.r.append(ev)
        for b in writes:
            b.w = ev
            b.r = []
        return ev

    def barrier(self, bufs):
        for e in ENGS:
            deps = self._deps(e, (), bufs)
            if deps:
                self._absorb(e, deps)
                self.streams[e].append([None, list(deps.items()), "wait"])
                self.snap[e].append(dict(self.known[e]))

    def final_wait(self, eng, bufs):
        deps = self._deps(eng, bufs, ())
        self._absorb(eng, deps)
        self.streams[eng].append([None, list(deps.items()), "wait"])
        self.snap[eng].append(dict(self.known[eng]))

    def emit(self):
        nc = self.nc
        marked = {e: set() for e in ENGS}
        for e in ENGS:
            for fn, waits, kind in self.streams[e]:
                for src, idx in waits:
                    if isinstance(src, str):
                        marked[src].add(idx)
        rank = {}
        nsem = {}
        for e in ENGS:
            r = 0
            for i in sorted(marked[e]):
                r += 1
                rank[(e, i)] = r
            nsem[e] = (r + EPOCH - 1) // EPOCH
        sems = {}
        for e in ENGS:
            for k in range(nsem[e]):
                sems[(e, k)] = self.stack.enter_context(nc.semaphore(f"s_{e}_{k}"))
        dsems = {}
        for e in ENGS:
            if self.dmacount[e]:
                for s in range(min(NDMASEM, self.dmacount[e])):
                    dsems[(e, s)] = self.stack.enter_context(nc.semaphore(f"d_{e}_{s}"))
        block = self.stack.enter_context(nc.Block())
        streams = self.streams

        def run(e, engine):
            for i, (fn, waits, kind) in enumerate(streams[e]):
                for src, idx in waits:
                    if isinstance(src, str):
                        r = rank[(src, idx)] - 1
                        engine.wait_ge(sems[(src, r // EPOCH)], r % EPOCH + 1)
                    else:
                        engine.wait_ge(dsems[(src[1], src[2])], idx)
                if fn is None:
                    continue
                ins = fn(engine)
                if kind == "op":
                    if (i + 1) in marked[e]:
                        r = rank[(e, i + 1)] - 1
                        ins.then_inc(sems[(e, r // EPOCH)], 1)
                else:
                    ins.then_inc(dsems[(e, kind[1])], 16)

        if streams["sp"]:
            @block.sync
            def _(eng):
                run("sp", eng)
        if streams["pe"]:
            @block.tensor
            def _(eng):
                run("pe", eng)
        if streams["act"]:
            @block.scalar
            def _(eng):
                run("act", eng)
        if streams["dve"]:
            @block.vector
            def _(eng):
                run("dve", eng)
        if streams["pool"]:
            @block.gpsimd
            def _(eng):
                run("pool", eng)

    def close(self):
        self.stack.close()


D = 1024
DFF = 2816
NT = 2048
EPS = 1e-6


def build_mods():
    nc = bass.Bass("TRN2", target_bir_lowering=False)
    P = Prog(nc)
    cT = nc.dram_tensor("cT", [128, 8, 2], F32, kind="ExternalInput").ap()
    aw = nc.dram_tensor("aw", [128, 8, 4608], F32, kind="ExternalInput").ap()
    ab = nc.dram_tensor("ab", [128, 36], F32, kind="ExternalInput").ap()
    out = nc.dram_tensor("modsT", [128, 36, 2], F32, kind="ExternalOutput").ap()
    c_sb = P.sbuf("c_sb", [128, 8, 2], F32); bc = P.buf()
    ca = P.sbuf("ca", [128, 8, 2], F32); bca = P.buf()
    ab_sb = P.sbuf("ab_sb", [128, 36], F32); bab = P.buf()
    res = P.sbuf("res", [128, 36, 2], F32); bres = P.buf()
    ps = P.psum("ps", [128, 36, 2]); bps = P.buf()
    w = [P.sbuf(f"w{i}", [128, 8, 1152], F32) for i in range(2)]
    bw = [P.buf() for _ in range(2)]
    P.dma("sp", lambda e: e.dma_start(out=c_sb[:], in_=cT), writes=[bc])
    P.dma("sp", lambda e: e.dma_start(out=ab_sb[:], in_=ab), writes=[bab])
    P.op("act", lambda e: e.activation(out=ca[:], in_=c_sb[:], func=AF.Silu), reads=[bc], writes=[bca])
    for g in range(4):
        k = g % 2
        P.dma("sp", lambda e, g=g, k=k: e.dma_start(out=w[k][:], in_=aw[:, :, g * 1152:(g + 1) * 1152]), writes=[bw[k]])
        for j in range(9):
            jj = g * 9 + j
            for kc in range(8):
                P.op("pe", lambda e, k=k, j=j, jj=jj, kc=kc: e.matmul(
                    ps[:, jj, :], lhsT=w[k][:, kc, j * 128:(j + 1) * 128], rhs=ca[:, kc, :],
                    start=(kc == 0), stop=(kc == 7)), reads=[bw[k], bca], writes=[bps], nosync_same=True)
    for b in range(2):
        P.op("dve", lambda e, b=b: e.tensor_tensor(out=res[:, :, b], in0=ps[:, :, b], in1=ab_sb[:], op=ALU.add),
             reads=[bps, bab], writes=[bres])
    bo = P.buf()
    P.dma("sp", lambda e: e.dma_start(out=out, in_=res[:]), reads=[bres], writes=[bo])
    P.final_wait("sp", [bo])
    P.emit(); P.close()
    return nc


class TokCtx:
    pass


def build_token(F_in, do_next, final):
    nc = bass.Bass("TRN2", target_bir_lowering=False)
    P = Prog(nc)
    dt = nc.dram_tensor
    xT_in = dt("xT_in", [8, 128, NT], F32, kind="ExternalInput").ap()
    xT_out = dt("xT_out", [8, 128, NT], F32, kind="ExternalOutput").ap()
    if F_in:
        KO = F_in // 128
        oT = dt("oT", [KO, 128, NT], BF16, kind="ExternalInput").ap()
        wmo = dt("wmo", [128, KO, D], F32, kind="ExternalInput").ap()
        w_in2 = dt("w_in2", [11, 128, 8, 512], F32, kind="ExternalInput").ap()
        w_out2 = dt("w_out2", [4, 128, 22, 256], F32, kind="ExternalInput").ap()
        modsA = dt("modsA", [128, 72], F32, kind="ExternalInput").ap()
        gA = dt("gA", [128, 3, 8], F32, kind="ExternalInput").ap()
    if do_next:
        w_in1 = dt("w_in1", [11, 128, 8, 512], F32, kind="ExternalInput").ap()
        w_out1 = dt("w_out1", [4, 128, 22, 256], F32, kind="ExternalInput").ap()
        modsB = dt("modsB", [128, 72], F32, kind="ExternalInput").ap()
        gB = dt("gB", [128, 3, 8], F32, kind="ExternalInput").ap()
        hT_out = dt("hT_out", [8, 128, NT], BF16, kind="ExternalOutput").ap()
    if final:
        gF = dt("gF", [128, 8], F32, kind="ExternalInput").ap()

    x = P.sbuf("x", [128, 8, NT], F32)
    bx = [[P.buf(f"x{h}_{c}") for c in range(8)] for h in range(4)]
    ones = P.sbuf("ones", [128, 128], BF16); bones = P.buf()
    hT = P.sbuf("hT", [128, 8, 1024], BF16)
    bh = [[P.buf() for _ in range(8)] for _ in range(2)]
    act = P.sbuf("act", [128, 22, 1024], BF16)
    bact = [[P.buf() for _ in range(22)] for _ in range(2)]
    wi = [P.sbuf(f"wi{i}", [128, 8, 512], BF16) for i in range(3)]
    bwi = [P.buf() for _ in range(3)]
    wo = [P.sbuf(f"wo{i}", [128, 22, 256], BF16) for i in range(2)]
    bwo = [P.buf() for _ in range(2)]
    sq = [P.sbuf(f"sq{i}", [128, 512], BF16) for i in range(2)]
    bsq = [P.buf() for _ in range(2)]
    rstd = P.sbuf("rstd", [128, 512], F32); brstd = P.buf()
    tmp = [P.sbuf(f"tmp{i}", [128, 512], F32) for i in range(2)]
    btmp = [P.buf() for _ in range(2)]
    sg = [P.sbuf(f"sg{i}", [128, 512], F32) for i in range(2)]
    bsg = [P.buf() for _ in range(2)]
    mods = {}
    gsb = {}
    small = P.sbuf("small", [128, 2, 72 + 24 + 72], F32)
    bsmall = [P.buf() for _ in range(2)]
    gf_sb = P.sbuf("gf_sb", [128, 8], F32); bgf = P.buf()
    pg = [P.psum(f"pg{i}", [128, 512]) for i in range(2)]; bpg = [P.buf(excl=True) for _ in range(2)]
    pu = [P.psum(f"pu{i}", [128, 512]) for i in range(2)]; bpu = [P.buf(excl=True) for _ in range(2)]
    po = [P.psum(f"po{i}", [128, 512]) for i in range(2)]; bpo = [P.buf(excl=True) for _ in range(2)]
    pn = P.psum("pn", [128, 512]); bpn = P.buf(excl=True)
    cnt = {"wi": 0, "wo": 0, "sq": 0, "tmp": 0, "sg": 0, "pg": 0, "po": 0}

    P.op("dve", lambda e: e.memset(ones[:], 1.0), writes=[bones])
    for c in range(8):
        for h in range(4):
            P.dma("sp", lambda e, c=c, h=h: e.dma_start(out=x[:, c, h * 512:(h + 1) * 512], in_=xT_in[c, :, h * 512:(h + 1) * 512]),
                  writes=[bx[h][c]])

    def load_small(slot, mods_ap, g_ap):
        P.dma("sp", lambda e: e.dma_start(out=small[:, slot, 0:72], in_=mods_ap), writes=[bsmall[slot]])
        P.dma("sp", lambda e: e.dma_start(out=small[:, slot, 72:96], in_=g_ap.rearrange("p a b -> p (a b)")), writes=[bsmall[slot]])
        for sub in range(3):
            sc = small[:, slot, sub * 24 + 8: sub * 24 + 16]
            gg = small[:, slot, 72 + sub * 8: 72 + sub * 8 + 8]
            dst = small[:, slot, 96 + sub * 8: 96 + sub * 8 + 8]
            P.op("dve", lambda e, sc=sc, gg=gg, dst=dst: e.scalar_tensor_tensor(out=dst, in0=sc, scalar=1.0, in1=gg, op0=ALU.add, op1=ALU.mult),
                 reads=[bsmall[slot]], writes=[bsmall[slot]])
            gt = small[:, slot, sub * 24 + 16: sub * 24 + 24]
            dst2 = small[:, slot, 120 + sub * 8: 120 + sub * 8 + 8]
            P.op("dve", lambda e, gt=gt, dst2=dst2, sub=sub: e.tensor_scalar(out=dst2, in0=gt, scalar1=(1.0 if sub == 1 else 0.5), scalar2=None, op0=ALU.mult),
                 reads=[bsmall[slot]], writes=[bsmall[slot]])

    def shift_ap(slot, sub, c):
        return small[:, slot, sub * 24 + c: sub * 24 + c + 1]

    def gs_ap(slot, sub, c):
        return small[:, slot, 96 + sub * 8 + c: 96 + sub * 8 + c + 1]

    def gm_ap(slot, sub, c):
        return small[:, slot, 120 + sub * 8 + c: 120 + sub * 8 + c + 1]

    def norm_tile(h4, slot, sub, dst_fn, dst_bufs, plain_g=None):
        tsl = slice(h4 * 512, (h4 + 1) * 512)
        for c in range(8):
            k = cnt["sq"] % 2; cnt["sq"] += 1
            P.op("act", lambda e, c=c, k=k: e.activation(out=sq[k][:], in_=x[:, c, tsl], func=AF.Square),
                 reads=[bx[h4][c]], writes=[bsq[k]])
            P.op("pe", lambda e, c=c, k=k: e.matmul(pn[:], lhsT=ones[:], rhs=sq[k][:], start=(c == 0), stop=(c == 7)),
                 reads=[bones, bsq[k]], writes=[bpn], nosync_same=True)
        P.op("act", lambda e: e.activation(out=rstd[:], in_=pn[:], func=AF.Sqrt, bias=EPS, scale=1.0 / D),
             reads=[bpn], writes=[brstd])
        P.op("dve", lambda e: e.reciprocal(out=rstd[:], in_=rstd[:]), reads=[brstd], writes=[brstd])
        for c in range(8):
            k = cnt["tmp"] % 2; cnt["tmp"] += 1
            P.op("dve", lambda e, c=c, k=k: e.tensor_tensor(out=tmp[k][:], in0=x[:, c, tsl], in1=rstd[:], op=ALU.mult),
                 reads=[bx[h4][c], brstd], writes=[btmp[k]])
            if plain_g is None:
                P.op("act", lambda e, c=c, k=k: e.activation(out=dst_fn(c), in_=tmp[k][:], func=AF.Identity,
                                                               bias=shift_ap(slot, sub, c), scale=gs_ap(slot, sub, c)),
                     reads=[btmp[k], bsmall[slot]], writes=dst_bufs(c))
            else:
                P.op("act", lambda e, c=c, k=k: e.activation(out=dst_fn(c), in_=tmp[k][:], func=AF.Identity,
                                                               scale=plain_g[:, c:c + 1]),
                     reads=[btmp[k], bgf], writes=dst_bufs(c))

    def u_norm(u):
        slot, sub, w_in_ap, w_out_ap, half = u
        for s2 in range(2):
            h4 = half * 2 + s2
            norm_tile(h4, slot, sub, lambda c, s2=s2: hT[:, c, s2 * 512:(s2 + 1) * 512], lambda c, s2=s2: [bh[s2][c]])

    def u_inproj(u):
        slot, sub, w_in_ap, w_out_ap, half = u
        for g in range(11):
            k = cnt["wi"] % 3; cnt["wi"] += 1
            P.dma("pool", lambda e, g=g, k=k: e.dma_start(out=wi[k][:], in_=w_in_ap[g]), writes=[bwi[k]])
            for s2 in range(2):
                for cp in range(2):
                    j = g * 2 + cp
                    q = cnt["pg"] % 2; cnt["pg"] += 1
                    for kc in range(8):
                        P.op("pe", lambda e, k=k, q=q, kc=kc, cp=cp, s2=s2: e.matmul(
                            pg[q][:], lhsT=wi[k][:, kc, cp * 128:(cp + 1) * 128], rhs=hT[:, kc, s2 * 512:(s2 + 1) * 512],
                            start=(kc == 0), stop=(kc == 7)), reads=[bwi[k], bh[s2][kc]], writes=[bpg[q]], nosync_same=True)
                    for kc in range(8):
                        P.op("pe", lambda e, k=k, q=q, kc=kc, cp=cp, s2=s2: e.matmul(
                            pu[q][:], lhsT=wi[k][:, kc, 256 + cp * 128:256 + (cp + 1) * 128], rhs=hT[:, kc, s2 * 512:(s2 + 1) * 512],
                            start=(kc == 0), stop=(kc == 7)), reads=[bwi[k], bh[s2][kc]], writes=[bpu[q]], nosync_same=True)
                    r = cnt["sg"] % 2; cnt["sg"] += 1
                    P.op("act", lambda e, q=q, r=r: e.activation(out=sg[r][:], in_=pg[q][:], func=AF.Silu),
                         reads=[bpg[q]], writes=[bsg[r]])
                    P.op("dve", lambda e, q=q, r=r, j=j, s2=s2: e.tensor_tensor(
                        out=act[:, j, s2 * 512:(s2 + 1) * 512], in0=pu[q][:], in1=sg[r][:], op=ALU.mult),
                        reads=[bpu[q], bsg[r]], writes=[bact[s2][j]])

    def u_outproj(u):
        slot, sub, w_in_ap, w_out_ap, half = u
        for og in range(4):
            k = cnt["wo"] % 2; cnt["wo"] += 1
            P.dma("pool", lambda e, og=og, k=k: e.dma_start(out=wo[k][:], in_=w_out_ap[og]), writes=[bwo[k]])
            for s2 in range(2):
                h4 = half * 2 + s2
                for ocl in range(2):
                    oc = og * 2 + ocl
                    q = cnt["po"] % 2; cnt["po"] += 1
                    for kc in range(22):
                        P.op("pe", lambda e, k=k, q=q, kc=kc, ocl=ocl, s2=s2: e.matmul(
                            po[q][:], lhsT=wo[k][:, kc, ocl * 128:(ocl + 1) * 128], rhs=act[:, kc, s2 * 512:(s2 + 1) * 512],
                            start=(kc == 0), stop=(kc == 21)), reads=[bwo[k], bact[s2][kc]], writes=[bpo[q]], nosync_same=True)
                    P.op("dve", lambda e, q=q, oc=oc, h4=h4: e.scalar_tensor_tensor(
                        out=x[:, oc, h4 * 512:(h4 + 1) * 512], in0=po[q][:], scalar=gm_ap(slot, sub, oc),
                        in1=x[:, oc, h4 * 512:(h4 + 1) * 512], op0=ALU.mult, op1=ALU.add),
                        reads=[bpo[q], bsmall[slot]], writes=[bx[h4][oc]])

    def run_units(units):
        if not units:
            return
        u_norm(units[0])
        for i, u in enumerate(units):
            u_inproj(u)
            if i + 1 < len(units):
                u_norm(units[i + 1])
            u_outproj(u)

    if F_in:
        load_small(0, modsA, gA)
        KO = F_in // 128
        otv = hT[:].rearrange("p a (b t) -> p (a b) t", t=512)
        nbuf_o = 16 // KO
        for kc in range(KO):
            P.dma("pool", lambda e, kc=kc: e.dma_start(out=act[:, kc, :], in_=wmo[:, kc, :]), writes=[bact[0][kc], bact[1][kc]])
        for h4 in range(4):
            k = h4 % nbuf_o
            for kc in range(KO):
                c16 = k * KO + kc
                P.dma("sp", lambda e, kc=kc, c16=c16, h4=h4: e.dma_start(out=otv[:, c16, :], in_=oT[kc, :, h4 * 512:(h4 + 1) * 512]),
                      writes=[bh[c16 % 2][c16 // 2]])
            for oc in range(8):
                q = cnt["po"] % 2; cnt["po"] += 1
                for kc in range(KO):
                    c16 = k * KO + kc
                    P.op("pe", lambda e, q=q, kc=kc, oc=oc, c16=c16: e.matmul(
                        po[q][:], lhsT=act[:, kc, oc * 128:(oc + 1) * 128], rhs=otv[:, c16, :],
                        start=(kc == 0), stop=(kc == KO - 1)), reads=[bact[oc // 4][kc], bh[c16 % 2][c16 // 2]], writes=[bpo[q]], nosync_same=True)
                P.op("dve", lambda e, q=q, oc=oc, h4=h4: e.scalar_tensor_tensor(
                    out=x[:, oc, h4 * 512:(h4 + 1) * 512], in0=po[q][:], scalar=gm_ap(0, 1, oc),
                    in1=x[:, oc, h4 * 512:(h4 + 1) * 512], op0=ALU.mult, op1=ALU.add),
                    reads=[bpo[q], bsmall[0]], writes=[bx[h4][oc]])
    outs = []
    units = []
    if F_in:
        units += [(0, 2, w_in2, w_out2, 0), (0, 2, w_in2, w_out2, 1)]
    if do_next:
        load_small(1, modsB, gB)
        units += [(1, 0, w_in1, w_out1, 0), (1, 0, w_in1, w_out1, 1)]
    run_units(units)
    if do_next:
        for h4 in range(4):
            s2 = h4 % 2
            norm_tile(h4, 1, 1, lambda c, s2=s2: hT[:, c, s2 * 512:(s2 + 1) * 512], lambda c, s2=s2: [bh[s2][c]])
            for c in range(8):
                b = P.buf(); outs.append(b)
                P.dma("sp", lambda e, c=c, s2=s2, h4=h4: e.dma_start(out=hT_out[c, :, h4 * 512:(h4 + 1) * 512],
                                                                     in_=hT[:, c, s2 * 512:(s2 + 1) * 512]),
                      reads=[bh[s2][c]], writes=[b])
    if final:
        P.dma("sp", lambda e: e.dma_start(out=gf_sb[:], in_=gF), writes=[bgf])
        for h4 in range(4):
            norm_tile(h4, 0, 0, lambda c, h4=h4: x[:, c, h4 * 512:(h4 + 1) * 512], lambda c, h4=h4: [bx[h4][c]], plain_g=gf_sb)
    for c in range(8):
        for h4 in range(4):
            b = P.buf(); outs.append(b)
            P.dma("sp", lambda e, c=c, h4=h4: e.dma_start(out=xT_out[c, :, h4 * 512:(h4 + 1) * 512], in_=x[:, c, h4 * 512:(h4 + 1) * 512]),
                  reads=[bx[h4][c]], writes=[b])
    P.final_wait("sp", outs)
    P.emit(); P.close()
    return nc


S = 8192
NQT = 16


def attn_core(P, cnt, R, kT, bk, kdim, vaug, bv, q_tile_fn, scale, out_dram, h, col0, masks, bmask, exp_fn=None):
    for j in range(NQT):
        qT, bq = q_tile_fn(j)
        oq = R["cnt_o"] % 2; R["cnt_o"] += 1
        po, bpo = R["po"][oq], R["bpo"][oq]
        nkb = 4 * j + 4

        def emit_S(kb, qT=qT, bq=bq):
            sq = R["cnt_s"] % 3; R["cnt_s"] += 1
            ps, bps = R["ps"][sq], R["bps"][sq]
            P.op("pe", lambda e, kb=kb, ps=ps, qT=qT: e.matmul(ps[:], lhsT=kT[0:kdim, kb * 128:(kb + 1) * 128], rhs=qT, start=True, stop=True),
                 reads=[bk, bq], writes=[bps], nosync_same=True)
            return ps, bps
        pend = [emit_S(0)]
        if nkb > 1:
            pend.append(emit_S(1))
        for kb in range(nkb):
            if kb + 2 < nkb:
                pend.append(emit_S(kb + 2))
            ps, bps = pend.pop(0)
            pq = R["cnt_p"] % 3; R["cnt_p"] += 1
            pt, bpt = R["pt"][pq], R["bpt"][pq]
            d = kb - 4 * j
            if exp_fn is not None:
                exp_fn(j, kb, ps, bps, pt, bpt)
            else:
                P.op("act", lambda e, ps=ps, pt=pt: e.activation(out=pt[:], in_=ps[:], func=AF.Exp, scale=scale), reads=[bps], writes=[bpt])
            if d >= 0:
                P.op("dve", lambda e, pt=pt, d=d: e.tensor_tensor(out=pt[:], in0=pt[:], in1=masks[:, d, :], op=ALU.mult),
                     reads=[bpt, bmask], writes=[bpt])
            for qs in range(4):
                if d > qs:
                    continue
                last = 4 * j + qs
                P.op("pe", lambda e, pt=pt, qs=qs, kb=kb, po=po, last=last: e.matmul(
                    po[:, qs, :], lhsT=pt[:, qs * 128:(qs + 1) * 128], rhs=vaug[:, kb, :], start=(kb == 0 and qs == 0), stop=(kb == last),
                    skip_group_check=True),
                    reads=[bpt, bv], writes=[bpo], nosync_same=True)
        rq = R["cnt_r"] % 2; R["cnt_r"] += 1
        rden, brden = R["rden"][rq], R["brden"][rq]
        ot, bot = R["ot"][rq], R["bot"][rq]
        P.op("dve", lambda e, po=po, rden=rden: e.reciprocal(out=rden[:], in_=po[:, :, 64]), reads=[bpo], writes=[brden])
        for qs in range(4):
            P.op("dve", lambda e, po=po, rden=rden, ot=ot, qs=qs: e.tensor_scalar(
                out=ot[:, qs, :], in0=po[:, qs, 0:64], scalar1=rden[:, qs:qs + 1], scalar2=None, op0=ALU.mult),
                reads=[bpo, brden], writes=[bot])
        b = P.buf(); R["outs"].append(b)
        P.dma("sp", lambda e, ot=ot, j=j: e.dma_start(
            out=out_dram[j * 512:(j + 1) * 512, col0:col0 + 64].rearrange("(a p) c -> p a c", p=128), in_=ot[:]),
            reads=[bot], writes=[b])


def attn_resources(P):
    R = {"cnt_o": 0, "cnt_s": 0, "cnt_p": 0, "cnt_r": 0, "outs": []}
    R["po"] = [P.psum(f"a_po{i}", [128, 4, 128])[:, :, 0:65] for i in range(2)]; R["bpo"] = [P.buf(excl=True) for _ in range(2)]
    R["ps"] = [P.psum(f"a_ps{i}", [128, 512]) for i in range(3)]; R["bps"] = [P.buf(excl=True) for _ in range(3)]
    R["pt"] = [P.sbuf(f"a_pt{i}", [128, 512], BF16) for i in range(3)]; R["bpt"] = [P.buf() for _ in range(3)]
    R["rden"] = [P.sbuf(f"a_rd{i}", [128, 4], F32) for i in range(2)]; R["brden"] = [P.buf() for _ in range(2)]
    R["ot"] = [P.sbuf(f"a_ot{i}", [128, 4, 64], BF16) for i in range(2)]; R["bot"] = [P.buf() for _ in range(2)]
    return R


def causal_masks_np():
    m = np.zeros((128, 4, 512), np.float32)
    p = np.arange(128)[:, None]; f = np.arange(512)[None, :]
    for d in range(4):
        m[:, d, :] = (128 * d + p <= f)
    return m


def build_mla():
    nc = bass.Bass("TRN2", target_bir_lowering=False)
    P = Prog(nc)
    dt = nc.dram_tensor
    hT_d = dt("hT", [8, 128, S], BF16, kind="ExternalInput").ap()
    w_in_d = dt("w_in", [128, 8, 832], F32, kind="ExternalInput").ap()
    wuq_d = dt("wuq", [4, 128, 3, 192], F32, kind="ExternalInput").ap()
    wuk_d = dt("wuk", [4, 128, 2, 64], F32, kind="ExternalInput").ap()
    wuv_d = dt("wuv", [128, 2, 256], F32, kind="ExternalInput").ap()
    qn_d = dt("qn", [128, 3], F32, kind="ExternalInput").ap()
    kvn_d = dt("kvn", [128, 2], F32, kind="ExternalInput").ap()
    cs_d = dt("cs", [96, 2, S], F32, kind="ExternalInput").ap()
    mask_d = dt("mask", [128, 4, 512], F32, kind="ExternalInput").ap()
    o_d = dt("o", [S, 256], BF16, kind="ExternalOutput").ap()

    w_in = P.sbuf("w_in_sb", [128, 8, 832], BF16); bw_in = P.buf()
    wuq = P.sbuf("wuq_sb", [128, 4, 3, 192], BF16); bwuq = P.buf()
    wuk = P.sbuf("wuk_sb", [128, 4, 2, 64], BF16); bwuk = P.buf()
    wuv = P.sbuf("wuv_sb", [128, 2, 256], BF16); bwuv = P.buf()
    qn = P.sbuf("qn_sb", [128, 3], F32); bqn = P.buf()
    kvn = P.sbuf("kvn_sb", [128, 2], F32); bkvn = P.buf()
    masks = P.sbuf("masks_sb", [128, 4, 512], BF16); bmask = P.buf()
    ones = P.sbuf("ones", [128, 128], BF16); bones = P.buf()
    cqn = P.sbuf("cqn", [128, 3, S], BF16); bcqn = [P.buf() for _ in range(NQT)]
    ckvn = P.sbuf("ckvn", [128, 2, S], BF16); bckvn = [P.buf() for _ in range(NQT)]
    kT1 = P.sbuf("kT", [96, S], BF16); bkr = [P.buf() for _ in range(NQT)]; bkT1 = P.buf()
    ht = [P.sbuf(f"ht{i}", [128, 8, 512], BF16) for i in range(2)]; bht = [[P.buf() for _ in range(8)] for _ in range(2)]
    lat = [P.sbuf(f"lat{i}", [128, 512], F32) for i in range(3)]; blat = [P.buf() for _ in range(3)]
    sqt = [P.sbuf(f"sqt{i}", [128, 512], BF16) for i in range(2)]; bsqt = [P.buf() for _ in range(2)]
    rstd = P.sbuf("rstd", [128, 512], F32); brstd = P.buf()
    cst = [P.sbuf(f"cst{i}", [96, 2, 512], F32) for i in range(2)]; bcst = [P.buf() for _ in range(2)]
    t1 = P.sbuf("t1", [96, 512], F32); bt1 = P.buf()
    t2 = P.sbuf("t2", [96, 512], F32); bt2 = P.buf()
    pl = [P.psum(f"pl{i}", [128, 512]) for i in range(2)]; bpl = [P.buf(excl=True) for _ in range(2)]
    pn = P.psum("pn", [128, 512]); bpn = P.buf(excl=True)
    cnt = {"lat": 0, "sq": 0, "pl": 0}

    P.dma("pool", lambda e: e.dma_start(out=w_in[:], in_=w_in_d), writes=[bw_in])
    for h in range(4):
        P.dma("pool", lambda e, h=h: e.dma_start(out=wuq[:, h], in_=wuq_d[h]), writes=[bwuq])
        P.dma("pool", lambda e, h=h: e.dma_start(out=wuk[:, h], in_=wuk_d[h]), writes=[bwuk])
    P.dma("pool", lambda e: e.dma_start(out=wuv[:], in_=wuv_d), writes=[bwuv])
    P.dma("pool", lambda e: e.dma_start(out=masks[:], in_=mask_d), writes=[bmask])
    P.dma("sp", lambda e: e.dma_start(out=qn[:], in_=qn_d), writes=[bqn])
    P.dma("sp", lambda e: e.dma_start(out=kvn[:], in_=kvn_d), writes=[bkvn])
    P.op("dve", lambda e: e.memset(ones[:], 1.0), writes=[bones])

    def proj_chunk(k, col0, M, tile_rhs_bufs):
        q = cnt["pl"] % 2; cnt["pl"] += 1
        for kc in range(8):
            P.op("pe", lambda e, kc=kc, q=q: e.matmul(pl[q][0:M, :], lhsT=w_in[:, kc, col0:col0 + M], rhs=ht[k][:, kc, :],
                                                      start=(kc == 0), stop=(kc == 7)),
                 reads=[bw_in, bht[k][kc]], writes=[bpl[q]], nosync_same=True)
        return q

    for t in range(NQT):
        k = t % 2
        tsl = slice(t * 512, (t + 1) * 512)
        for kc in range(8):
            P.dma("sp", lambda e, kc=kc, k=k, tsl=tsl: e.dma_start(out=ht[k][:, kc, :], in_=hT_d[kc, :, tsl]), writes=[bht[k][kc]])
        for grp, (c0, nch, gain, bgain, dst, bdst, rank) in enumerate([(0, 3, qn, bqn, cqn, bcqn, 384), (384, 2, kvn, bkvn, ckvn, bckvn, 256)]):
            lats = []
            for c in range(nch):
                q = proj_chunk(k, c0 + c * 128, 128, None)
                li = cnt["lat"] % 3; cnt["lat"] += 1
                lats.append(li)
                P.op("act", lambda e, q=q, li=li: e.activation(out=lat[li][:], in_=pl[q][:], func=AF.Identity), reads=[bpl[q]], writes=[blat[li]])
                si = cnt["sq"] % 2; cnt["sq"] += 1
                P.op("dve", lambda e, li=li, si=si: e.tensor_tensor(out=sqt[si][:], in0=lat[li][:], in1=lat[li][:], op=ALU.mult),
                     reads=[blat[li]], writes=[bsqt[si]])
                P.op("pe", lambda e, si=si, c=c, nch=nch: e.matmul(pn[:], lhsT=ones[:], rhs=sqt[si][:], start=(c == 0), stop=(c == nch - 1)),
                     reads=[bones, bsqt[si]], writes=[bpn], nosync_same=True)
            P.op("act", lambda e, rank=rank: e.activation(out=rstd[:], in_=pn[:], func=AF.Sqrt, bias=1e-6, scale=1.0 / rank), reads=[bpn], writes=[brstd])
            P.op("dve", lambda e: e.reciprocal(out=rstd[:], in_=rstd[:]), reads=[brstd], writes=[brstd])
            for c in range(nch):
                li = lats[c]
                P.op("dve", lambda e, li=li, c=c, dst=dst, gain=gain, tsl=tsl: e.scalar_tensor_tensor(
                    out=dst[:, c, tsl], in0=lat[li][:], scalar=gain[:, c:c + 1], in1=rstd[:], op0=ALU.mult, op1=ALU.mult),
                    reads=[blat[li], bgain, brstd], writes=[bdst[t]])
        kc_ = t % 2
        P.dma("sp", lambda e, kc_=kc_, tsl=tsl: e.dma_start(out=cst[kc_][:], in_=cs_d[:, :, tsl]), writes=[bcst[kc_]])
        qa = proj_chunk(k, 640, 96, None)
        P.op("dve", lambda e, qa=qa, kc_=kc_: e.tensor_tensor(out=t1[64:96, :], in0=pl[qa][64:96, :], in1=cst[kc_][64:96, 0, :], op=ALU.mult),
             reads=[bpl[qa], bcst[kc_]], writes=[bt1])
        qb = proj_chunk(k, 736, 96, None)
        P.op("dve", lambda e, qb=qb, kc_=kc_: e.tensor_tensor(out=t2[64:96, :], in0=pl[qb][64:96, :], in1=cst[kc_][64:96, 1, :], op=ALU.mult),
             reads=[bpl[qb], bcst[kc_]], writes=[bt2])
        P.op("dve", lambda e, tsl=tsl: e.tensor_tensor(out=kT1[64:96, tsl], in0=t1[64:96, :], in1=t2[64:96, :], op=ALU.add),
             reads=[bt1, bt2], writes=[bkr[t]])

    R = attn_resources(P)
    kT = [kT1, kT1]; bkT = [bkT1, bkT1]
    va1 = P.sbuf("va", [128, 64, 65], BF16); bva1 = P.buf()
    va = [va1, va1]; bva = [bva1, bva1]
    qt = [P.sbuf(f"qt{i}", [96, 512], BF16) for i in range(2)]; bqt = [P.buf() for _ in range(2)]
    cntq = {"q": 0}
    for h in range(4):
        hb = h % 2
        P.op("pool", lambda e, hb=hb: e.memset(va[hb][:, :, 64:65], 1.0), writes=[bva[hb]])
        for t in range(NQT):
            tsl = slice(t * 512, (t + 1) * 512)
            q = cnt["pl"] % 2; cnt["pl"] += 1
            for kc in range(2):
                P.op("pe", lambda e, kc=kc, q=q, h=h, tsl=tsl: e.matmul(pl[q][0:64, :], lhsT=wuk[:, h, kc, :], rhs=ckvn[:, kc, tsl],
                                                                         start=(kc == 0), stop=(kc == 1)),
                     reads=[bwuk, bckvn[t]], writes=[bpl[q]], nosync_same=True)
            P.op("act", lambda e, q=q, hb=hb, tsl=tsl: e.activation(out=kT[hb][0:64, tsl], in_=pl[q][0:64, :], func=AF.Identity),
                 reads=[bpl[q]] + (bkr if (t == 0) else []), writes=[bkT[hb]])
            for tb in range(4):
                blk = t * 4 + tb
                q = cnt["pl"] % 2; cnt["pl"] += 1
                for kc in range(2):
                    P.op("pe", lambda e, kc=kc, q=q, h=h, blk=blk: e.matmul(
                        pl[q][:, 0:64], lhsT=ckvn[:, kc, blk * 128:(blk + 1) * 128], rhs=wuv[:, kc, h * 64:(h + 1) * 64],
                        start=(kc == 0), stop=(kc == 1)), reads=[bwuv, bckvn[t]], writes=[bpl[q]], nosync_same=True)
                P.op("act", lambda e, q=q, hb=hb, blk=blk: e.activation(out=va[hb][:, blk, 0:64], in_=pl[q][:, 0:64], func=AF.Identity),
                     reads=[bpl[q]], writes=[bva[hb]])

        def q_tile(j, h=h):
            tsl = slice(j * 512, (j + 1) * 512)
            kc_ = cntq["q"] % 2; cntq["q"] += 1
            P.dma("sp", lambda e, kc_=kc_, tsl=tsl: e.dma_start(out=cst[kc_][:], in_=cs_d[:, :, tsl]), writes=[bcst[kc_]])
            qa = cnt["pl"] % 2; cnt["pl"] += 1
            for kc in range(3):
                P.op("pe", lambda e, kc=kc, qa=qa, tsl=tsl: e.matmul(pl[qa][0:96, :], lhsT=wuq[:, h, kc, 0:96], rhs=cqn[:, kc, tsl],
                                                                    start=(kc == 0), stop=(kc == 2)),
                     reads=[bwuq, bcqn[j]], writes=[bpl[qa]], nosync_same=True)
            P.op("dve", lambda e, qa=qa, kc_=kc_: e.tensor_tensor(out=t1[:, :], in0=pl[qa][0:96, :], in1=cst[kc_][:, 0, :], op=ALU.mult),
                 reads=[bpl[qa], bcst[kc_]], writes=[bt1])
            qb = cnt["pl"] % 2; cnt["pl"] += 1
            for kc in range(3):
                P.op("pe", lambda e, kc=kc, qb=qb, tsl=tsl: e.matmul(pl[qb][0:96, :], lhsT=wuq[:, h, kc, 96:192], rhs=cqn[:, kc, tsl],
                                                                    start=(kc == 0), stop=(kc == 2)),
                     reads=[bwuq, bcqn[j]], writes=[bpl[qb]], nosync_same=True)
            P.op("dve", lambda e, qb=qb, kc_=kc_: e.tensor_tensor(out=t2[:, :], in0=pl[qb][0:96, :], in1=cst[kc_][:, 1, :], op=ALU.mult),
                 reads=[bpl[qb], bcst[kc_]], writes=[bt2])
            P.op("dve", lambda e, kc_=kc_: e.tensor_tensor(out=qt[kc_][:, :], in0=t1[:, :], in1=t2[:, :], op=ALU.add),
                 reads=[bt1, bt2], writes=[bqt[kc_]])
            return qt[kc_][:, :], bqt[kc_]

        attn_core(P, cnt, R, kT[hb], bkT[hb], 96, va[hb], bva[hb], q_tile, 96 ** -0.5, o_d, h, h * 64, masks, bmask)
    print('mla sbuf remaining', nc.sbuf_bytes_remaining, {e: len(v) for e, v in P.streams.items()})
    P.final_wait("sp", R["outs"])
    P.emit(); P.close()
    return nc


def mla_host_inputs(inp, hT_full, b, g):
    w_in = inp["mla_w_in"][0]
    kr = w_in[:, 640:672]
    krp = np.concatenate([kr[:, 16:], kr[:, :16]], axis=1)
    z64 = np.zeros((1024, 64), np.float32)
    w_in_l = np.concatenate([w_in[:, :640], z64, kr, z64, krp], axis=1)
    w_in_l = w_in_l.reshape(8, 128, 832).transpose(1, 0, 2)
    wuq = inp["mla_w_uq"][0].reshape(384, 16, 96)
    wukv = inp["mla_w_ukv"][0].reshape(256, 16, 128)
    wq_l = np.zeros((4, 128, 3, 192), np.float32)
    wk_l = np.zeros((4, 128, 2, 64), np.float32)
    wv_l = np.zeros((128, 2, 256), np.float32)
    for hh in range(4):
        H = g * 4 + hh
        wq = wuq[:, H, :]
        wqp = np.concatenate([wq[:, :64], wq[:, 80:96], wq[:, 64:80]], axis=1)
        wq_l[hh] = np.concatenate([wq, wqp], axis=1).reshape(3, 128, 192).transpose(1, 0, 2)
        wk_l[hh] = wukv[:, H, :64].reshape(2, 128, 64).transpose(1, 0, 2)
        wv_l[:, :, hh * 64:(hh + 1) * 64] = wukv[:, H, 64:].reshape(2, 128, 64).transpose(1, 0, 2)
    inv = 10000.0 ** (-np.arange(0, 32, 2, dtype=np.float32) / 32)
    ang = np.arange(S, dtype=np.float32)[:, None] * inv[None, :]
    cos, sin = np.cos(ang).T, np.sin(ang).T
    cs = np.zeros((96, 2, S), np.float32)
    cs[:64, 0] = 1.0
    cs[64:80, 0] = cos; cs[80:96, 0] = cos
    cs[64:80, 1] = -sin; cs[80:96, 1] = sin
    return {"hT": hT_full, "w_in": np.ascontiguousarray(w_in_l),
            "wuq": wq_l, "wuk": wk_l, "wuv": wv_l,
            "qn": np.ascontiguousarray(inp["mla_q_norm"][0].reshape(3, 128).T), "kvn": np.ascontiguousarray(inp["mla_kv_norm"][0].reshape(2, 128).T),
            "cs": cs, "mask": causal_masks_np()}


S = 8192
NCH = 64
NEG = -0.6065306597126334


def build_rwkv():
    nc = bass.Bass("TRN2", target_bir_lowering=False)
    P = Prog(nc)
    dt = nc.dram_tensor
    hT_d = dt("hT", [8, 128, S], BF16, kind="ExternalInput").ap()
    mu_d = dt("mu", [128, 6, 8], F32, kind="ExternalInput").ap()
    wrkv_d = dt("wrkv", [128, 3, 8, 256], F32, kind="ExternalInput").ap()
    wl1_d = dt("wl1", [128, 8, 288], F32, kind="ExternalInput").ap()
    wl2_d = dt("wl2", [128, 4, 256], F32, kind="ExternalInput").ap()
    rows_d = dt("rows", [128, 7, 256], F32, kind="ExternalInput").ap()
    cm_d = dt("cm", [128, 5, 128], F32, kind="ExternalInput").ap()
    o_d = dt("o", [S, 256], BF16, kind="ExternalOutput").ap()

    mu = P.sbuf("mu_sb", [128, 6, 8], F32); bmu = P.buf()
    wrkv = P.sbuf("wrkv_sb", [128, 3, 8, 256], BF16); bwrkv = P.buf()
    wl1 = P.sbuf("wl1_sb", [128, 8, 288], BF16); bwl1 = P.buf()
    wl2 = P.sbuf("wl2_sb", [128, 4, 256], BF16); bwl2 = P.buf()
    rows = P.sbuf("rows_sb", [128, 7, 256], F32); brows = P.buf()
    cm = P.sbuf("cm_sb", [128, 5, 128], F32); bcm = P.buf()
    TriT, ones, mST, mL, ident = (cm[:, i, :] for i in range(5))
    P.dma("sp", lambda e: e.dma_start(out=mu[:], in_=mu_d), writes=[bmu])
    P.dma("pool", lambda e: e.dma_start(out=wl2[:], in_=wl2_d), writes=[bwl2])
    P.dma("sp", lambda e: e.dma_start(out=rows[:], in_=rows_d), writes=[brows])
    P.dma("sp", lambda e: e.dma_start(out=cm[:], in_=cm_d), writes=[bcm])
    omu = P.sbuf("omu", [128, 6, 8], F32); bomu = P.buf()
    P.op("dve", lambda e: e.tensor_scalar(out=omu[:], in0=mu[:], scalar1=-1.0, scalar2=1.0, op0=ALU.mult, op1=ALU.add), reads=[bmu], writes=[bomu])
    wrkvB = P.sbuf("wrkvB_sb", [128, 3, 8, 256], BF16); bwrkvB = P.buf()
    wl1B = P.sbuf("wl1B_sb", [128, 8, 288], BF16); bwl1B = P.buf()
    stg = P.sbuf("stg", [128, 8, 288], F32); bstg = P.buf()
    for n in range(3):
        P.dma("sp", lambda e, n=n: e.dma_start(out=stg[:, :, 0:256], in_=wrkv_d[:, n, :, :]), writes=[bstg])
        for kc in range(8):
            P.op("dve", lambda e, n=n, kc=kc: e.tensor_scalar(out=wrkv[:, n, kc, :], in0=stg[:, kc, 0:256], scalar1=omu[:, n, kc:kc + 1], scalar2=None, op0=ALU.mult),
                 reads=[bstg, bomu], writes=[bwrkv])
            P.op("dve", lambda e, n=n, kc=kc: e.tensor_scalar(out=wrkvB[:, n, kc, :], in0=stg[:, kc, 0:256], scalar1=mu[:, n, kc:kc + 1], scalar2=None, op0=ALU.mult),
                 reads=[bstg, bmu], writes=[bwrkvB])
    P.dma("sp", lambda e: e.dma_start(out=stg[:], in_=wl1_d), writes=[bstg])
    for kc in range(8):
        for (n, c0, c1) in [(3, 0, 64), (4, 64, 128), (5, 128, 288)]:
            P.op("dve", lambda e, n=n, kc=kc, c0=c0, c1=c1: e.tensor_scalar(out=wl1[:, kc, c0:c1], in0=stg[:, kc, c0:c1], scalar1=omu[:, n, kc:kc + 1], scalar2=None, op0=ALU.mult),
                 reads=[bstg, bomu], writes=[bwl1])
            P.op("dve", lambda e, n=n, kc=kc, c0=c0, c1=c1: e.tensor_scalar(out=wl1B[:, kc, c0:c1], in0=stg[:, kc, c0:c1], scalar1=mu[:, n, kc:kc + 1], scalar2=None, op0=ALU.mult),
                 reads=[bstg, bmu], writes=[bwl1B])

    hA = [P.sbuf(f"hA{i}", [128, 8, 128], BF16) for i in range(2)]; bhA = [P.buf() for _ in range(2)]
    hB = [P.sbuf(f"hB{i}", [128, 8, 128], BF16) for i in range(2)]; bhB = [P.buf() for _ in range(2)]
    l1 = P.sbuf("l1", [128, 4, 128], BF16); bl1 = [P.buf() for _ in range(4)]
    F = {}
    def ft(name, shape=(128, 256), dtype=F32):
        F[name] = ([P.sbuf(f"f_{name}{i}", list(shape), dtype) for i in range(2)], [P.buf() for _ in range(2)])
        return F[name]
    for nm in ["r", "kraw", "v", "lw", "a", "gg", "kk", "kmod", "kb", "t0", "t1", "cs", "e1", "e2", "e3", "e4",
               "kat", "rt", "kbh", "kh", "kg", "kbg", "y", "cen", "yn"]:
        ft(nm)
    ft("ss", (128, 4)); ft("rn", (128, 4)); ft("bc", (128, 4)); ft("mean", (128, 4)); ft("var", (128, 4)); ft("rstd", (128, 4))
    ft("gcol", (64, 4)); ft("z", (128, 256), BF16)
    H = P.sbuf("H", [64, 4, 64], F32); bH = [P.buf() for _ in range(4)]
    featT2 = [[P.sbuf(f"featT{j}_{i}", [64, 4, 128], F32) for i in range(4)] for j in range(2)]; bfeat2 = [[P.buf() for _ in range(4)] for _ in range(2)]
    def sq2(nm):
        return ([[P.sbuf(f"{nm}{h}_{i}", [128, 128], F32) for i in range(2)] for h in range(4)], [[P.buf() for _ in range(2)] for _ in range(4)])
    def sq1(nm, w=128):
        return ([P.sbuf(f"{nm}{h}", [128, w], F32) for h in range(4)], [P.buf() for _ in range(4)])
    A_, bA = sq2("A"); Q_, bQ = sq2("Q"); Tt_, bTt = sq2("Tt"); Tm_, bTm = sq2("Tm")
    MbT, bMbT = sq1("MbT"); LkT, bLkT = sq1("LkT"); MkT, bMkT = sq1("MkT")
    W1s, bW1s = sq1("W1s", 64); Us, bUs = sq1("Us", 64)
    bank = [P.psum(f"bk{i}", [128, 512]) for i in range(8)]
    bb = [[P.buf(excl=True)] * 4 for _ in range(8)]

    for hh in range(4):
        P.op("dve", lambda e, hh=hh: e.memset(H[:, hh, :], 0.0), writes=[bH[hh]])

    def dve(fn, reads, writes): P.op("dve", fn, reads=reads, writes=writes)
    def act(fn, reads, writes): P.op("act", fn, reads=reads, writes=writes)
    def mm(out, lhsT, rhs, start, stop, reads, writes): P.op("pe", lambda e: e.matmul(out, lhsT=lhsT, rhs=rhs, start=start, stop=stop), reads=reads, writes=writes, nosync_same=True)

    def stageA(c):
        k = c % 2
        lo = c * 128
        par = c % 2
        def T(name): return F[name][0][par]
        def B(name): return F[name][1][par]
        featT = featT2[par]; bfeat = bfeat2[par]
        P.dma("sp", lambda e, k=k, lo=lo: e.dma_start(out=hA[k][:], in_=hT_d[:, :, lo:lo + 128].rearrange("k p t -> p k t")), writes=[bhA[k]])
        if c == 0:
            P.op("pool", lambda e, k=k: e.memset(hB[k][:, :, 0:1], 0.0), writes=[bhB[k]])
            P.dma("sp", lambda e, k=k: e.dma_start(out=hB[k][:, :, 1:128], in_=hT_d[:, :, 0:127].rearrange("k p t -> p k t")), writes=[bhB[k]])
        else:
            P.dma("sp", lambda e, k=k, lo=lo: e.dma_start(out=hB[k][:], in_=hT_d[:, :, lo - 1:lo + 127].rearrange("k p t -> p k t")), writes=[bhB[k]])
        yield
        regs = [(0, 0, 0), (0, 1, 1), (1, 0, 2)]
        for n, (bk, half, _) in enumerate(regs):
            for kc in range(8):
                mm(bank[bk][:, half * 256:(half + 1) * 256], hA[k][:, kc, :], wrkv[:, n, kc, :], kc == 0, False, [bhA[k], bwrkv], [bb[bk][half]])
                mm(bank[bk][:, half * 256:(half + 1) * 256], hB[k][:, kc, :], wrkvB[:, n, kc, :], False, kc == 7, [bhB[k], bwrkvB], [bb[bk][half]])
        yield
        for (o_ap, c0, c1, reg) in [(bank[3][0:64, 0:128], 0, 64, 0), (bank[3][0:64, 128:256], 64, 128, 1),
                                     (bank[3][:, 256:384], 128, 256, 2), (bank[3][0:32, 384:512], 256, 288, 3)]:
            for kc in range(8):
                mm(o_ap, wl1[:, kc, c0:c1], hA[k][:, kc, :], kc == 0, False, [bwl1, bhA[k]], [bb[3][reg]])
                mm(o_ap, wl1B[:, kc, c0:c1], hB[k][:, kc, :], False, kc == 7, [bwl1B, bhB[k]], [bb[3][reg]])
        act(lambda e: e.activation(out=l1[0:64, 0, :], in_=bank[3][0:64, 0:128], func=AF.Tanh), [bb[3][0]], [bl1[0]])
        act(lambda e: e.activation(out=l1[0:64, 1, :], in_=bank[3][0:64, 128:256], func=AF.Identity), [bb[3][1]], [bl1[1]])
        act(lambda e: e.activation(out=l1[:, 2, :], in_=bank[3][:, 256:384], func=AF.Sigmoid), [bb[3][2]], [bl1[2]])
        act(lambda e: e.activation(out=l1[0:32, 3, :], in_=bank[3][0:32, 384:512], func=AF.Sigmoid), [bb[3][3]], [bl1[3]])
        yield
        act(lambda e: e.activation(out=T("r")[:], in_=bank[0][:, 0:256], func=AF.Identity), [bb[0][0]], [B("r")])
        act(lambda e: e.activation(out=T("kraw")[:], in_=bank[0][:, 256:512], func=AF.Identity), [bb[0][1]], [B("kraw")])
        act(lambda e: e.activation(out=T("v")[:], in_=bank[1][:, 0:256], func=AF.Identity), [bb[1][0]], [B("v")])
        yield
        mm(bank[1][:, 256:512], l1[0:64, 0, :], wl2[0:64, 0, :], True, True, [bl1[0], bwl2], [bb[1][1]])
        mm(bank[2][:, 0:256], l1[0:64, 1, :], wl2[0:64, 1, :], True, True, [bl1[1], bwl2], [bb[2][0]])
        mm(bank[2][:, 256:512], l1[:, 2, :], wl2[:, 2, :], True, False, [bl1[2], bwl2], [bb[2][1]])
        mm(bank[2][:, 256:512], l1[0:32, 3, :], wl2[0:32, 3, :], False, True, [bl1[3], bwl2], [bb[2][1]])
        dve(lambda e: e.tensor_tensor(out=T("lw")[:], in0=bank[1][:, 256:512], in1=rows[:, 0, :], op=ALU.add), [bb[1][1], brows], [B("lw")])
        act(lambda e: e.activation(out=T("lw")[:], in_=T("lw")[:], func=AF.Sigmoid), [B("lw")], [B("lw")])
        dve(lambda e: e.tensor_scalar(out=T("lw")[:], in0=T("lw")[:], scalar1=NEG, scalar2=None, op0=ALU.mult), [B("lw")], [B("lw")])
        dve(lambda e: e.tensor_tensor(out=T("a")[:], in0=bank[2][:, 0:256], in1=rows[:, 1, :], op=ALU.add), [bb[2][0], brows], [B("a")])
        act(lambda e: e.activation(out=T("a")[:], in_=T("a")[:], func=AF.Sigmoid), [B("a")], [B("a")])
        act(lambda e: e.activation(out=T("gg")[:], in_=bank[2][:, 256:512], func=AF.Identity), [bb[2][1]], [B("gg")])
        yield
        dve(lambda e: e.tensor_tensor(out=T("kk")[:], in0=T("kraw")[:], in1=rows[:, 2, :], op=ALU.mult), [B("kraw"), brows], [B("kk")])
        dve(lambda e: e.tensor_tensor(out=T("t0")[:], in0=T("kk")[:], in1=T("kk")[:], op=ALU.mult), [B("kk")], [B("t0")])
        dve(lambda e: e.tensor_reduce(out=T("ss")[:], in_=T("t0")[:].rearrange("p (h n) -> p h n", h=4), axis=AX.X, op=ALU.add), [B("t0")], [B("ss")])
        act(lambda e: e.activation(out=T("rn")[:], in_=T("ss")[:], func=AF.Sqrt), [B("ss")], [B("rn")])
        dve(lambda e: e.tensor_scalar(out=T("rn")[:], in0=T("rn")[:], scalar1=1e-12, scalar2=None, op0=ALU.max), [B("rn")], [B("rn")])
        dve(lambda e: e.reciprocal(out=T("rn")[:], in_=T("rn")[:]), [B("rn")], [B("rn")])
        for hh in range(4):
            sl = slice(hh * 64, (hh + 1) * 64)
            dve(lambda e, hh=hh, sl=sl: e.tensor_scalar(out=T("kk")[:, sl], in0=T("kk")[:, sl], scalar1=T("rn")[:, hh:hh + 1], scalar2=None, op0=ALU.mult),
                [B("kk"), B("rn")], [B("kk")])
        dve(lambda e: e.scalar_tensor_tensor(out=T("t1")[:], in0=T("a")[:], scalar=-1.0, in1=rows[:, 3, :], op0=ALU.add, op1=ALU.mult), [B("a"), brows], [B("t1")])
        dve(lambda e: e.scalar_tensor_tensor(out=T("kmod")[:], in0=T("t1")[:], scalar=1.0, in1=T("kraw")[:], op0=ALU.add, op1=ALU.mult), [B("t1"), B("kraw")], [B("kmod")])
        dve(lambda e: e.tensor_tensor(out=T("kb")[:], in0=T("kk")[:], in1=T("a")[:], op=ALU.mult), [B("kk"), B("a")], [B("kb")])
        yield
        dve(lambda e: e.tensor_tensor(out=T("t0")[:], in0=T("r")[:], in1=T("kmod")[:], op=ALU.mult), [B("r"), B("kmod")], [B("t0")])
        dve(lambda e: e.tensor_tensor(out=T("t0")[:], in0=T("t0")[:], in1=rows[:, 4, :], op=ALU.mult), [B("t0"), brows], [B("t0")])
        dve(lambda e: e.tensor_reduce(out=T("bc")[:], in_=T("t0")[:].rearrange("p (h n) -> p h n", h=4), axis=AX.X, op=ALU.add), [B("t0")], [B("bc")])
        yield
        mm(bank[0][:, 0:256], TriT, T("lw")[:], True, True, [bcm, B("lw")], [bb[0][0]])
        mm(bank[0][:, 256:512], ones, T("lw")[:], True, True, [bcm, B("lw")], [bb[0][1]])
        act(lambda e: e.activation(out=T("cs")[:], in_=bank[0][:, 0:256], func=AF.Identity), [bb[0][0]], [B("cs")])
        act(lambda e: e.activation(out=T("e2")[:], in_=T("cs")[:], func=AF.Exp), [B("cs")], [B("e2")])
        act(lambda e: e.activation(out=T("e3")[:], in_=T("cs")[:], func=AF.Exp, scale=-1.0), [B("cs")], [B("e3")])
        dve(lambda e: e.tensor_tensor(out=T("e1")[:], in0=T("cs")[:], in1=T("lw")[:], op=ALU.subtract), [B("cs"), B("lw")], [B("e1")])
        act(lambda e: e.activation(out=T("e1")[:], in_=T("e1")[:], func=AF.Exp), [B("e1")], [B("e1")])
        dve(lambda e: e.tensor_tensor(out=T("e4")[:], in0=bank[0][:, 256:512], in1=T("cs")[:], op=ALU.subtract), [bb[0][1], B("cs")], [B("e4")])
        act(lambda e: e.activation(out=T("e4")[:], in_=T("e4")[:], func=AF.Exp), [B("e4")], [B("e4")])
        dve(lambda e: e.scalar_tensor_tensor(out=T("kat")[:], in0=T("kk")[:], scalar=-1.0, in1=T("e1")[:], op0=ALU.mult, op1=ALU.mult), [B("kk"), B("e1")], [B("kat")])
        dve(lambda e: e.tensor_tensor(out=T("rt")[:], in0=T("r")[:], in1=T("e2")[:], op=ALU.mult), [B("r"), B("e2")], [B("rt")])
        dve(lambda e: e.tensor_tensor(out=T("kbh")[:], in0=T("kb")[:], in1=T("e3")[:], op=ALU.mult), [B("kb"), B("e3")], [B("kbh")])
        dve(lambda e: e.tensor_tensor(out=T("kh")[:], in0=T("kmod")[:], in1=T("e3")[:], op=ALU.mult), [B("kmod"), B("e3")], [B("kh")])
        dve(lambda e: e.tensor_tensor(out=T("kg")[:], in0=T("kmod")[:], in1=T("e4")[:], op=ALU.mult), [B("kmod"), B("e4")], [B("kg")])
        dve(lambda e: e.tensor_tensor(out=T("kbg")[:], in0=T("kb")[:], in1=T("e4")[:], op=ALU.mult), [B("kb"), B("e4")], [B("kbg")])
        yield
        for hh in range(4):
            mm(bank[1][0:64, 2 * hh:2 + 2 * hh], T("lw")[:, hh * 64:(hh + 1) * 64], ones[:, 0:2], True, True, [B("lw"), bcm], [bb[1][3]])
        act(lambda e: e.activation(out=T("gcol")[:], in_=bank[1][0:64, 0:8].rearrange("p (h t) -> p h t", t=2)[:, :, 0], func=AF.Exp), [bb[1][3]], [B("gcol")])

    def stageB(c):
        k = c % 2
        lo = c * 128
        par = c % 2
        def T(name): return F[name][0][par]
        def B(name): return F[name][1][par]
        featT = featT2[par]; bfeat = bfeat2[par]
        HS = range(4)
        sls = [slice(hh * 64, (hh + 1) * 64) for hh in HS]
        yield
        for hh in HS:
            tb = 2 + hh % 2
            for xi, nm in enumerate(["kat", "rt", "kbh", "kh"]):
                P.op("pe", lambda e, xi=xi, nm=nm, tb=tb, hh=hh: e.transpose(bank[tb][0:64, xi * 128:(xi + 1) * 128], T(nm)[:, sls[hh]], ident),
                     reads=[B(nm), bcm], writes=[bb[tb][0]], nosync_same=True)
            act(lambda e, hh=hh, tb=tb: e.activation(out=featT[hh][:].rearrange("p a b -> p (a b)"), in_=bank[tb][0:64, :], func=AF.Identity), [bb[tb][0]], [bfeat[hh]])
        kaT = [featT[hh][:, 0, :] for hh in HS]; rT = [featT[hh][:, 1, :] for hh in HS]
        kbT = [featT[hh][:, 2, :] for hh in HS]; khT = [featT[hh][:, 3, :] for hh in HS]
        karT = [featT[hh][:, 0:2, :].rearrange("p a b -> p (a b)") for hh in HS]
        SB = [bank[4 + hh] for hh in HS]; bSB = [bb[4 + hh][0] for hh in HS]
        yield
        for hh in HS:
            mm(SB[hh][:, 0:256], kbT[hh], karT[hh], True, True, [bfeat[hh]], [bSB[hh]])
        yield
        for hh in HS:
            dve(lambda e, hh=hh: e.tensor_tensor(out=A_[hh][0][:], in0=SB[hh][:, 0:128], in1=mST, op=ALU.mult), [bSB[hh], bcm], [bA[hh][0]])
            dve(lambda e, hh=hh: e.tensor_tensor(out=MbT[hh][:], in0=SB[hh][:, 128:256], in1=TriT, op=ALU.mult), [bSB[hh], bcm], [bMbT[hh]])
        yield
        for hh in HS:
            mm(SB[hh][:, 0:256], khT[hh], karT[hh], True, True, [bfeat[hh]], [bSB[hh]])
        yield
        for hh in HS:
            dve(lambda e, hh=hh: e.tensor_tensor(out=LkT[hh][:], in0=SB[hh][:, 0:128], in1=mST, op=ALU.mult), [bSB[hh], bcm], [bLkT[hh]])
            dve(lambda e, hh=hh: e.tensor_tensor(out=MkT[hh][:], in0=SB[hh][:, 128:256], in1=TriT, op=ALU.mult), [bSB[hh], bcm], [bMkT[hh]])
        yield
        for hh in HS:
            mm(SB[hh][:, 0:128], kaT[hh], kbT[hh], True, True, [bfeat[hh]], [bSB[hh]])
        yield
        for hh in HS:
            dve(lambda e, hh=hh: e.tensor_tensor(out=Q_[hh][0][:], in0=SB[hh][:, 0:128], in1=mL, op=ALU.mult), [bSB[hh], bcm], [bQ[hh][0]])
            dve(lambda e, hh=hh: e.tensor_tensor(out=Tt_[hh][0][:], in0=A_[hh][0][:], in1=ident, op=ALU.add), [bA[hh][0], bcm], [bTt[hh][0]])
        cur = 0
        for s_ in range(1, 7):
            nx = 1 - cur
            yield
            for hh in HS:
                mm(SB[hh][:, 0:128], Q_[hh][cur][:], A_[hh][cur][:], True, True, [bQ[hh][cur], bA[hh][cur]], [bSB[hh]])
            yield
            for hh in HS:
                act(lambda e, hh=hh, nx=nx: e.activation(out=A_[hh][nx][:], in_=SB[hh][:, 0:128], func=AF.Identity), [bSB[hh]], [bA[hh][nx]])
            yield
            for hh in HS:
                mm(SB[hh][:, 128:256], A_[hh][cur][:], Q_[hh][cur][:], True, True, [bQ[hh][cur], bA[hh][cur]], [bSB[hh]])
            yield
            for hh in HS:
                act(lambda e, hh=hh, nx=nx: e.activation(out=Q_[hh][nx][:], in_=SB[hh][:, 128:256], func=AF.Identity), [bSB[hh]], [bQ[hh][nx]])
            yield
            for hh in HS:
                mm(SB[hh][:, 256:384], Q_[hh][nx][:], Tt_[hh][cur][:], True, True, [bQ[hh][nx], bTt[hh][cur]], [bSB[hh]])
            yield
            for hh in HS:
                dve(lambda e, hh=hh, nx=nx, cur=cur: e.tensor_tensor(out=Tt_[hh][nx][:], in0=SB[hh][:, 256:384], in1=Tt_[hh][cur][:], op=ALU.add),
                    [bSB[hh], bTt[hh][cur]], [bTt[hh][nx]])
            cur = nx
        Vh = [T("v")[:, sls[hh]] for hh in HS]
        yield
        for hh in HS:
            mm(SB[hh][:, 0:64], LkT[hh][:], Vh[hh], True, False, [bLkT[hh], B("v")], [bSB[hh]])
            mm(SB[hh][:, 0:64], kaT[hh], H[:, hh, :], False, True, [bfeat[hh], bH[hh]], [bSB[hh]])
        yield
        for hh in HS:
            act(lambda e, hh=hh: e.activation(out=W1s[hh][:], in_=SB[hh][:, 0:64], func=AF.Identity), [bSB[hh]], [bW1s[hh]])
        yield
        for hh in HS:
            mm(SB[hh][:, 64:128], Tt_[hh][cur][:], W1s[hh][:], True, True, [bTt[hh][cur], bW1s[hh]], [bSB[hh]])
        yield
        for hh in HS:
            act(lambda e, hh=hh: e.activation(out=Us[hh][:], in_=SB[hh][:, 64:128], func=AF.Identity), [bSB[hh]], [bUs[hh]])
        yield
        for hh in HS:
            mm(SB[hh][:, 128:192], rT[hh], H[:, hh, :], True, False, [bfeat[hh], bH[hh]], [bSB[hh]])
            mm(SB[hh][:, 128:192], MkT[hh][:], Vh[hh], False, False, [bMkT[hh], B("v")], [bSB[hh]])
            mm(SB[hh][:, 128:192], MbT[hh][:], Us[hh][:], False, True, [bMbT[hh], bUs[hh]], [bSB[hh]])
        yield
        for hh in HS:
            act(lambda e, hh=hh: e.activation(out=T("y")[:, sls[hh]], in_=SB[hh][:, 128:192], func=AF.Identity), [bSB[hh]], [B("y")])
        yield
        for hh in HS:
            mm(SB[hh][0:64, 192:256], T("kg")[:, sls[hh]], Vh[hh], True, False, [B("kg"), B("v")], [bSB[hh]])
            mm(SB[hh][0:64, 192:256], T("kbg")[:, sls[hh]], Us[hh][:], False, True, [B("kbg"), bUs[hh]], [bSB[hh]])
        yield
        for hh in HS:
            dve(lambda e, hh=hh: e.scalar_tensor_tensor(out=H[:, hh, :], in0=H[:, hh, :], scalar=T("gcol")[:, hh:hh + 1], in1=SB[hh][0:64, 192:256],
                                                        op0=ALU.mult, op1=ALU.add), [bH[hh], B("gcol"), bSB[hh]], [bH[hh]])
        yield
        v3 = lambda nm: T(nm)[:].rearrange("p (h n) -> p h n", h=4)
        dve(lambda e: e.tensor_reduce(out=T("mean")[:], in_=v3("y"), axis=AX.X, op=ALU.add), [B("y")], [B("mean")])
        dve(lambda e: e.tensor_scalar(out=T("mean")[:], in0=T("mean")[:], scalar1=1.0 / 64, scalar2=None, op0=ALU.mult), [B("mean")], [B("mean")])
        for hh in range(4):
            sl = slice(hh * 64, (hh + 1) * 64)
            dve(lambda e, hh=hh, sl=sl: e.tensor_scalar(out=T("cen")[:, sl], in0=T("y")[:, sl], scalar1=T("mean")[:, hh:hh + 1], scalar2=None, op0=ALU.subtract),
                [B("y"), B("mean")], [B("cen")])
        dve(lambda e: e.tensor_tensor(out=T("t0")[:], in0=T("cen")[:], in1=T("cen")[:], op=ALU.mult), [B("cen")], [B("t0")])
        dve(lambda e: e.tensor_reduce(out=T("var")[:], in_=v3("t0"), axis=AX.X, op=ALU.add), [B("t0")], [B("var")])
        act(lambda e: e.activation(out=T("rstd")[:], in_=T("var")[:], func=AF.Sqrt, bias=64e-5, scale=1.0 / 64), [B("var")], [B("rstd")])
        dve(lambda e: e.reciprocal(out=T("rstd")[:], in_=T("rstd")[:]), [B("rstd")], [B("rstd")])
        for hh in range(4):
            sl = slice(hh * 64, (hh + 1) * 64)
            dve(lambda e, hh=hh, sl=sl: e.scalar_tensor_tensor(out=T("yn")[:, sl], in0=T("cen")[:, sl], scalar=T("rstd")[:, hh:hh + 1], in1=rows[:, 5, sl],
                                                               op0=ALU.mult, op1=ALU.mult), [B("cen"), B("rstd"), brows], [B("yn")])
        dve(lambda e: e.tensor_tensor(out=T("yn")[:], in0=T("yn")[:], in1=rows[:, 6, :], op=ALU.add), [B("yn"), brows], [B("yn")])
        for hh in range(4):
            sl = slice(hh * 64, (hh + 1) * 64)
            dve(lambda e, hh=hh, sl=sl: e.scalar_tensor_tensor(out=T("yn")[:, sl], in0=T("v")[:, sl], scalar=T("bc")[:, hh:hh + 1], in1=T("yn")[:, sl],
                                                               op0=ALU.mult, op1=ALU.add), [B("v"), B("bc"), B("yn")], [B("yn")])
        dve(lambda e: e.tensor_tensor(out=T("z")[:], in0=T("yn")[:], in1=T("gg")[:], op=ALU.mult), [B("yn"), B("gg")], [B("z")])
        bo = P.buf()
        P.dma("sp", lambda e, lo=lo: e.dma_start(out=o_d[lo:lo + 128, :], in_=T("z")[:]), reads=[B("z")], writes=[bo])
        OUTS.append(bo)
        yield
    OUTS = []
    def drain(g):
        for _ in g:
            pass
    drain(stageA(0))
    for c in range(NCH):
        gB = stageB(c)
        gA = stageA(c + 1) if c + 1 < NCH else None
        while gB is not None or gA is not None:
            if gB is not None:
                try:
                    next(gB)
                except StopIteration:
                    gB = None
            if gA is not None:
                try:
                    next(gA)
                except StopIteration:
                    gA = None
    print('rwkv sbuf remaining', nc.sbuf_bytes_remaining, {e: len(v) for e, v in P.streams.items()})
    P.final_wait("sp", OUTS)
    P.emit(); P.close()
    return nc


def rwkv_host_inputs(inp, hT_full, g):
    cs = slice(g * 256, (g + 1) * 256)
    def kp(w): return np.ascontiguousarray(w.reshape(8, 128, -1).transpose(1, 0, 2))
    mu = np.ascontiguousarray(inp["rwkv_mu"][0].reshape(6, 8, 128).transpose(2, 0, 1))
    wrkv = np.stack([kp(inp["rwkv_w_rkv"][0, n][:, cs]) for n in range(3)], axis=1)
    wl1 = kp(np.concatenate([inp["rwkv_wd1"][0], inp["rwkv_wa1"][0], inp["rwkv_wg1"][0]], axis=1))
    wl2 = np.zeros((128, 4, 256), np.float32)
    wl2[0:64, 0] = inp["rwkv_wd2"][0][:, cs]
    wl2[0:64, 1] = inp["rwkv_wa2"][0][:, cs]
    wl2[:, 2] = inp["rwkv_wg2"][0][0:128, cs]
    wl2[0:32, 3] = inp["rwkv_wg2"][0][128:160, cs]
    rws = np.stack([inp["rwkv_w0"][0][cs], inp["rwkv_a0"][0][cs], inp["rwkv_k_k"][0][cs], inp["rwkv_k_a"][0][cs],
                    inp["rwkv_r_k"][0].reshape(-1)[cs], inp["rwkv_gn_w"][0][cs], inp["rwkv_gn_b"][0][cs]], axis=0)
    rows = np.ascontiguousarray(np.broadcast_to(rws[None], (128, 7, 256))).astype(np.float32)
    i = np.arange(128)
    TriT = (i[:, None] <= i[None, :]).astype(np.float32)
    mST = (i[None, :] > i[:, None]).astype(np.float32)
    mL = (i[None, :] < i[:, None]).astype(np.float32)
    cm = np.stack([TriT, np.ones((128, 128), np.float32), mST, mL, np.eye(128, dtype=np.float32)], axis=1)
    return {"hT": hT_full, "mu": mu, "wrkv": np.ascontiguousarray(wrkv), "wl1": wl1, "wl2": wl2, "rows": rows, "cm": np.ascontiguousarray(cm)}


S = 8192
NQT = 16


def build_ret():
    nc = bass.Bass("TRN2", target_bir_lowering=False)
    P = Prog(nc)
    dt = nc.dram_tensor
    hT_d = dt("hT", [8, 128, S], BF16, kind="ExternalInput").ap()
    wqk_d = dt("wqk", [128, 8, 512], F32, kind="ExternalInput").ap()
    wv_d = dt("wv", [128, 8, 512], F32, kind="ExternalInput").ap()
    wg_d = dt("wg", [128, 8, 512], F32, kind="ExternalInput").ap()
    cs_d = dt("cs", [128, 2, S], F32, kind="ExternalInput").ap()
    dec_d = dt("dec", [128, 5, 512], F32, kind="ExternalInput").ap()
    gn_d = dt("gn", [128, 2, 512], F32, kind="ExternalInput").ap()
    gpow_d = dt("gpow", [128, 64], F32, kind="ExternalInput").ap()
    z_d = dt("z", [S, 512], BF16, kind="ExternalOutput").ap()
    DBG = None
    if DBG:
        qk_d = dt("qk_dbg", [128, 4, S], BF16, kind="ExternalOutput").ap()
        v_d = dt("v_dbg", [128, 64, 512], BF16, kind="ExternalOutput").ap()

    wqk = P.sbuf("wqk_sb", [128, 8, 512], BF16); bwqk = P.buf()
    wv = P.sbuf("wv_sb", [128, 8, 512], BF16); bwv = P.buf()
    wg = P.sbuf("wg_sb", [128, 8, 512], BF16); bwg = P.buf()
    dec = P.sbuf("dec_sb", [128, 5, 512], F32); bdec = P.buf()
    gn = P.sbuf("gn_sb", [128, 2, 512], F32); bgn = P.buf()
    gpow = P.sbuf("gpow_sb", [128, 64], F32); bgpow = P.buf()
    qT = P.sbuf("qT", [128, 2, S], BF16); bqT = [P.buf() for _ in range(NQT)]
    kT = P.sbuf("kT", [128, 2, S], BF16); bkT = [P.buf() for _ in range(NQT)]
    v = P.sbuf("v", [128, 64, 512], BF16); bv = [P.buf() for _ in range(64)]
    ht = P.sbuf("ht", [128, 8, 512], BF16); bht = [P.buf() for _ in range(8)]
    cst = P.sbuf("cst", [128, 2, 512], F32); bcst = P.buf()
    x1 = P.sbuf("x1", [128, 512], F32); bx1 = P.buf()
    ta = P.sbuf("ta", [128, 512], F32); bta = P.buf()
    tb = P.sbuf("tb", [128, 512], F32); btb = P.buf()
    pt = [P.sbuf(f"pt{i}", [128, 512], BF16) for i in range(3)]; bpt = [P.buf() for _ in range(3)]
    cen = P.sbuf("cen", [128, 512], F32); bcen = P.buf()
    sgt = P.sbuf("sgt", [128, 512], F32); bsgt = P.buf()
    zt = [P.sbuf(f"zt{i}", [128, 512], BF16) for i in range(2)]; bzt = [P.buf() for _ in range(2)]
    st = P.sbuf("st", [128, 4], F32); bst = P.buf()
    pl = [P.psum(f"pl{i}", [128, 512]) for i in range(2)]; bpl = [P.buf(excl=True) for _ in range(2)]
    ps = [P.psum(f"ps{i}", [128, 512]) for i in range(2)]; bps = [P.buf(excl=True) for _ in range(2)]
    po = [P.psum(f"po{i}", [128, 512]) for i in range(4)]; bpo = [P.buf(excl=True) for _ in range(4)]
    cnt = {"pl": 0, "ps": 0, "pt": 0, "z": 0}

    P.dma("pool", lambda e: e.dma_start(out=wqk[:], in_=wqk_d), writes=[bwqk])
    P.dma("pool", lambda e: e.dma_start(out=wv[:], in_=wv_d), writes=[bwv])
    P.dma("pool", lambda e: e.dma_start(out=wg[:], in_=wg_d), writes=[bwg])
    P.dma("sp", lambda e: e.dma_start(out=dec[:], in_=dec_d), writes=[bdec])
    P.dma("sp", lambda e: e.dma_start(out=gn[:], in_=gn_d), writes=[bgn])
    P.dma("sp", lambda e: e.dma_start(out=gpow[:], in_=gpow_d), writes=[bgpow])

    def load_ht(t):
        for kc in range(8):
            P.dma("sp", lambda e, kc=kc, t=t: e.dma_start(out=ht[:, kc, :], in_=hT_d[kc, :, t * 512:(t + 1) * 512]), writes=[bht[kc]])

    for t in range(NQT):
        tsl = slice(t * 512, (t + 1) * 512)
        load_ht(t)
        P.dma("sp", lambda e, tsl=tsl: e.dma_start(out=cst[:], in_=cs_d[:, :, tsl]), writes=[bcst])
        for which, (dst, bdst) in enumerate([(qT, bqT), (kT, bkT)]):
            qa = cnt["pl"] % 2; cnt["pl"] += 1
            for kc in range(8):
                P.op("pe", lambda e, kc=kc, qa=qa, which=which: e.matmul(pl[qa][:], lhsT=wqk[:, kc, which * 256:which * 256 + 128], rhs=ht[:, kc, :],
                                                                         start=(kc == 0), stop=(kc == 7)), reads=[bwqk, bht[kc]], writes=[bpl[qa]], nosync_same=True)
            P.op("act", lambda e, qa=qa: e.activation(out=x1[:], in_=pl[qa][:], func=AF.Identity), reads=[bpl[qa]], writes=[bx1])
            qb = cnt["pl"] % 2; cnt["pl"] += 1
            for kc in range(8):
                P.op("pe", lambda e, kc=kc, qb=qb, which=which: e.matmul(pl[qb][:], lhsT=wqk[:, kc, which * 256 + 128:which * 256 + 256], rhs=ht[:, kc, :],
                                                                         start=(kc == 0), stop=(kc == 7)), reads=[bwqk, bht[kc]], writes=[bpl[qb]], nosync_same=True)
            P.op("dve", lambda e: e.tensor_tensor(out=ta[:], in0=x1[:], in1=cst[:, 0, :], op=ALU.mult), reads=[bx1, bcst], writes=[bta])
            P.op("dve", lambda e, qb=qb: e.tensor_tensor(out=tb[:], in0=pl[qb][:], in1=cst[:, 1, :], op=ALU.mult), reads=[bpl[qb], bcst], writes=[btb])
            P.op("dve", lambda e, dst=dst, tsl=tsl: e.tensor_tensor(out=dst[:, 0, tsl], in0=ta[:], in1=tb[:], op=ALU.subtract), reads=[bta, btb], writes=[bdst[t]])
            P.op("dve", lambda e: e.tensor_tensor(out=ta[:], in0=x1[:], in1=cst[:, 1, :], op=ALU.mult), reads=[bx1, bcst], writes=[bta])
            P.op("dve", lambda e, qb=qb: e.tensor_tensor(out=tb[:], in0=pl[qb][:], in1=cst[:, 0, :], op=ALU.mult), reads=[bpl[qb], bcst], writes=[btb])
            P.op("dve", lambda e, dst=dst, tsl=tsl: e.tensor_tensor(out=dst[:, 1, tsl], in0=ta[:], in1=tb[:], op=ALU.add), reads=[bta, btb], writes=[bdst[t]])
        for tb4 in range(4):
            blk = t * 4 + tb4
            qa = cnt["pl"] % 2; cnt["pl"] += 1
            for kc in range(8):
                P.op("pe", lambda e, kc=kc, qa=qa, tb4=tb4: e.matmul(pl[qa][:], lhsT=ht[:, kc, tb4 * 128:(tb4 + 1) * 128], rhs=wv[:, kc, :],
                                                                     start=(kc == 0), stop=(kc == 7)), reads=[bwv, bht[kc]], writes=[bpl[qa]], nosync_same=True)
            P.op("act", lambda e, qa=qa, blk=blk: e.activation(out=v[:, blk, :], in_=pl[qa][:], func=AF.Identity), reads=[bpl[qa]], writes=[bv[blk]])

    outs = []
    if DBG:
        for (src, bsrc, o0) in [(qT, bqT, 0), (kT, bkT, 2)]:
            for c in range(2):
                b = P.buf(); outs.append(b)
                P.dma('sp', lambda e, src=src, c=c, o0=o0: e.dma_start(out=qk_d[:, o0 + c, :], in_=src[:, c, :]), reads=bsrc, writes=[b])
        b = P.buf(); outs.append(b)
        P.dma('sp', lambda e: e.dma_start(out=v_d, in_=v[:]), reads=bv, writes=[b])
    for j in range(NQT):
        tsl = slice(j * 512, (j + 1) * 512)
        load_ht(j)
        nkb = 4 * j + 4

        def emit_S(kb):
            sq = cnt["ps"] % 2; cnt["ps"] += 1
            for c in range(2):
                P.op("pe", lambda e, c=c, sq=sq, kb=kb, tsl=tsl: e.matmul(ps[sq][:], lhsT=kT[:, c, kb * 128:(kb + 1) * 128], rhs=qT[:, c, tsl],
                                                                 start=(c == 0), stop=(c == 1)), reads=[bkT[kb // 4], bqT[j]], writes=[bps[sq]], nosync_same=True)
            return sq
        cur = emit_S(0)
        for kb in range(nkb):
            nxt = emit_S(kb + 1) if kb + 1 < nkb else None
            d = kb - 4 * j
            pq = cnt["pt"] % 3; cnt["pt"] += 1
            if d >= 0:
                P.op("dve", lambda e, cur=cur, pq=pq, d=d: e.tensor_tensor(out=pt[pq][:], in0=ps[cur][:], in1=dec[:, 1 + d, :], op=ALU.mult),
                     reads=[bps[cur], bdec], writes=[bpt[pq]])
            else:
                i = (512 * j - 128 * kb) // 128
                P.op("dve", lambda e, cur=cur, pq=pq, i=i: e.scalar_tensor_tensor(out=pt[pq][:], in0=ps[cur][:], scalar=gpow[:, i:i + 1], in1=dec[:, 0, :],
                                                                                  op0=ALU.mult, op1=ALU.mult), reads=[bps[cur], bdec, bgpow], writes=[bpt[pq]])
            for qs in range(4):
                if d > qs:
                    continue
                last = 4 * j + qs
                P.op("pe", lambda e, pq=pq, qs=qs, kb=kb, last=last: e.matmul(po[qs][:], lhsT=pt[pq][:, qs * 128:(qs + 1) * 128], rhs=v[:, kb, :],
                                                                              start=(kb == 0), stop=(kb == last)), reads=[bpt[pq], bv[kb]], writes=[bpo[qs]], nosync_same=True)
            cur = nxt
        for qs in range(4):
            qa = cnt["pl"] % 2; cnt["pl"] += 1
            for kc in range(8):
                P.op("pe", lambda e, kc=kc, qa=qa, qs=qs: e.matmul(pl[qa][:], lhsT=ht[:, kc, qs * 128:(qs + 1) * 128], rhs=wg[:, kc, :],
                                                                   start=(kc == 0), stop=(kc == 7)), reads=[bwg, bht[kc]], writes=[bpl[qa]], nosync_same=True)
            P.op("act", lambda e, qa=qa: e.activation(out=sgt[:], in_=pl[qa][:], func=AF.Silu), reads=[bpl[qa]], writes=[bsgt])
            P.op("dve", lambda e, qs=qs: e.tensor_reduce(out=st[:, 0:1], in_=po[qs][:], axis=AX.X, op=ALU.add), reads=[bpo[qs]], writes=[bst])
            P.op("dve", lambda e: e.tensor_scalar(out=st[:, 1:2], in0=st[:, 0:1], scalar1=-1.0 / 512, scalar2=None, op0=ALU.mult), reads=[bst], writes=[bst])
            P.op("act", lambda e, qs=qs: e.activation(out=cen[:], in_=po[qs][:], func=AF.Identity, bias=st[:, 1:2]), reads=[bpo[qs], bst], writes=[bcen])
            P.op("dve", lambda e: e.tensor_tensor(out=ta[:], in0=cen[:], in1=cen[:], op=ALU.mult), reads=[bcen], writes=[bta])
            P.op("dve", lambda e: e.tensor_reduce(out=st[:, 2:3], in_=ta[:], axis=AX.X, op=ALU.add), reads=[bta], writes=[bst])
            P.op("act", lambda e: e.activation(out=st[:, 3:4], in_=st[:, 2:3], func=AF.Sqrt, bias=1e-5, scale=1.0 / 512), reads=[bst], writes=[bst])
            P.op("dve", lambda e: e.reciprocal(out=st[:, 3:4], in_=st[:, 3:4]), reads=[bst], writes=[bst])
            P.op("dve", lambda e: e.scalar_tensor_tensor(out=tb[:], in0=cen[:], scalar=st[:, 3:4], in1=gn[:, 0, :], op0=ALU.mult, op1=ALU.mult),
                 reads=[bcen, bst, bgn], writes=[btb])
            P.op("dve", lambda e: e.tensor_tensor(out=tb[:], in0=tb[:], in1=gn[:, 1, :], op=ALU.add), reads=[btb, bgn], writes=[btb])
            zi = cnt["z"] % 2; cnt["z"] += 1
            P.op("dve", lambda e, zi=zi: e.tensor_tensor(out=zt[zi][:], in0=tb[:], in1=sgt[:], op=ALU.mult), reads=[btb, bsgt], writes=[bzt[zi]])
            b = P.buf(); outs.append(b)
            r0 = j * 512 + qs * 128
            P.dma("sp", lambda e, zi=zi, r0=r0: e.dma_start(out=z_d[r0:r0 + 128, :], in_=zt[zi][:]), reads=[bzt[zi]], writes=[b])
    print('ret sbuf remaining', nc.sbuf_bytes_remaining, {e: len(v_) for e, v_ in P.streams.items()})
    P.final_wait("sp", outs)
    P.emit(); P.close()
    return nc


def ret_host_inputs(inp, hT_full, h):
    D = 1024
    w = inp["ret_w_in"][0]
    def kp(a): return np.ascontiguousarray(a.reshape(8, 128, -1).transpose(1, 0, 2))
    wq = w[:, h * 256:(h + 1) * 256]; wk = w[:, D + h * 256:D + (h + 1) * 256]
    wv = w[:, 2 * D + h * 512:2 * D + (h + 1) * 512]; wg = w[:, 4 * D + h * 512:4 * D + (h + 1) * 512]
    inv = 10000.0 ** (-np.arange(0, 256, 2, dtype=np.float32) / 256)
    ang = np.arange(S, dtype=np.float32)[:, None] * inv[None, :]
    cs = np.ascontiguousarray(np.stack([np.cos(ang).T, np.sin(ang).T], axis=1)).astype(np.float32)
    lg = np.log1p(-np.exp2(-5.0 - h))
    p = np.arange(128, dtype=np.float64)[:, None]; f = np.arange(512, dtype=np.float64)[None, :]
    dec = np.zeros((128, 5, 512), np.float64)
    dec[:, 0] = np.exp(lg * (f - p))
    for d in range(4):
        e_ = f - p - 128 * d
        dec[:, 1 + d] = np.where(e_ >= 0, np.exp(lg * np.maximum(e_, 0)), 0.0)
    dec *= 256 ** -0.5
    gpow = np.broadcast_to(np.exp(lg * 128.0 * np.arange(64, dtype=np.float64))[None, :], (128, 64))
    gn = np.stack([np.broadcast_to(inp["ret_gn_w"][0][h * 512:(h + 1) * 512][None], (128, 512)),
                   np.broadcast_to(inp["ret_gn_b"][0][h * 512:(h + 1) * 512][None], (128, 512))], axis=1)
    return {"hT": hT_full, "wqk": kp(np.concatenate([wq, wk], axis=1)), "wv": kp(wv), "wg": kp(wg), "cs": cs,
            "dec": dec.astype(np.float32), "gn": np.ascontiguousarray(gn).astype(np.float32),
            "gpow": np.ascontiguousarray(gpow).astype(np.float32)}


S = 8192


def build_moba():
    nc = bass.Bass("TRN2", target_bir_lowering=False)
    P = Prog(nc)
    dt = nc.dram_tensor
    hT_d = dt("hT", [8, 128, S], BF16, kind="ExternalInput").ap()
    w_d = dt("wqkv", [128, 8, 768], F32, kind="ExternalInput").ap()
    tab_d = dt("tab", [32, 4], F32, kind="ExternalInput").ap()
    oh_d = dt("oh", [32, 1152 + 128], F32, kind="ExternalInput").ap()
    cm_d = dt("cm", [128, 2, 128], F32, kind="ExternalInput").ap()
    kblk_d = dt("kblk", [96, S], F32, kind="ExternalInput").ap()
    sel_d = dt("selc", [128, 2, 64, 32], F32, kind="ExternalInput").ap()
    mask_d = dt("mask", [128, 4, 512], F32, kind="ExternalInput").ap()
    tv_d = dt("tv_scratch", [4, 1152], F32, kind="Internal").ap()
    o_d = dt("o", [S, 256], BF16, kind="ExternalOutput").ap()

    w = P.sbuf("w_sb", [128, 8, 768], BF16); bw = P.buf()
    tab = P.sbuf("tab_sb", [32, 4], F32); btab = P.buf()
    oh = P.sbuf("oh_sb", [32, 1280], F32); boh = P.buf()
    cm = P.sbuf("cm_sb", [128, 2, 128], F32); bcm = P.buf()
    selc = P.sbuf("selc_sb", [128, 2, 64, 32], F32); bselc = P.buf()
    masks = P.sbuf("masks_sb", [128, 4, 512], BF16); bmask = P.buf()
    tv = P.sbuf("tv_sb", [4, 1152], F32); btv = P.buf()
    b31 = P.sbuf("b31", [128, 4], F32); bb31 = P.buf()
    hk = P.sbuf("hk", [128, 512], F32); bhk = P.buf()
    bt = P.sbuf("bt", [128, 5, 512], F32); bbt = P.buf()
    kT = P.sbuf("kT", [96, S], BF16); bkT = P.buf(); bkrow = P.buf()
    qT = P.sbuf("qT", [96, S], BF16); bqT = [P.buf() for _ in range(NQT)]
    va = P.sbuf("va", [128, 64, 65], BF16); bva = P.buf()
    kmT = P.sbuf("kmT", [64, 32], F32); bkm = P.buf()
    ht = [P.sbuf(f"ht{i}", [128, 8, 512], BF16) for i in range(2)]; bht = [[P.buf() for _ in range(8)] for _ in range(2)]
    qf = P.sbuf("qf", [64, 512], F32); bqf = P.buf()
    gm = [P.sbuf(f"gm{i}", [128, 4, 32], F32) for i in range(2)]; bgm = [P.buf() for _ in range(2)]
    top8 = [P.sbuf(f"top8{i}", [128, 4, 8], F32) for i in range(2)]; btop = [P.buf() for _ in range(2)]
    pen = [P.sbuf(f"pen{i}", [128, 4, 96], F32) for i in range(2)]; bpen = [P.buf() for _ in range(2)]
    stmp = [P.sbuf(f"stmp{i}", [128, 512], F32) for i in range(2)]; bstmp = [P.buf() for _ in range(2)]
    pl = [P.psum(f"pl{i}", [128, 512]) for i in range(2)]; bpl = [P.buf(excl=True) for _ in range(2)]
    pg = P.psum("pg", [128, 512]); bpg = P.buf(excl=True)
    cnt = {"pl": 0, "ht": 0, "st": 0}
    anti, ident = cm[:, 0, :], cm[:, 1, :]

    P.dma("pool", lambda e: e.dma_start(out=w[:], in_=w_d), writes=[bw])
    P.dma("pool", lambda e: e.dma_start(out=masks[:], in_=mask_d), writes=[bmask])
    P.dma("pool", lambda e: e.dma_start(out=kT[64:96, :], in_=kblk_d[64:96, :]), writes=[bkrow])
    P.dma("sp", lambda e: e.dma_start(out=tab[:], in_=tab_d), writes=[btab])
    P.dma("sp", lambda e: e.dma_start(out=oh[:], in_=oh_d), writes=[boh])
    P.dma("sp", lambda e: e.dma_start(out=cm[:], in_=cm_d), writes=[bcm])
    P.dma("sp", lambda e: e.dma_start(out=selc[:], in_=sel_d), writes=[bselc])
    for i_ in range(2):
        P.op("dve", lambda e, i_=i_: e.memset(pen[i_][:], 0.0), writes=[bpen[i_]])
    for c3 in range(3):
        n0 = c3 * 384
        P.op("pe", lambda e, n0=n0: e.matmul(pg[0:4, 0:384], lhsT=tab[:], rhs=oh[:, n0:n0 + 384], start=True, stop=True),
             reads=[btab, boh], writes=[bpg], nosync_same=True)
        P.op("act", lambda e, n0=n0: e.activation(out=tv[:, n0:n0 + 384], in_=pg[0:4, 0:384], func=AF.Identity), reads=[bpg], writes=[btv])
    P.op("pe", lambda e: e.matmul(pg[:, 0:4], lhsT=oh[:, 1152:1280], rhs=tab[:], start=True, stop=True), reads=[btab, boh], writes=[bpg], nosync_same=True)
    P.op("act", lambda e: e.activation(out=b31[:], in_=pg[:, 0:4], func=AF.Identity), reads=[bpg], writes=[bb31])
    btvd = P.buf()
    P.dma("sp", lambda e: e.dma_start(out=tv_d, in_=tv[:]), reads=[btv], writes=[btvd])

    R = attn_resources(P)
    for hh in range(4):
        for x in range(5):
            src = bass.AP(tv_d.tensor, hh * 1152 + 128 * x, [[1, 128], [1, 512]])
            P.dma("sp", lambda e, src=src: e.dma_start(out=hk[:], in_=src), reads=[btvd], writes=[bhk])
            P.op("pe", lambda e: e.matmul(pg[:], lhsT=anti, rhs=hk[:], start=True, stop=True), reads=[bcm, bhk], writes=[bpg], nosync_same=True)
            P.op("act", lambda e, x=x: e.activation(out=bt[:, x, :], in_=pg[:], func=AF.Identity), reads=[bpg], writes=[bbt])
        P.op("pool", lambda e: e.memset(va[:, :, 64:65], 1.0), writes=[bva])
        for t in range(NQT):
            tsl = slice(t * 512, (t + 1) * 512)
            k = cnt["ht"] % 2; cnt["ht"] += 1
            for kc in range(8):
                P.dma("sp", lambda e, kc=kc, k=k, tsl=tsl: e.dma_start(out=ht[k][:, kc, :], in_=hT_d[kc, :, tsl]), writes=[bht[k][kc]])
            q_ = cnt["pl"] % 2; cnt["pl"] += 1
            for kc in range(8):
                P.op("pe", lambda e, kc=kc, q_=q_, k=k, hh=hh: e.matmul(pl[q_][0:64, :], lhsT=w[:, kc, 256 + hh * 64:256 + (hh + 1) * 64], rhs=ht[k][:, kc, :],
                                                                        start=(kc == 0), stop=(kc == 7)), reads=[bw, bht[k][kc]], writes=[bpl[q_]], nosync_same=True)
            P.op("act", lambda e, q_=q_, tsl=tsl: e.activation(out=kT[0:64, tsl], in_=pl[q_][0:64, :], func=AF.Identity), reads=[bpl[q_]], writes=[bkT])
            P.op("dve", lambda e, q_=q_, t=t: e.tensor_reduce(out=kmT[:, 2 * t:2 * t + 2], in_=pl[q_][0:64, :].rearrange("p (a b) -> p a b", a=2),
                                                             axis=AX.X, op=ALU.add), reads=[bpl[q_]], writes=[bkm])
            P.op("dve", lambda e, t=t: e.tensor_scalar(out=kmT[:, 2 * t:2 * t + 2], in0=kmT[:, 2 * t:2 * t + 2], scalar1=1.0 / 256, scalar2=None, op0=ALU.mult),
                 reads=[bkm], writes=[bkm])
            for tb4 in range(4):
                blk = t * 4 + tb4
                q_ = cnt["pl"] % 2; cnt["pl"] += 1
                for kc in range(8):
                    P.op("pe", lambda e, kc=kc, q_=q_, k=k, hh=hh, tb4=tb4: e.matmul(pl[q_][:, 0:64], lhsT=ht[k][:, kc, tb4 * 128:(tb4 + 1) * 128],
                                                                                     rhs=w[:, kc, 512 + hh * 64:512 + (hh + 1) * 64], start=(kc == 0), stop=(kc == 7)),
                         reads=[bw, bht[k][kc]], writes=[bpl[q_]], nosync_same=True)
                P.op("act", lambda e, q_=q_, blk=blk: e.activation(out=va[:, blk, 0:64], in_=pl[q_][:, 0:64], func=AF.Identity), reads=[bpl[q_]], writes=[bva])
            q_ = cnt["pl"] % 2; cnt["pl"] += 1
            for kc in range(8):
                P.op("pe", lambda e, kc=kc, q_=q_, k=k, hh=hh: e.matmul(pl[q_][0:64, :], lhsT=w[:, kc, hh * 64:(hh + 1) * 64], rhs=ht[k][:, kc, :],
                                                                        start=(kc == 0), stop=(kc == 7)), reads=[bw, bht[k][kc]], writes=[bpl[q_]], nosync_same=True)
            P.op("act", lambda e, q_=q_: e.activation(out=qf[:], in_=pl[q_][0:64, :], func=AF.Identity, scale=0.125), reads=[bpl[q_]], writes=[bqf])
            P.op("dve", lambda e, tsl=tsl: e.tensor_copy(out=qT[0:64, tsl], in_=qf[:]), reads=[bqf], writes=[bqT[t]])
            gi = t % 2
            q_ = cnt["pl"] % 2; cnt["pl"] += 1
            for qb in range(4):
                P.op("pe", lambda e, qb=qb, q_=q_: e.matmul(pl[q_][:, qb * 32:(qb + 1) * 32], lhsT=qf[:, qb * 128:(qb + 1) * 128], rhs=kmT[:], start=True, stop=True),
                     reads=[bqf, bkm], writes=[bpl[q_]], nosync_same=True)
            P.op("dve", lambda e, q_=q_, gi=gi, t=t: e.tensor_tensor(out=gm[gi][:], in0=pl[q_][:, 0:128].rearrange("p (a b) -> p a b", a=4),
                                                                   in1=selc[:, 0, 4 * t:4 * t + 4, :], op=ALU.add), reads=[bpl[q_], bselc], writes=[bgm[gi]])
            for qb in range(4):
                P.op("dve", lambda e, qb=qb, gi=gi: e.max(out=top8[gi][:, qb, :], in_=gm[gi][:, qb, :]), reads=[bgm[gi]], writes=[btop[gi]])
            for qb in range(4):
                P.op("dve", lambda e, qb=qb, gi=gi: e.tensor_scalar(out=gm[gi][:, qb, :], in0=gm[gi][:, qb, :], scalar1=top8[gi][:, qb, 2:3], scalar2=-1.0,
                                                                   op0=ALU.is_ge, op1=ALU.add), reads=[bgm[gi], btop[gi]], writes=[bgm[gi]])
            P.op("dve", lambda e, gi=gi, t=t: e.scalar_tensor_tensor(out=pen[gi][:, :, 64:96], in0=gm[gi][:], scalar=30000.0, in1=selc[:, 1, 4 * t:4 * t + 4, :],
                                                                    op0=ALU.mult, op1=ALU.mult), reads=[bgm[gi], bselc], writes=[bpen[gi]])
            for qb in range(4):
                P.op("pe", lambda e, qb=qb, gi=gi: e.transpose(pg[0:96, qb * 128:(qb + 1) * 128], pen[gi][:, qb, :], ident), reads=[bpen[gi], bcm], writes=[bpg], nosync_same=True)
            P.op("act", lambda e, tsl=tsl: e.activation(out=qT[64:96, tsl], in_=pg[64:96, :], func=AF.Identity), reads=[bpg], writes=[bqT[t]])

        def q_tile(j):
            return qT[0:96, j * 512:(j + 1) * 512], bqT[j]

        def exp_fn(j, kb, ps, bps, pt, bpt, hh=hh):
            d = kb - 4 * j
            if d >= -1:
                x = 3 - d
                si = cnt["st"] % 2; cnt["st"] += 1
                P.op("dve", lambda e, ps=ps, si=si, x=x: e.tensor_tensor(out=stmp[si][:], in0=ps[:], in1=bt[:, x, :], op=ALU.add), reads=[bps, bbt], writes=[bstmp[si]])
                P.op("act", lambda e, pt=pt, si=si: e.activation(out=pt[:], in_=stmp[si][:], func=AF.Exp), reads=[bstmp[si]], writes=[bpt])
            else:
                P.op("act", lambda e, ps=ps, pt=pt, hh=hh: e.activation(out=pt[:], in_=ps[:], func=AF.Exp, bias=b31[:, hh:hh + 1]), reads=[bps, bb31], writes=[bpt])

        class _KB:
            pass
        attn_core(P, cnt, R, kT, bkT, 96, va, bva, q_tile, 1.0, o_d, hh, hh * 64, masks, bmask, exp_fn=exp_fn)
    print('moba sbuf remaining', nc.sbuf_bytes_remaining, {e: len(v_) for e, v_ in P.streams.items()})
    P.final_wait("sp", R["outs"])
    P.emit(); P.close()
    return nc


def _t5_bucket_np(dist):
    n = np.maximum(dist, 0)
    nf = np.maximum(n, 16).astype(np.float32)
    large = 16 + (np.log(nf / np.float32(16)) / np.float32(np.log(128 / 16)) * np.float32(16)).astype(np.int32)
    large = np.minimum(large, 31)
    return np.where(n < 16, n, large)


def moba_host_inputs(inp, hT_full, g):
    w = inp["moba_w_in"][0]
    cols = []
    for part in range(3):
        cols.append(w[:, part * 1024 + g * 256: part * 1024 + (g + 1) * 256])
    wl = np.concatenate(cols, axis=1)
    wl = np.ascontiguousarray(wl.reshape(8, 128, 768).transpose(1, 0, 2))
    tab = np.ascontiguousarray(inp["rel_table"][:, g * 4:(g + 1) * 4]).astype(np.float32)
    bk = _t5_bucket_np(np.arange(1152) - 511)
    oh = np.zeros((32, 1280), np.float32)
    oh[bk, np.arange(1152)] = 1.0
    oh[31, 1152:] = 1.0
    i = np.arange(128)
    cm = np.stack([(i[:, None] + i[None, :] == 127).astype(np.float32), np.eye(128, dtype=np.float32)], axis=1)
    kblk = np.zeros((96, S), np.float32)
    kblk[64 + np.arange(S) // 256, np.arange(S)] = 1.0
    n = np.arange(32)
    own = (np.arange(64) // 2)[:, None]
    negm = np.where(n[None, :] < own, 0.0, -1e30).astype(np.float32)
    valid = (n[None, :] < own).astype(np.float32)
    selc = np.ascontiguousarray(np.broadcast_to(np.stack([negm, valid], axis=0)[None], (128, 2, 64, 32))).astype(np.float32)
    return {"hT": hT_full, "wqkv": wl, "tab": tab, "oh": oh, "cm": np.ascontiguousarray(cm), "kblk": kblk, "selc": selc,
            "mask": causal_masks_np()}


def lay_w_in(w):
    gate = w[:, :2816].reshape(8, 128, 11, 256)
    up = w[:, 2816:].reshape(8, 128, 11, 256)
    t = np.concatenate([gate, up], axis=3)
    return np.ascontiguousarray(t.transpose(2, 1, 0, 3))
def lay_w_out(w):
    t = w.reshape(22, 128, 4, 256)
    return np.ascontiguousarray(t.transpose(2, 1, 0, 3))
def lay_xT(xs):
    return np.ascontiguousarray(xs.T.reshape(8, 128, xs.shape[0]))
def lay_kp(w):
    K, N = w.shape
    return np.ascontiguousarray(w.reshape(K // 128, 128, N).transpose(1, 0, 2))
def mods_inputs(c, ada_w, ada_b):
    W = np.concatenate([ada_w[i] for i in range(4)], axis=1)
    bflat = ada_b.reshape(-1)
    cT = np.ascontiguousarray(c.T.reshape(8, 128, 2).transpose(1, 0, 2))
    maps = []
    for k in range(8):
        sl = W[:, k * 4608:(k + 1) * 4608]
        maps.append({"cT": cT, "aw": lay_kp(sl), "ab": np.ascontiguousarray(bflat[k * 4608:(k + 1) * 4608].reshape(36, 128).T)})
    return maps
def mods_assemble(results):
    allm = np.concatenate([r["modsT"] for r in results], axis=1)
    return [[np.ascontiguousarray(allm[:, l * 72:(l + 1) * 72, b]) for b in range(2)] for l in range(4)]


import ml_dtypes

_CORES = list(range(8))


def _gather_hT(results):
    return [np.ascontiguousarray(np.concatenate([results[b * 4 + q]["hT_out"] for q in range(4)], axis=2)) for b in range(2)]


def _oT_for_tokens(o_tok, b, q):
    sl = o_tok[b][q * 2048:(q + 1) * 2048]
    return np.ascontiguousarray(sl.T.reshape(-1, 128, 2048))


def kernel(**inp):
    inp = {k: np.asarray(v) for k, v in inp.items()}
    res = run_bass_kernel_spmd(build_mods(), mods_inputs(inp["c"], inp["ada_w"], inp["ada_b"]), core_ids=_CORES)
    modsT = mods_assemble(res.results)
    gT = [np.ascontiguousarray(inp["norm_g"][l].reshape(3, 8, 128).transpose(2, 0, 1)) for l in range(4)]
    x = inp["x"]
    maps = []
    for cid in _CORES:
        b, q = cid // 4, cid % 4
        maps.append({"xT_in": lay_xT(x[b, q * 2048:(q + 1) * 2048]), "w_in1": lay_w_in(inp["ffn_w_in"][0, 0]),
                     "w_out1": lay_w_out(inp["ffn_w_out"][0, 0]), "modsB": modsT[0][b], "gB": gT[0]})
    res = run_bass_kernel_spmd(build_token(0, True, False), maps, core_ids=_CORES)
    xT = [r["xT_out"] for r in res.results]
    hT = _gather_hT(res.results)
    w_mix_out = [inp["mla_w_out"][0], inp["rwkv_w_out"][0], inp["moba_w_out"][0], inp["ret_w_out"][0]]
    for l in range(4):
        if l == 0:
            res = run_bass_kernel_spmd(build_mla(), [mla_host_inputs(inp, hT[c // 4], c // 4, c % 4) for c in _CORES], core_ids=_CORES)
            key = "o"
        elif l == 1:
            res = run_bass_kernel_spmd(build_rwkv(), [rwkv_host_inputs(inp, hT[c // 4], c % 4) for c in _CORES], core_ids=_CORES)
            key = "o"
        elif l == 2:
            res = run_bass_kernel_spmd(build_moba(), [moba_host_inputs(inp, hT[c // 4], c % 4) for c in _CORES], core_ids=_CORES)
            key = "o"
        else:
            res = run_bass_kernel_spmd(build_ret(), [ret_host_inputs(inp, hT[c // 4], c % 4) for c in _CORES], core_ids=_CORES)
            key = "z"
        o_tok = [np.concatenate([res.results[b * 4 + g][key] for g in range(4)], axis=1) for b in range(2)]
        F_in = o_tok[0].shape[1]
        last = (l == 3)
        maps = []
        for cid in _CORES:
            b, q = cid // 4, cid % 4
            m = {"xT_in": xT[cid], "oT": _oT_for_tokens(o_tok, b, q), "wmo": lay_kp(w_mix_out[l]),
                 "w_in2": lay_w_in(inp["ffn_w_in"][l, 1]), "w_out2": lay_w_out(inp["ffn_w_out"][l, 1]),
                 "modsA": modsT[l][b], "gA": gT[l]}
            if not last:
                m.update({"w_in1": lay_w_in(inp["ffn_w_in"][l + 1, 0]), "w_out1": lay_w_out(inp["ffn_w_out"][l + 1, 0]),
                          "modsB": modsT[l + 1][b], "gB": gT[l + 1]})
            else:
                m["gF"] = np.ascontiguousarray(inp["final_g"].reshape(8, 128).T)
            maps.append(m)
        res = run_bass_kernel_spmd(build_token(F_in, not last, last), maps, core_ids=_CORES)
        xT = [r["xT_out"] for r in res.results]
        if not last:
            hT = _gather_hT(res.results)
    out = np.empty((2, 8192, 1024), np.float32)
    for cid in _CORES:
        b, q = cid // 4, cid % 4
        out[b, q * 2048:(q + 1) * 2048] = xT[cid].reshape(1024, 2048).T
    return out
```
